# Optimizing a Trainium2 kernel written in Bass

```python
import functools
import jax, jax.numpy as jnp
from jax import lax
import numpy as np

D_MODEL = 2048
BATCH = 4
SEQ = 4096
DEPTH = 1
DEC_BATCH = 128
DEC_SEQ = 8
PAST_LEN = 16384
PAGE_SIZE = 128

ATTN_HEADS = 16
ATTN_KV_HEADS = 4
ATTN_HEAD_DIM = 64
ATTN_GROUP = ATTN_HEADS // ATTN_KV_HEADS
WINDOW = 128
ATTN_WIDTH = ATTN_HEADS * ATTN_HEAD_DIM
KV_WIDTH = ATTN_KV_HEADS * ATTN_HEAD_DIM
HGRN_HEADS = 8
HGRN_DK = 128
HGRN_DV = 128
HGRN_KW = HGRN_HEADS * HGRN_DK
HGRN_VW = HGRN_HEADS * HGRN_DV
HGRN_CHUNK = 64
MIX_WIDTH = ATTN_WIDTH + HGRN_VW
IN_PROJ_WIDTH = ATTN_WIDTH + 2 * KV_WIDTH + 2 * HGRN_KW + 2 * HGRN_VW
IN_PROJ_SPLITS = (ATTN_WIDTH,
                  ATTN_WIDTH + KV_WIDTH,
                  ATTN_WIDTH + 2 * KV_WIDTH,
                  ATTN_WIDTH + 2 * KV_WIDTH + HGRN_KW,
                  ATTN_WIDTH + 2 * KV_WIDTH + 2 * HGRN_KW,
                  ATTN_WIDTH + 2 * KV_WIDTH + 2 * HGRN_KW + HGRN_VW)
MEM_TOKENS = 256
MEM_HEADS = 4
MEM_HEAD_DIM = 128
MEM_WIDTH = MEM_HEADS * MEM_HEAD_DIM
D_FF = 5632
FFN_RESIDUAL = 0.5
EPS = 1e-6

kernel_name = 'hymba_swa_sink_hgrn2_macaron_memxattn_step'


def rmsnorm(x, g):
    xf = x.astype(jnp.float32)
    y = xf * lax.rsqrt(jnp.mean(xf * xf, axis=-1, keepdims=True) + EPS)
    return (y * g.astype(jnp.float32)).astype(x.dtype)


def half_ffn(x, g_pre, g_post, w_gate, w_up, w_down):
    h = rmsnorm(x, g_pre)
    f = (jax.nn.silu(h @ w_gate) * (h @ w_up)) @ w_down
    return x + FFN_RESIDUAL * rmsnorm(f, g_post)


def alibi_slopes():
    return 2.0 ** (-8.0 * jnp.arange(1, ATTN_HEADS + 1, dtype=jnp.float32) / ATTN_HEADS)


def sink_attention(q, k, v, qpos, kpos, sinks):
    s = jnp.einsum('bnqhgd,bnkhd->bnhgqk', q, k).astype(jnp.float32) * (ATTN_HEAD_DIM ** -0.5)
    dist = qpos[:, :, None] - kpos[:, None, :]
    valid = (kpos[:, None, :] >= 0) & (dist >= 0) & (dist < WINDOW)
    slopes = alibi_slopes().reshape(ATTN_KV_HEADS, ATTN_GROUP)[:, :, None, None]
    s = s - slopes * dist[None, :, None, None].astype(jnp.float32)
    s = jnp.where(valid[None, :, None, None], s, -jnp.inf)
    sink = sinks.astype(jnp.float32).reshape(ATTN_KV_HEADS, ATTN_GROUP)[:, :, None, None]
    m = jnp.maximum(jnp.max(s, axis=-1, keepdims=True), sink)
    p = jnp.exp(s - m)
    p = p / (jnp.sum(p, axis=-1, keepdims=True) + jnp.exp(sink - m))
    return jnp.einsum('bnhgqk,bnkhd->bnqhgd', p.astype(v.dtype), v)


def window_attention_prompt(q, k, v, sinks):
    B, S = q.shape[:2]
    nb = S // WINDOW
    qb = q.reshape(B, nb, WINDOW, ATTN_KV_HEADS, ATTN_GROUP, ATTN_HEAD_DIM)
    kb = k.reshape(B, nb, WINDOW, ATTN_KV_HEADS, ATTN_HEAD_DIM)
    vb = v.reshape(B, nb, WINDOW, ATTN_KV_HEADS, ATTN_HEAD_DIM)

    def band(a):
        prev = jnp.concatenate([jnp.zeros_like(a[:, :1]), a[:, :-1]], axis=1)
        return jnp.concatenate([prev, a], axis=2)

    start = jnp.arange(nb)[:, None] * WINDOW
    qpos = start + jnp.arange(WINDOW)[None, :]
    kpos = start - WINDOW + jnp.arange(2 * WINDOW)[None, :]
    o = sink_attention(qb, band(kb), band(vb), qpos, kpos, sinks)
    return o.reshape(B, S, ATTN_WIDTH), k[:, -WINDOW:], v[:, -WINDOW:]


def window_attention_sample(q, k, v, sinks, cache_k, cache_v):
    B, T = q.shape[:2]
    kk = jnp.concatenate([cache_k, k], axis=1)
    vv = jnp.concatenate([cache_v, v], axis=1)
    qpos = (PAST_LEN + jnp.arange(T))[None, :]
    kpos = (PAST_LEN - WINDOW + jnp.arange(WINDOW + T))[None, :]
    o = sink_attention(q[:, None], kk[:, None], vv[:, None], qpos, kpos, sinks)
    return o.reshape(B, T, ATTN_WIDTH), kk[:, -WINDOW:], vv[:, -WINDOW:]


def hgrn2_features(q_raw, f_raw, i_raw, lower_bound):
    B, T = q_raw.shape[:2]
    shp = (B, T, HGRN_HEADS, HGRN_DK)
    fr = f_raw.astype(jnp.float32).reshape(shp)
    lb = lower_bound.reshape(HGRN_HEADS, HGRN_DK)
    logf = jnp.log(lb + (1.0 - lb) * jax.nn.sigmoid(fr))
    k = (1.0 - lb) * jax.nn.sigmoid(-fr)
    q = jax.nn.silu(q_raw.astype(jnp.float32)).reshape(shp)
    v = i_raw.astype(jnp.float32).reshape(B, T, HGRN_HEADS, HGRN_DV)
    return q, k, v, logf


def hgrn2_chunked(q, k, v, logf, S0, chunk):
    B, T = q.shape[:2]
    n = T // chunk

    def to_chunks(a):
        return jnp.moveaxis(a.reshape(B, n, chunk, *a.shape[2:]), 1, 0)

    causal = jnp.tril(jnp.ones((chunk, chunk), dtype=bool))[None, :, :, None, None]

    def step(S, inp):
        qc, kc, vc, lc = inp
        L = jnp.cumsum(lc, axis=1)
        diff = L[:, :, None] - L[:, None, :]
        decay = jnp.exp(jnp.where(causal, diff, -jnp.inf))
        A = jnp.einsum('bthk,btshk,bshk->bhts', qc, decay, kc)
        o = jnp.einsum('bhts,bshv->bthv', A, vc) + jnp.einsum('bthk,bhkv->bthv', qc * jnp.exp(L), S)
        L_end = L[:, -1]
        S_new = jnp.exp(L_end)[..., None] * S + jnp.einsum(
            'bshk,bshv->bhkv', kc * jnp.exp(L_end[:, None] - L), vc)
        return S_new, o

    S_fin, o = lax.scan(step, S0.astype(jnp.float32),
                        (to_chunks(q), to_chunks(k), to_chunks(v), to_chunks(logf)))
    o = jnp.moveaxis(o, 0, 1).reshape(B, T, HGRN_HEADS, HGRN_DV)
    return o, S_fin


def memory_kv(mem, g, w_k, w_v):
    B = mem.shape[0]
    m = rmsnorm(mem, g)
    k = (m @ w_k).reshape(B, -1, MEM_HEADS, MEM_HEAD_DIM)
    v = (m @ w_v).reshape(B, -1, MEM_HEADS, MEM_HEAD_DIM)
    return k, v


def memory_attention(h, mem_k, mem_v, w_q, w_o):
    B, T = h.shape[:2]
    q = (h @ w_q).reshape(B, T, MEM_HEADS, MEM_HEAD_DIM)
    s = jnp.einsum('bthd,bmhd->bhtm', q, mem_k).astype(jnp.float32) * (MEM_HEAD_DIM ** -0.5)
    p = jax.nn.softmax(s, axis=-1).astype(mem_v.dtype)
    o = jnp.einsum('bhtm,bmhd->bthd', p, mem_v).reshape(B, T, MEM_WIDTH)
    return o @ w_o


def trunk_layer(x, mem_k, mem_v, window_mixer, recurrent_mixer, p):
    x = half_ffn(x, p['ffn1_norm_pre'], p['ffn1_norm_post'], p['ffn1_w_gate'], p['ffn1_w_up'], p['ffn1_w_down'])
    B, T = x.shape[:2]
    h = rmsnorm(x, p['mix_norm_pre'])
    q_a, k_a, v_a, q_h, f_h, i_h, g_h = jnp.split(h @ p['w_in'], IN_PROJ_SPLITS, axis=-1)
    q_a = q_a.reshape(B, T, ATTN_KV_HEADS, ATTN_GROUP, ATTN_HEAD_DIM)
    k_a = k_a.reshape(B, T, ATTN_KV_HEADS, ATTN_HEAD_DIM)
    v_a = v_a.reshape(B, T, ATTN_KV_HEADS, ATTN_HEAD_DIM)
    o_a, win_k, win_v = window_mixer(q_a, k_a, v_a, p['attn_sinks'])
    hq, hk, hv, hlogf = hgrn2_features(q_h, f_h, i_h, p['lower_bound'])
    o_h, S = recurrent_mixer(hq, hk, hv, hlogf)
    a = rmsnorm(o_a, p['attn_out_gain'])
    r = rmsnorm(o_h.astype(x.dtype), p['hgrn_out_gain']).reshape(B, T, HGRN_VW) * jax.nn.silu(g_h)
    mixed = jnp.concatenate([a, r], axis=-1) @ p['w_out']
    x = x + rmsnorm(mixed, p['mix_norm_post'])
    h = rmsnorm(x, p['mem_norm_pre'])
    x = x + rmsnorm(memory_attention(h, mem_k, mem_v, p['w_mem_q'], p['w_mem_o']), p['mem_norm_post'])
    x = half_ffn(x, p['ffn2_norm_pre'], p['ffn2_norm_post'], p['ffn2_w_gate'], p['ffn2_w_up'], p['ffn2_w_down'])
    return x, win_k, win_v, S


def setup_inputs(seed: int = 0) -> dict:
    key = jax.random.key(seed)
    keys = iter(jax.random.split(key, 40))

    def nrm(shape, scale):
        return scale * jax.random.normal(next(keys), shape, jnp.float32)

    def gain(width):
        return 1.0 + 0.01 * nrm((DEPTH, width), 1.0)

    return {
        'x_prompt': nrm((BATCH, SEQ, D_MODEL), 1.0),
        'x_sample': nrm((DEC_BATCH, DEC_SEQ, D_MODEL), 1.0),
        'mem_prompt': nrm((BATCH, MEM_TOKENS, D_MODEL), 1.0),
        'cache_win_k': nrm((DEPTH, DEC_BATCH, WINDOW, ATTN_KV_HEADS, ATTN_HEAD_DIM), 1.0),
        'cache_win_v': nrm((DEPTH, DEC_BATCH, WINDOW, ATTN_KV_HEADS, ATTN_HEAD_DIM), 1.0),
        'state_hgrn': nrm((DEPTH, DEC_BATCH, HGRN_HEADS, HGRN_DK, HGRN_DV), 0.5),
        'cache_mem_k': nrm((DEPTH, DEC_BATCH, MEM_TOKENS, MEM_HEADS, MEM_HEAD_DIM), 1.0),
        'cache_mem_v': nrm((DEPTH, DEC_BATCH, MEM_TOKENS, MEM_HEADS, MEM_HEAD_DIM), 1.0),
        'ffn1_norm_pre': gain(D_MODEL),
        'ffn1_norm_post': gain(D_MODEL),
        'ffn1_w_gate': nrm((DEPTH, D_MODEL, D_FF), D_MODEL ** -0.5),
        'ffn1_w_up': nrm((DEPTH, D_MODEL, D_FF), D_MODEL ** -0.5),
        'ffn1_w_down': nrm((DEPTH, D_FF, D_MODEL), D_FF ** -0.5),
        'mix_norm_pre': gain(D_MODEL),
        'mix_norm_post': gain(D_MODEL),
        'w_in': nrm((DEPTH, D_MODEL, IN_PROJ_WIDTH), D_MODEL ** -0.5),
        'attn_sinks': nrm((DEPTH, ATTN_HEADS), 1.0),
        'hgrn_lb_logits': nrm((DEPTH + 1, HGRN_KW), 0.5),
        'attn_out_gain': gain(ATTN_WIDTH),
        'hgrn_out_gain': gain(HGRN_DV),
        'w_out': nrm((DEPTH, MIX_WIDTH, D_MODEL), MIX_WIDTH ** -0.5),
        'mem_norm_pre': gain(D_MODEL),
        'mem_norm_post': gain(D_MODEL),
        'mem_norm_kv': gain(D_MODEL),
        'w_mem_q': nrm((DEPTH, D_MODEL, MEM_WIDTH), D_MODEL ** -0.5),
        'w_mem_k': nrm((DEPTH, D_MODEL, MEM_WIDTH), D_MODEL ** -0.5),
        'w_mem_v': nrm((DEPTH, D_MODEL, MEM_WIDTH), D_MODEL ** -0.5),
        'w_mem_o': nrm((DEPTH, MEM_WIDTH, D_MODEL), MEM_WIDTH ** -0.5),
        'ffn2_norm_pre': gain(D_MODEL),
        'ffn2_norm_post': gain(D_MODEL),
        'ffn2_w_gate': nrm((DEPTH, D_MODEL, D_FF), D_MODEL ** -0.5),
        'ffn2_w_up': nrm((DEPTH, D_MODEL, D_FF), D_MODEL ** -0.5),
        'ffn2_w_down': nrm((DEPTH, D_FF, D_MODEL), D_FF ** -0.5),
    }


def reference(x_prompt, x_sample, mem_prompt, cache_win_k, cache_win_v, state_hgrn, cache_mem_k, cache_mem_v,
              ffn1_norm_pre, ffn1_norm_post, ffn1_w_gate, ffn1_w_up, ffn1_w_down,
              mix_norm_pre, mix_norm_post, w_in, attn_sinks, hgrn_lb_logits, attn_out_gain, hgrn_out_gain, w_out,
              mem_norm_pre, mem_norm_post, mem_norm_kv, w_mem_q, w_mem_k, w_mem_v, w_mem_o,
              ffn2_norm_pre, ffn2_norm_post, ffn2_w_gate, ffn2_w_up, ffn2_w_down):
    lower_bounds = jnp.cumsum(jax.nn.softmax(hgrn_lb_logits.astype(jnp.float32), axis=0), axis=0)
    y_p = x_prompt
    y_s = x_sample
    p_wk, p_wv, p_S, p_mk, p_mv, s_wk, s_wv, s_S = [], [], [], [], [], [], [], []
    for l in range(DEPTH):
        p = dict(ffn1_norm_pre=ffn1_norm_pre[l], ffn1_norm_post=ffn1_norm_post[l], ffn1_w_gate=ffn1_w_gate[l],
                 ffn1_w_up=ffn1_w_up[l], ffn1_w_down=ffn1_w_down[l],
                 mix_norm_pre=mix_norm_pre[l], mix_norm_post=mix_norm_post[l], w_in=w_in[l],
                 attn_sinks=attn_sinks[l], lower_bound=lower_bounds[l], attn_out_gain=attn_out_gain[l],
                 hgrn_out_gain=hgrn_out_gain[l], w_out=w_out[l],
                 mem_norm_pre=mem_norm_pre[l], mem_norm_post=mem_norm_post[l], w_mem_q=w_mem_q[l], w_mem_o=w_mem_o[l],
                 ffn2_norm_pre=ffn2_norm_pre[l], ffn2_norm_post=ffn2_norm_post[l], ffn2_w_gate=ffn2_w_gate[l],
                 ffn2_w_up=ffn2_w_up[l], ffn2_w_down=ffn2_w_down[l])
        mk, mv = memory_kv(mem_prompt, mem_norm_kv[l], w_mem_k[l], w_mem_v[l])
        S0_p = jnp.zeros((x_prompt.shape[0], HGRN_HEADS, HGRN_DK, HGRN_DV), jnp.float32)
        y_p, wk, wv, S = trunk_layer(y_p, mk, mv, window_attention_prompt,
                                     functools.partial(hgrn2_chunked, S0=S0_p, chunk=HGRN_CHUNK), p)
        p_wk.append(wk)
        p_wv.append(wv)
        p_S.append(S)
        p_mk.append(mk)
        p_mv.append(mv)
        y_s, wk, wv, S = trunk_layer(
            y_s, cache_mem_k[l], cache_mem_v[l],
            functools.partial(window_attention_sample, cache_k=cache_win_k[l], cache_v=cache_win_v[l]),
            functools.partial(hgrn2_chunked, S0=state_hgrn[l], chunk=x_sample.shape[1]), p)
        s_wk.append(wk)
        s_wv.append(wv)
        s_S.append(S)
    return (y_p, y_s, jnp.stack(p_wk), jnp.stack(p_wv), jnp.stack(p_S), jnp.stack(p_mk), jnp.stack(p_mv),
            jnp.stack(s_wk), jnp.stack(s_wv), jnp.stack(s_S))
```

```python
import numpy as np
from contextlib import ExitStack
import concourse.bass as bass
import concourse.mybir as mybir
from concourse.bass_utils import run_bass_kernel_spmd

F32 = mybir.dt.float32
BF16 = mybir.dt.bfloat16
AF = mybir.ActivationFunctionType
ALU = mybir.AluOpType

D = 2048
DFF = 5632
NCH = 16
NFF = 44
EPS = 1e-6
NEG = -30000.0
TT = 512
NPT = 4
NSEQ = 16
TS = 128


class Sem:
    def __init__(self, handle, name):
        self.h = handle
        self.name = name
        self.val = 0


class Buf:
    __slots__ = ("name", "w", "r")

    def __init__(self, name=""):
        self.name = name
        self.w = None
        self.r = {}


class Eng:
    def __init__(self, name, sem, selfsync):
        self.name = name
        self.sem = sem
        self.selfsync = selfsync
        self.known = {}
        self.prog = []
        self.dma_pool = []
        self.dma_i = 0


class FW:
    def __init__(self, nc, stack, dry=False):
        self.nc = nc
        self.dry = dry
        self.engs = {}
        self.n_inst = 0
        self.dma_events = {}
        self.bar_buf = Buf("bar")
        self.mark_tile = None
        if dry:
            return
        for name, ss in (("pe", False), ("act", True), ("dve", True), ("pool", True), ("sp", False)):
            s = Sem(stack.enter_context(nc.semaphore("s_" + name)), name)
            self.engs[name] = Eng(name, s, ss)
        for q, n in (("sp", 24), ("pool", 16)):
            for i in range(n):
                s = Sem(stack.enter_context(nc.semaphore(f"d_{q}{i}")), f"d_{q}{i}")
                self.engs[q].dma_pool.append(s)

    def _collect(self, E, reads, writes):
        waits = {}

        def need(ev, raw):
            if ev is None:
                return
            sem, v = ev
            if sem is E.sem and not E.selfsync:
                return
            if E.known.get(sem, 0) >= v:
                return
            if waits.get(sem, 0) < v:
                waits[sem] = v

        for b in reads:
            need(b.w, True)
        for b in writes:
            need(b.w, False)
            for s, v in b.r.items():
                need((s, v), False)
        return waits

    def _commit(self, E, waits, fi, ev, reads, writes):
        for s, v in waits.items():
            E.known[s] = v
        E.prog.append((list(waits.items()), fi, ev))
        sem, v = ev
        for b in reads:
            if b.r.get(sem, 0) < v:
                b.r[sem] = v
        for b in writes:
            b.w = ev
            b.r = {}
        self.n_inst += 1

    def op(self, eng, fn, reads=(), writes=()):
        if self.dry:
            return
        E = self.engs[eng]
        waits = self._collect(E, reads, writes)
        E.sem.val += 1
        ev = (E.sem, E.sem.val)
        self._commit(E, waits, (fn, 1), ev, reads, writes)

    def dma(self, q, fn, reads=(), writes=(), track=True):
        if self.dry:
            return
        E = self.engs[q]
        s = E.dma_pool[E.dma_i % len(E.dma_pool)]
        E.dma_i += 1
        waits = self._collect(E, reads, writes)
        if s.val > 0 and E.known.get(s, 0) < s.val:
            waits[s] = s.val
        s.val += 16
        ev = (s, s.val)
        self._commit(E, waits, (fn, 16), ev, reads, writes)
        if track:
            self.dma_events[s] = s.val

    def barrier(self, engs=("pe", "act", "dve", "sp")):
        if self.dry:
            return
        for e in engs:
            E = self.engs[e]
            waits = {}
            for x in ("pe", "act", "dve"):
                X = self.engs[x]
                if X.sem.val > 0 and E.known.get(X.sem, 0) < X.sem.val:
                    waits[X.sem] = X.sem.val
            for s, v in self.dma_events.items():
                if E.known.get(s, 0) < v:
                    waits[s] = v
            for s, v in waits.items():
                E.known[s] = v
            if waits:
                E.prog.append((list(waits.items()), None, None))
        self.dma_events = {}

    def wait_all(self, eng, bufs):
        if self.dry:
            return
        E = self.engs[eng]
        waits = {}
        for b in bufs:
            for ev in [b.w] + list(b.r.items()):
                if ev is None:
                    continue
                s, v = ev
                if E.known.get(s, 0) < v and waits.get(s, 0) < v:
                    waits[s] = v
        for s, v in waits.items():
            E.known[s] = v
        E.prog.append((list(waits.items()), None, None))

    def replay(self):
        nc = self.nc
        engs = self.engs

        def run(E, h):
            for waits, fi, ev in E.prog:
                for s, v in waits:
                    h.wait_ge(s.h, v)
                if fi is None:
                    continue
                fn, inc = fi
                fn(h).then_inc(ev[0].h, inc)

        with nc.Block() as block:
            @block.tensor
            def _(h):
                run(engs["pe"], h)

            @block.scalar
            def _(h):
                run(engs["act"], h)

            @block.vector
            def _(h):
                run(engs["dve"], h)

            @block.gpsimd
            def _(h):
                run(engs["pool"], h)

            @block.sync
            def _(h):
                run(engs["sp"], h)


class StopEmit(Exception):
    pass


def cp(n):
    import os
    if int(os.environ.get('KCUT', '999')) <= n:
        raise StopEmit()


class Arena:
    def __init__(self, t2d, nwords):
        self.t = t2d
        self.n = nwords
        self.top = 0

    def alloc(self, nelem, dt):
        words = nelem if dt == F32 else (nelem + 1) // 2
        a = self.top
        self.top += words
        assert self.top <= self.n, f"arena overflow {self.top} > {self.n}"
        ap = self.t[:, a:a + words]
        return ap if dt == F32 else ap.bitcast(BF16)

    def mark(self):
        return self.top

    def reset(self, m):
        self.top = m


class WQ:
    SLOT = 4096

    def __init__(self, fw, slots, plan, nc=None):
        self.fw = fw
        self.nc = nc
        self.slots = slots
        self.bufs = [Buf(f"wslot{i}") for i in range(len(slots))]
        self.plan = plan
        self.rec = []
        self.i = 0
        self.issued = 0
        self.LA = len(slots) - 2
        self.in_sample = False
        self.store_after = {}
        self.load_from = {}
        if plan is not None:
            occ = {}
            for i, p in enumerate(plan):
                if p[4] is not None:
                    occ.setdefault(p[4], []).append(i)
            for key, idx in occ.items():
                if len(idx) < 2:
                    continue
                _, p_, a_, b_, _, _ = plan[idx[0]]
                name = "scr_" + "_".join(str(x) for x in key)
                scr = nc.dram_tensor(name, [p_, a_ * b_], BF16).ap()
                sb = Buf(name)
                self.store_after[idx[0]] = (scr, sb)
                for j in idx[1:]:
                    self.load_from[j] = (scr, sb)

    def view(self, k, npart, a, b):
        return self.slots[k % len(self.slots)][0:npart, 0:a * b].rearrange("p (a b) -> p a b", b=b)

    def get(self, src, npart, a, b):
        key = None
        if isinstance(src, tuple):
            src, key = src
        assert a * b <= self.SLOT
        if self.plan is None:
            self.rec.append((src, npart, a, b, key, self.in_sample))
            return self.view(0, npart, a, b), self.bufs[0]
        i = self.i
        self.i += 1
        lim = min(i + self.LA, len(self.plan) - 1)
        while self.issued <= lim:
            k = self.issued
            s, p_, a_, b_, _, _ = self.plan[k]
            dst = self.view(k, p_, a_, b_)
            if k in self.load_from:
                scr, sb = self.load_from[k]
                dst2 = self.slots[k % len(self.slots)][0:p_, 0:a_ * b_]
                self.fw.dma("pool", lambda e, dst2=dst2, scr=scr: e.dma_start(out=dst2, in_=scr),
                            reads=[sb], writes=[self.bufs[k % len(self.slots)]], track=False)
            else:
                self.fw.dma("pool", lambda e, dst=dst, s=s: e.dma_start(out=dst, in_=s),
                            writes=[self.bufs[k % len(self.slots)]], track=False)
            self.issued += 1
        if i in self.store_after:
            scr, sb = self.store_after[i]
            p_, a_, b_ = self.plan[i][1:4]
            src2 = self.slots[i % len(self.slots)][0:p_, 0:a_ * b_]
            self.fw.dma("sp", lambda e, src2=src2, scr=scr: e.dma_start(out=scr, in_=src2),
                        reads=[self.bufs[i % len(self.slots)]], writes=[sb], track=False)
        return self.view(i, npart, a, b), self.bufs[i % len(self.slots)]


GV = dict(f1pre=0, f1post=16, mpre=32, mpost=48, epre=64, epost=80, ekv=96, f2pre=112, f2post=128,
          again=144, hgain=160, lb0=161, lb1=169, sinks=177, flag=193)
NGV = 194
GQA, GK, GV_, GQH, GF, GI, GG = 0, 4, 5, 6, 10, 14, 18


def build_program(with_sample=True):
    nc = bass.Bass("TRN2", target_bir_lowering=False)
    specs = {}

    def din(name, shape):
        specs[name] = (list(shape), "ExternalInput")

    def dout(name, shape):
        specs[name] = (list(shape), "ExternalOutput")

    class LazyDR(dict):
        def __missing__(self, name):
            shape, kind = specs[name]
            ap = nc.dram_tensor(name, shape, F32, kind=kind).ap()
            self[name] = ap
            return ap

    dr = LazyDR()
    din("xm", [D, NPT * TT]); din("xp", [D, NPT * TT]); din("xs", [D, TS]); din("memT", [D, 256])
    for n in ("w1g", "w1u", "w2g", "w2u", "win"):
        din(n, [D, DFF])
    for n in ("w1d", "w2d"):
        din(n, [DFF, D])
    din("wout", [D, D]); din("wmq", [D, 512]); din("wmk", [D, 512]); din("wmv", [D, 512]); din("wmo", [512, D])
    din("gv", [128, NGV]); din("cab", [128, 2 * 16 * 128]); din("chm", [128, 128]); din("crs", [128, 512])
    din("cid", [128, 128])
    din("ckT", [64, NSEQ * 4, 128]); din("cwk", [NSEQ, 128, 256]); din("cwv", [NSEQ, 128, 256])
    din("shs", [NSEQ, 8, 128, 128]); din("cmkT", [128, NSEQ * 4, 256]); din("cmv", [NSEQ, 256, 512])
    din("csc", [128, 128]); din("csn", [128, 2048]); din("chm8", [128, 128]); din("crs8", [128, 128]); din("csm", [128, 16])
    dout("ym", [D, NPT * TT]); dout("ys", [D, TS])
    dout("pwk", [128, 256]); dout("pwv", [128, 256]); dout("phg", [8, 128, 128])
    dout("pmk", [256, 512]); dout("pmv", [256, 512])
    dout("swk", [NSEQ, 128, 256]); dout("swv", [NSEQ, 128, 256]); dout("shg", [NSEQ, 8, 128, 128])
    import os
    if os.environ.get("KSTAGE", "all") == "all":
        for n in specs:
            dr[n]

    with ExitStack() as st:
        def sb(name, shape, dt):
            return st.enter_context(nc.sbuf_tensor("sb_" + name, list(shape), dt))

        xT = sb("xT", [128, NCH, TT], F32)
        hT = sb("hT", [128, NCH, TT], BF16)
        wsl = [sb(f"wsl{i}", [128, WQ.SLOT], BF16) for i in range(6)]
        gv = sb("gv", [128, NGV], F32)
        gd = sb("gd", [128, 64], F32)
        negflag = sb("negflag", [128, 1], F32)
        hmask = sb("hmask", [128, 128], F32)
        crs = sb("crs", [128, 512], F32)
        ident = sb("ident", [128, 128], BF16)
        ones = sb("ones", [128, 128], BF16)
        S32 = sb("S32", [128, 8, 128], F32)
        Sbf = sb("Sbf", [128, 8, 128], BF16)
        kT = sb("kT", [128, 4, TT + 128], BF16)
        vtok = sb("vtok", [128, 5, 256], BF16)
        KmT = sb("KmT", [128, 4, 256], BF16)
        Vmt = sb("Vmt", [128, 2, 512], BF16)
        rstd = sb("rstd", [128, TT], F32)
        lnv = sb("lnv", [128, TT], F32)
        sqt = [sb(f"sqt{i}", [128, TT], BF16) for i in range(2)]
        sgt = [sb(f"sgt{i}", [128, TT], F32) for i in range(2)]
        AW = 20224
        arena_t = sb("arena", [128, AW], F32)
        PS = [st.enter_context(nc.psum_tensor(f"ps{i}", [128, 512], F32)) for i in range(7)]
        PT7 = st.enter_context(nc.psum_tensor("ps7", [128, 1024], BF16))

        def emit(fw, wq):
            ar = Arena(arena_t, AW)
            PB = [Buf(f"pb{i}") for i in range(8)]
            bank_i = [0]

            fresh = [False] * 8

            def nb():
                i = bank_i[0] % 6
                bank_i[0] += 1
                fresh[i] = True
                return i

            b_x = [Buf(f"xT{c}") for c in range(NCH)]; b_h = [Buf(f"hT{c}") for c in range(NCH)]; b_rstd = Buf("rstd"); b_lnv = Buf("lnv")
            pend = {"ss": False, "mm": None}
            b_sq = [Buf("sq0"), Buf("sq1")]; b_sg = [Buf("sg0"), Buf("sg1")]
            b_gv = Buf("gv"); b_gd = Buf("gd"); b_const = Buf("const")
            b_S32 = [Buf(f"S32_{j}") for j in range(8)]; b_Sbf = [Buf(f"Sbf_{j}") for j in range(8)]
            b_kT = Buf("kT"); b_vtok = [Buf(f"vtok{i}") for i in range(5)]
            b_Km = Buf("Km"); b_Vm = Buf("Vm")
            cnt = {"sq": 0, "sg": 0}
            outbufs = []

            def gcol(name, c=0):
                o = GV[name] + c
                return gv[:, o:o + 1]

            fw.dma("sp", lambda e: e.dma_start(out=gv[:], in_=dr["gv"]), writes=[b_gv])
            fw.dma("sp", lambda e: e.dma_start(out=hmask[:], in_=dr["chm"]), writes=[b_const])
            fw.dma("sp", lambda e: e.dma_start(out=crs[:], in_=dr["crs"]), writes=[b_const])
            fw.dma("pool", lambda e: e.dma_start(out=ident[:], in_=dr["cid"]), writes=[b_const])
            fw.op("dve", lambda e: e.memset(ones[:], 1.0), writes=[b_const])
            fw.op("dve", lambda e: e.tensor_scalar(out=gd[:, 0:16], in0=gv[:, GV["f1post"]:GV["f1post"] + 16],
                                                   scalar1=0.5, scalar2=None, op0=ALU.mult), reads=[b_gv], writes=[b_gd])
            fw.op("dve", lambda e: e.tensor_scalar(out=gd[:, 16:32], in0=gv[:, GV["f2post"]:GV["f2post"] + 16],
                                                   scalar1=0.5, scalar2=None, op0=ALU.mult), reads=[b_gv], writes=[b_gd])
            fw.op("dve", lambda e: e.tensor_tensor(out=gd[:, 32:40], in0=gv[:, GV["lb1"]:GV["lb1"] + 8],
                                                   in1=gv[:, GV["lb0"]:GV["lb0"] + 8], op=ALU.subtract),
                  reads=[b_gv], writes=[b_gd])
            fw.op("act", lambda e: e.activation(out=gd[:, 32:40], in_=gd[:, 32:40], func=AF.Exp), reads=[b_gd], writes=[b_gd])
            fw.op("dve", lambda e: e.tensor_scalar(out=gd[:, 32:40], in0=gd[:, 32:40], scalar1=1.0, scalar2=None, op0=ALU.add),
                  reads=[b_gd], writes=[b_gd])
            fw.op("dve", lambda e: e.reciprocal(out=gd[:, 32:40], in_=gd[:, 32:40]), reads=[b_gd], writes=[b_gd])
            fw.op("dve", lambda e: e.tensor_scalar(out=gd[:, 40:48], in0=gd[:, 32:40], scalar1=-1.0, scalar2=1.0,
                                                   op0=ALU.mult, op1=ALU.add), reads=[b_gd], writes=[b_gd])
            fw.op("act", lambda e: e.activation(out=gd[:, 48:64], in_=gv[:, GV["sinks"]:GV["sinks"] + 16], func=AF.Exp),
                  reads=[b_gv], writes=[b_gd])
            fw.op("dve", lambda e: e.tensor_scalar(out=negflag[:], in0=gv[:, GV["flag"]:GV["flag"] + 1], scalar1=-1.0,
                                                   scalar2=-NEG, op0=ALU.add, op1=ALU.mult), reads=[b_gv], writes=[b_gd])

            def wgroup(w, g, width=256):
                return (dr[w].rearrange("(c p) n -> p c n", p=128)[:, :, g * width:(g + 1) * width], (w, g))

            def mm(out, lhsT, rhs, start, stop, reads, pb, **kw):
                idx = PB.index(pb)
                st_ = fresh[idx]
                if st_:
                    assert start
                    fresh[idx] = False
                kw.setdefault("skip_group_check", True)
                fw.op("pe", lambda e: e.matmul(out, lhsT=lhsT, rhs=rhs, start=st_, stop=stop, **kw),
                      reads=reads, writes=[pb])

            def proj_fm(wt, bw, col0, M, T, c0=0):
                ib = nb()
                for c in range(NCH):
                    mm(PS[ib][0:M, 0:T], wt[:, c, col0:col0 + M], hT[:, c, c0:c0 + T], c == 0, c == NCH - 1, [bw, b_h[c]], PB[ib])
                return ib

            def proj_tm(wt, bw, blk, N):
                ib = nb()
                for c in range(NCH):
                    mm(PS[ib][:, 0:N], hT[:, c, blk * 128:(blk + 1) * 128], wt[:, c, 0:N], c == 0, c == NCH - 1, [bw, b_h[c]], PB[ib])
                return ib

            def sumsq_acc(cap, nparts, T, bufs_in, first, last):
                if first:
                    fresh[6] = True
                    pend["mm"] = None
                k = cnt["sq"] % 2
                cnt["sq"] += 1
                fw.op("act", lambda e: e.activation(out=sqt[k][0:nparts, 0:T], in_=cap, func=AF.Square),
                      reads=bufs_in, writes=[b_sq[k]])

                def emit_mm(kk, f_, l_):
                    mm(PS[6][0:nparts, 0:T], ones[0:nparts, 0:nparts], sqt[kk][0:nparts, 0:T], f_, l_, [b_sq[kk], b_const], PB[6])

                if pend["mm"] is not None:
                    emit_mm(*pend["mm"])
                pend["mm"] = (k, first, last)
                if last:
                    emit_mm(*pend["mm"])
                    pend["mm"] = None

            def rstd_finish(nparts, Dn, T):
                fw.op("act", lambda e: e.activation(out=lnv[0:nparts, 0:T], in_=PS[6][0:nparts, 0:T], func=AF.Ln,
                                                    scale=1.0 / Dn, bias=EPS), reads=[PB[6]], writes=[b_lnv])
                fw.op("act", lambda e: e.activation(out=rstd[0:nparts, 0:T], in_=lnv[0:nparts, 0:T], func=AF.Exp, scale=-0.5),
                      reads=[b_lnv], writes=[b_rstd])

            def sumsq_rstd(chunks, nparts, Dn, T, bufs_in):
                n = len(chunks)
                for i, cap in enumerate(chunks):
                    sumsq_acc(cap, nparts, T, bufs_in, i == 0, i == n - 1)
                rstd_finish(nparts, Dn, T)

            def prenorm(gname, T):
                if pend["ss"]:
                    pend["ss"] = False
                else:
                    for c in range(NCH):
                        sumsq_acc(xT[:, c, 0:T], 128, T, [b_x[c]], c == 0, c == NCH - 1)
                rstd_finish(128, D, T)
                for c in range(NCH):
                    fw.op("dve", lambda e, c=c: e.scalar_tensor_tensor(out=hT[:, c, 0:T], in0=xT[:, c, 0:T], scalar=gcol(gname, c),
                                                                       in1=rstd[:, 0:T], op0=ALU.mult, op1=ALU.mult),
                          reads=[b_x[c], b_rstd, b_gv], writes=[b_h[c]])

            def postnorm_residual(fT, b_f, gap_fn, T, fuse_next=True):
                rstd_finish(128, D, T)
                for c in range(NCH):
                    fw.op("dve", lambda e, c=c: e.scalar_tensor_tensor(out=fT[:, c, 0:T], in0=fT[:, c, 0:T], scalar=gap_fn(c),
                                                                       in1=rstd[:, 0:T], op0=ALU.mult, op1=ALU.mult),
                          reads=[b_f[c], b_rstd, b_gv, b_gd], writes=[b_f[c]])
                    fw.op("dve", lambda e, c=c: e.tensor_tensor(out=xT[:, c, 0:T], in0=xT[:, c, 0:T], in1=fT[:, c, 0:T], op=ALU.add),
                          reads=[b_f[c], b_x[c]], writes=[b_x[c]])
                    if fuse_next:
                        sumsq_acc(xT[:, c, 0:T], 128, T, [b_x[c]], c == 0, c == NCH - 1)
                pend["ss"] = fuse_next

            def ffn(T, gpre, gpost_fn, wg, wu, wd, fuse_next=True):
                m0 = ar.mark()
                fT = ar.alloc(NCH * TT, F32).rearrange("p (c t) -> p c t", t=TT)
                actT = ar.alloc(NFF * TT, BF16).rearrange("p (c t) -> p c t", t=TT)
                b_f, b_act = [Buf(f"fT{m}") for m in range(NCH)], Buf("actT")
                prenorm(gpre, T)
                for g in range(NFF // 2):
                    wgs, bg = wq.get(wgroup(wg, g), 128, NCH, 256)
                    wus, bu = wq.get(wgroup(wu, g), 128, NCH, 256)
                    if g == 0:
                        first_banks = [nb() for _ in range(4)]
                        for c in range(NCH):
                            for q4, (wt_, bw_, col) in enumerate(((wgs, bg, 0), (wus, bu, 0), (wgs, bg, 128), (wus, bu, 128))):
                                ib4 = first_banks[q4]
                                mm(PS[ib4][:, 0:T], wt_[:, c, col:col + 128], hT[:, c, 0:T], c == 0, c == NCH - 1, [bw_, b_h[c]], PB[ib4])
                    for jj in range(2):
                        j = g * 2 + jj
                        if g == 0:
                            ig, iu = first_banks[2 * jj], first_banks[2 * jj + 1]
                        else:
                            ig = proj_fm(wgs, bg, jj * 128, 128, T)
                            iu = proj_fm(wus, bu, jj * 128, 128, T)
                        k = cnt["sg"] % 2
                        cnt["sg"] += 1
                        fw.op("act", lambda e, ig=ig, k=k: e.activation(out=sgt[k][:, 0:T], in_=PS[ig][:, 0:T], func=AF.Silu),
                              reads=[PB[ig]], writes=[b_sg[k]])
                        fw.op("dve", lambda e, iu=iu, k=k, j=j: e.tensor_tensor(out=actT[:, j, 0:T], in0=sgt[k][:, 0:T], in1=PS[iu][:, 0:T],
                                                                                op=ALU.mult),
                              reads=[b_sg[k], PB[iu]], writes=[b_act])
                wdv = dr[wd].rearrange("(j p) n -> p j n", p=128)
                for m in range(NCH):
                    ib = nb()
                    for half in range(2):
                        wds, bd = wq.get((wdv[:, half * 22:(half + 1) * 22, m * 128:(m + 1) * 128], (wd, m, half)), 128, 22, 128)
                        for jj in range(22):
                            j = half * 22 + jj
                            mm(PS[ib][:, 0:T], wds[:, jj, :], actT[:, j, 0:T], j == 0, j == NFF - 1, [bd, b_act], PB[ib])
                    fw.op("act", lambda e, ib=ib, m=m: e.activation(out=fT[:, m, 0:T], in_=PS[ib][:, 0:T], func=AF.Copy),
                          reads=[PB[ib]], writes=[b_f[m]])
                    sumsq_acc(fT[:, m, 0:T], 128, T, [b_f[m]], m == 0, m == NCH - 1)
                postnorm_residual(fT, b_f, gpost_fn, T, fuse_next)
                fw.barrier()
                ar.reset(m0)

            def hgrn_pair(T, hp, need_out, vh_tok, b_vh, tmp, b_tmp, chunk=64, crs_ap=None, pre=None):
                nblk = T // 128
                nchk = T // chunk
                if crs_ap is None:
                    crs_ap = crs
                wf, bwf = wq.get(wgroup("win", GF + hp), 128, NCH, 256)
                if need_out:
                    wqh, bwq = wq.get(wgroup("win", GQH + hp), 128, NCH, 256)
                heads = []
                for jj in range(2):
                    j = hp * 2 + jj
                    if pre is not None:
                        hd = pre[jj]
                        hd["j"] = j
                        hd["tA"], hd["tB"], hd["tC"], hd["tD"], hd["kh"] = tmp[jj]
                        hd["btmp"] = b_tmp[jj]
                        heads.append(hd)
                        continue
                    hd = {"bt": Buf(f"hd{j}"), "j": j}
                    hd["eLend"] = ar.alloc(16, F32)
                    hd["kt"] = ar.alloc(TT, BF16)
                    hd["khtok"] = ar.alloc(nblk * 128, BF16).rearrange("p (b k) -> p b k", k=128)
                    if need_out:
                        hd["qt"] = ar.alloc(TT, BF16)
                        hd["o32"] = ar.alloc(TT, F32)
                        hd["gs"] = ar.alloc(TT, BF16)
                    hd["tA"], hd["tB"], hd["tC"], hd["tD"], hd["kh"] = tmp[jj]
                    hd["btmp"] = b_tmp[jj]
                    heads.append(hd)
                R2 = range(2)
                ibs = [proj_fm(wf, bwf, jj * 128, 128, T) for jj in R2]
                for jj in R2:
                    hd = heads[jj]
                    fw.op("act", lambda e, hd=hd, ib=ibs[jj]: e.activation(out=hd["tA"][:, 0:T], in_=PS[ib][:, 0:T], func=AF.Sigmoid),
                          reads=[PB[ibs[jj]]], writes=[hd["btmp"]])
                    fw.op("act", lambda e, hd=hd, ib=ibs[jj]: e.activation(out=hd["tB"][:, 0:T], in_=PS[ib][:, 0:T], func=AF.Sigmoid, scale=-1.0),
                          reads=[PB[ibs[jj]]], writes=[hd["btmp"]])
                for jj in R2:
                    hd = heads[jj]
                    j = hd["j"]
                    fw.op("act", lambda e, hd=hd, j=j: e.activation(out=hd["tA"][:, 0:T], in_=hd["tA"][:, 0:T], func=AF.Ln,
                                                                    scale=gd[:, 40 + j:41 + j], bias=gd[:, 32 + j:33 + j]),
                          reads=[hd["btmp"], b_gd], writes=[hd["btmp"]])
                for jj in R2:
                    hd = heads[jj]
                    fw.op("dve", lambda e, hd=hd: e.tensor_tensor_scan(out=hd["tC"][:, 0:T], data0=crs_ap[:, 0:T], data1=hd["tA"][:, 0:T],
                                                                       initial=0.0, op0=ALU.mult, op1=ALU.add),
                          reads=[hd["btmp"], b_const], writes=[hd["btmp"]])
                for jj in R2:
                    hd = heads[jj]
                    fw.op("act", lambda e, hd=hd: e.activation(out=hd["tD"][:, 0:T], in_=hd["tC"][:, 0:T], func=AF.Exp, scale=-1.0),
                          reads=[hd["btmp"]], writes=[hd["btmp"]])
                    fw.op("act", lambda e, hd=hd: e.activation(
                        out=hd["eLend"][:, 0:nchk], in_=hd["tC"][:, 0:T].rearrange("p (c t) -> p c t", t=chunk)[:, :, chunk - 1], func=AF.Exp),
                        reads=[hd["btmp"]], writes=[hd["bt"]])
                    if need_out:
                        fw.op("act", lambda e, hd=hd: e.activation(out=hd["tA"][:, 0:T], in_=hd["tC"][:, 0:T], func=AF.Exp),
                              reads=[hd["btmp"]], writes=[hd["btmp"]])
                for jj in R2:
                    hd = heads[jj]
                    j = hd["j"]
                    fw.op("dve", lambda e, hd=hd, j=j: e.scalar_tensor_tensor(out=hd["kt"][:, 0:T], in0=hd["tB"][:, 0:T], scalar=gd[:, 40 + j:41 + j],
                                                                              in1=hd["tD"][:, 0:T], op0=ALU.mult, op1=ALU.mult),
                          reads=[hd["btmp"], b_gd], writes=[hd["bt"]])
                    fw.op("dve", lambda e, hd=hd: e.tensor_tensor(
                        out=hd["kh"][:, 0:T].rearrange("p (c t) -> p c t", t=chunk), in0=hd["kt"][:, 0:T].rearrange("p (c t) -> p c t", t=chunk),
                        in1=hd["eLend"][:, 0:nchk].unsqueeze(2).to_broadcast([128, nchk, chunk]), op=ALU.mult),
                        reads=[hd["bt"]], writes=[hd["btmp"]])
                for jj in R2:
                    hd = heads[jj]
                    for blk in range(nblk):
                        fw.op("pe", lambda e, blk=blk, hd=hd, jj=jj: e.transpose(out=PT7[:, jj * 512 + blk * 128:jj * 512 + (blk + 1) * 128],
                                                                              in_=hd["kh"][:, blk * 128:(blk + 1) * 128], identity=ident[:]),
                              reads=[hd["btmp"], b_const], writes=[PB[7]])
                for jj in R2:
                    hd = heads[jj]
                    fw.op("act", lambda e, hd=hd, jj=jj: e.activation(out=hd["khtok"],
                                                                      in_=PT7[:, jj * 512:jj * 512 + nblk * 128].rearrange("p (b k) -> p b k", k=128),
                                                                      func=AF.Copy), reads=[PB[7]], writes=[hd["bt"]])
                if need_out:
                    ibq = [proj_fm(wqh, bwq, jj * 128, 128, T) for jj in R2]
                    for jj in R2:
                        hd = heads[jj]
                        fw.op("act", lambda e, hd=hd, ib=ibq[jj]: e.activation(out=hd["tB"][:, 0:T], in_=PS[ib][:, 0:T], func=AF.Silu),
                              reads=[PB[ibq[jj]]], writes=[hd["btmp"]])
                    for jj in R2:
                        hd = heads[jj]
                        fw.op("dve", lambda e, hd=hd: e.tensor_tensor(out=hd["qt"][:, 0:T], in0=hd["tB"][:, 0:T], in1=hd["tA"][:, 0:T], op=ALU.mult),
                              reads=[hd["btmp"]], writes=[hd["bt"]])
                wi, bwi = wq.get(wgroup("win", GI + hp), 128, NCH, 256)
                for blk in range(nblk):
                    ib = proj_tm(wi, bwi, blk, 256)
                    fw.op("act", lambda e, ib=ib, blk=blk: e.activation(out=vh_tok[:, blk, :], in_=PS[ib][:, 0:256], func=AF.Copy),
                          reads=[PB[ib]], writes=[b_vh])
                if need_out:
                    wgh, bwg = wq.get(wgroup("win", GG + hp), 128, NCH, 256)
                    for jj in R2:
                        hd = heads[jj]
                        ib = proj_fm(wgh, bwg, jj * 128, 128, T)
                        fw.op("act", lambda e, ib=ib, hd=hd: e.activation(out=hd["gs"][:, 0:T], in_=PS[ib][:, 0:T], func=AF.Silu),
                              reads=[PB[ib]], writes=[hd["bt"]])
                return heads

            def state_chain(hd, vh_tok, b_vh, jj, keep):
                j = hd["j"]
                iu = [nb(), nb()]
                for c in range(8):
                    blk, e2 = c // 2, c % 2
                    r0 = 64 * e2
                    mm(PS[iu[c % 2]][:, (c // 2) * 128:(c // 2 + 1) * 128], hd["khtok"][r0:r0 + 64, blk, :],
                       vh_tok[r0:r0 + 64, blk, jj * 128:(jj + 1) * 128], True, True, [hd["bt"], b_vh], PB[iu[c % 2]])
                cp(13)
                if keep:
                    Sall = ar.alloc(9 * 128, F32).rearrange("p (c v) -> p c v", v=128)
                    Sallbf = ar.alloc(8 * 128, BF16).rearrange("p (c v) -> p c v", v=128)
                    b_Sall, b_Sbf2 = Buf("Sall"), Buf("Sallbf")
                    fw.op("dve", lambda e: e.tensor_copy(out=Sall[:, 0, :], in_=S32[:, j, :]), reads=[b_S32[j]], writes=[b_Sall])
                    for c in range(8):
                        fw.op("dve", lambda e, c=c: e.scalar_tensor_tensor(out=Sall[:, c + 1, :], in0=Sall[:, c, :], scalar=hd["eLend"][:, c:c + 1],
                                                                           in1=PS[iu[c % 2]][:, (c // 2) * 128:(c // 2 + 1) * 128], op0=ALU.mult, op1=ALU.add),
                              reads=[b_Sall, hd["bt"], PB[iu[c % 2]]], writes=[b_Sall])
                    cp(14)
                    fw.op("act", lambda e: e.activation(out=Sallbf, in_=Sall[:, 0:8, :], func=AF.Copy), reads=[b_Sall], writes=[b_Sbf2])
                    cp(15)
                    fw.op("dve", lambda e: e.tensor_copy(out=S32[:, j, :], in_=Sall[:, 8, :]), reads=[b_Sall], writes=[b_S32[j]])
                    return Sallbf, b_Sbf2
                for c in range(8):
                    fw.op("dve", lambda e, c=c: e.scalar_tensor_tensor(out=S32[:, j, :], in0=S32[:, j, :], scalar=hd["eLend"][:, c:c + 1],
                                                                       in1=PS[iu[c % 2]][:, (c // 2) * 128:(c // 2 + 1) * 128], op0=ALU.mult, op1=ALU.add),
                          reads=[b_S32[j], hd["bt"], PB[iu[c % 2]]], writes=[b_S32[j]])
                return None, None

            def state_chain_snap(hd, vh_tok, b_vh, jj):
                j = hd["j"]
                iu = [nb(), nb()]
                for c in range(8):
                    blk, e2 = c // 2, c % 2
                    r0 = 64 * e2
                    mm(PS[iu[c % 2]][:, (c // 2) * 128:(c // 2 + 1) * 128], hd["khtok"][r0:r0 + 64, blk, :],
                       vh_tok[r0:r0 + 64, blk, jj * 128:(jj + 1) * 128], True, True, [hd["bt"], b_vh], PB[iu[c % 2]])
                Sab, b_Sab = hd["Sab"], hd["b_Sab"]
                for c in range(8):
                    fw.op("dve", lambda e, c=c: e.tensor_copy(out=Sab[:, c, :], in_=S32[:, j, :]), reads=[b_S32[j]], writes=[b_Sab])
                    fw.op("dve", lambda e, c=c: e.scalar_tensor_tensor(out=S32[:, j, :], in0=S32[:, j, :], scalar=hd["eLend"][:, c:c + 1],
                                                                       in1=PS[iu[c % 2]][:, (c // 2) * 128:(c // 2 + 1) * 128], op0=ALU.mult, op1=ALU.add),
                          reads=[b_S32[j], hd["bt"], PB[iu[c % 2]]], writes=[b_S32[j]])
                return Sab, b_Sab

            def hgrn_tmp():
                tmp = [[ar.alloc(TT, F32) for _ in range(4)] + [ar.alloc(TT, BF16)] for _ in range(2)]
                return tmp, [Buf("hgtmp0"), Buf("hgtmp1")]

            def kv_proj(T, blks, prev_slot=False, out_last=False):
                wk_, bk_ = wq.get(wgroup("win", GK), 128, NCH, 256)
                wv_, bv_ = wq.get(wgroup("win", GV_), 128, NCH, 256)
                if prev_slot:
                    c0, n, dst0 = T - 128, 128, 0
                else:
                    c0, n, dst0 = 0, T, 128
                for kvh in range(4):
                    ib = proj_fm(wk_, bk_, kvh * 64, 64, n, c0)
                    fw.op("dve", lambda e, ib=ib, kvh=kvh: e.tensor_copy(out=kT[0:64, kvh, dst0:dst0 + n], in_=PS[ib][0:64, 0:n]),
                          reads=[PB[ib]], writes=[b_kT])
                for blk in blks:
                    slot = 0 if prev_slot else blk + 1
                    iv = proj_tm(wv_, bv_, blk, 256)
                    fw.op("act", lambda e, iv=iv, slot=slot: e.activation(out=vtok[:, slot, :], in_=PS[iv][:, 0:256], func=AF.Copy),
                          reads=[PB[iv]], writes=[b_vtok[slot]])
                    if out_last and blk == blks[-1]:
                        ik = proj_tm(wk_, bk_, blk, 256)
                        kvl = ar.alloc(512, F32)
                        b_kvl = Buf("kvl")
                        fw.op("dve", lambda e, ik=ik, kvl=kvl: e.tensor_copy(out=kvl[:, 0:256], in_=PS[ik][:, 0:256]), reads=[PB[ik]], writes=[b_kvl])
                        fw.op("act", lambda e, iv=iv, kvl=kvl: e.activation(out=kvl[:, 256:512], in_=PS[iv][:, 0:256], func=AF.Copy), reads=[PB[iv]], writes=[b_kvl])
                        bo = Buf("o_pwk")
                        fw.dma("sp", lambda e, kvl=kvl: e.dma_start(out=dr["pwk"], in_=kvl[:, 0:256]), reads=[b_kvl], writes=[bo])
                        fw.dma("sp", lambda e, kvl=kvl: e.dma_start(out=dr["pwv"], in_=kvl[:, 256:512]), reads=[b_kvl], writes=[bo])
                        outbufs.append(bo)

            def pass0_tile(t, last):
                T = TT
                fw.dma("sp", lambda e: e.dma_start(out=xT[:], in_=dr["xp"].rearrange("(c p) n -> p c n", p=128)[:, :, t * TT:(t + 1) * TT]),
                       writes=b_x)
                ffn(T, "f1pre", lambda c: gd[:, c:c + 1], "w1g", "w1u", "w1d")
                prenorm("mpre", T)
                m0 = ar.mark()
                tmps = [hgrn_tmp(), hgrn_tmp()]
                for hp in range(4):
                    vh_tok = ar.alloc(4 * 256, BF16).rearrange("p (b k) -> p b k", k=256)
                    b_vh = Buf("vh")
                    tmp, b_tmp = tmps[hp % 2]
                    heads = hgrn_pair(T, hp, False, vh_tok, b_vh, tmp, b_tmp)
                    for jj in range(2):
                        state_chain(heads[jj], vh_tok, b_vh, jj, False)
                if last:
                    kv_proj(T, [3], prev_slot=True)
                fw.barrier()
                ar.reset(m0)

            def wout_stage(T, mixA, mixR, b_mixA, b_mixR):
                mixA2 = ar.alloc(8 * T, BF16).rearrange("p (h t) -> p h t", t=T)
                b_mixA2 = Buf("mixA2")
                mv = mixA[0:64, :, 0:T].rearrange("p (a two) t -> p a two t", two=2)
                fw.dma("sp", lambda e: e.dma_start(out=mixA2[0:64, :, :], in_=mv[:, :, 0, :]), reads=[b_mixA], writes=[b_mixA2])
                fw.dma("sp", lambda e: e.dma_start(out=mixA2[64:128, :, :], in_=mv[:, :, 1, :]), reads=[b_mixA], writes=[b_mixA2])
                fT = ar.alloc(NCH * TT, F32).rearrange("p (c t) -> p c t", t=TT)
                b_f = [Buf(f"fT{m}") for m in range(NCH)]
                for g in range(8):
                    wo_, bwo_ = wq.get(wgroup("wout", g), 128, NCH, 256)
                    for mm_ in range(2):
                        m = g * 2 + mm_
                        ib = nb()
                        for c in range(8):
                            mm(PS[ib][:, 0:T], wo_[:, c, mm_ * 128:(mm_ + 1) * 128], mixA2[:, c, 0:T], c == 0, False, [bwo_, b_mixA2], PB[ib])
                        for j in range(8):
                            mm(PS[ib][:, 0:T], wo_[:, 8 + j, mm_ * 128:(mm_ + 1) * 128], mixR[:, j, 0:T], False, j == 7, [bwo_, b_mixR], PB[ib])
                        fw.op("act", lambda e, ib=ib, m=m: e.activation(out=fT[:, m, 0:T], in_=PS[ib][:, 0:T], func=AF.Copy),
                              reads=[PB[ib]], writes=[b_f[m]])
                        sumsq_acc(fT[:, m, 0:T], 128, T, [b_f[m]], m == 0, m == NCH - 1)
                postnorm_residual(fT, b_f, lambda c: gcol("mpost", c), T)
                fw.barrier()

            def main_tile(t):
                import os
                skip = os.environ.get("KSKIP", "").split(",")
                T = TT
                last = (t == NPT - 1)
                fw.dma("sp", lambda e: e.dma_start(out=xT[:], in_=dr["xm"].rearrange("(c p) n -> p c n", p=128)[:, :, t * TT:(t + 1) * TT]),
                       writes=b_x)
                if "ffn1" not in skip:
                    ffn(T, "f1pre", lambda c: gd[:, c:c + 1], "w1g", "w1u", "w1d")
                if "mix" in skip:
                    return main_tail(t, skip, T)
                prenorm("mpre", T)
                mA = ar.mark()
                mixA = ar.alloc(16 * TT, BF16).rearrange("p (h t) -> p h t", t=TT)
                mixR = ar.alloc(8 * TT, BF16).rearrange("p (h t) -> p h t", t=TT)
                b_mixA, b_mixR = Buf("mixA"), Buf("mixR")
                m1 = ar.mark()
                abias = ar.alloc(2 * 16 * 128, F32).rearrange("p (k h q) -> p k h q", k=2, h=16)
                b_ab = Buf("abias")
                fw.dma("sp", lambda e: e.dma_start(out=abias.rearrange("p k h q -> p (k h q)"), in_=dr["cab"]), writes=[b_ab])
                qT = ar.alloc(16 * TT, BF16).rearrange("p (h t) -> p h t", t=TT)
                b_qT = Buf("qT")
                b_qs = Buf("qstage")
                for g in range(4):
                    wqa, bwqa = wq.get(wgroup("win", GQA + g), 128, NCH, 256)
                    for pp in range(2):
                        h0 = g * 4 + pp * 2
                        ib = proj_fm(wqa, bwqa, pp * 128, 128, T)
                        fw.op("act", lambda e, ib=ib, h0=h0: e.activation(out=qT[0:64, h0, 0:T], in_=PS[ib][0:64, 0:T], func=AF.Copy, scale=0.125),
                              reads=[PB[ib]], writes=[b_qT])
                        fw.op("act", lambda e, ib=ib, h0=h0: e.activation(out=qT[64:128, h0 + 1, 0:T], in_=PS[ib][64:128, 0:T], func=AF.Copy, scale=0.125),
                              reads=[PB[ib]], writes=[b_qs])
                qv = qT[:, :, 0:T].rearrange("p (a two) t -> p a two t", two=2)
                fw.dma("sp", lambda e: e.dma_start(out=qv[0:64, :, 1, :], in_=qv[64:128, :, 1, :]), reads=[b_qs], writes=[b_qT])
                kv_proj(T, [0, 1, 2, 3], out_last=last)
                o32 = ar.alloc(16 * 128, F32).rearrange("p (h q) -> p h q", q=128)
                b_o32 = Buf("o32")
                scb = [ar.alloc(512, F32) for _ in range(2)]
                PTt = [ar.alloc(512, BF16) for _ in range(2)]
                b_scb = [Buf("scb0"), Buf("scb1")]
                b_PT = [Buf("PT0"), Buf("PT1")]
                rden = ar.alloc(512, F32)
                b_rden = Buf("rden")
                sq16 = ar.alloc(16 * 128, BF16).rearrange("p (h q) -> p h q", q=128)
                b_sq16 = Buf("sq16")
                def att_A(blk, kvh):
                    qs = slice(blk * 128, (blk + 1) * 128)
                    ibs = [nb(), nb()]
                    for kc in range(2):
                        ks = slice(blk * 128 + kc * 128, blk * 128 + kc * 128 + 128)
                        for g in range(4):
                            h = 4 * kvh + g
                            mm(PS[ibs[kc]][:, g * 128:(g + 1) * 128], kT[0:64, kvh, ks], qT[0:64, h, qs], True, True, [b_kT, b_qT], PB[ibs[kc]])
                        fw.op("dve", lambda e, kc=kc, ib=ibs[kc]: e.tensor_tensor(
                            out=scb[kc].rearrange("p (g q) -> p g q", q=128), in0=PS[ib][:, :].rearrange("p (g q) -> p g q", q=128),
                            in1=abias[:, kc, 4 * kvh:4 * kvh + 4, :], op=ALU.add), reads=[PB[ibs[kc]], b_ab], writes=[b_scb[kc]])
                        if t == 0 and blk == 0 and kc == 0:
                            fw.op("act", lambda e, kc=kc: e.activation(out=PTt[kc], in_=scb[kc], func=AF.Exp, bias=negflag[:, 0:1]),
                                  reads=[b_scb[kc], b_gd], writes=[b_PT[kc]])
                        else:
                            fw.op("act", lambda e, kc=kc: e.activation(out=PTt[kc], in_=scb[kc], func=AF.Exp),
                                  reads=[b_scb[kc]], writes=[b_PT[kc]])
                    io, idn = nb(), nb()
                    for kc in range(2):
                        for g in range(4):
                            mm(PS[io][0:64, g * 128:(g + 1) * 128], vtok[:, blk + kc, kvh * 64:(kvh + 1) * 64], PTt[kc][:, g * 128:(g + 1) * 128],
                               kc == 0, kc == 1, [b_vtok[blk + kc], b_PT[kc]], PB[io])
                    for kc in range(2):
                        for g in range(4):
                            mm(PS[idn][0:64, g * 128:(g + 1) * 128], ones[:, 0:64], PTt[kc][:, g * 128:(g + 1) * 128],
                               kc == 0, kc == 1, [b_const, b_PT[kc]], PB[idn])
                    return io, idn

                def att_B(blk, kvh, io, idn):
                    fw.op("dve", lambda e: e.tensor_tensor(
                        out=rden[0:64, :].rearrange("p (g q) -> p g q", q=128), in0=PS[idn][0:64, :].rearrange("p (g q) -> p g q", q=128),
                        in1=gd[0:64, 48 + 4 * kvh:52 + 4 * kvh].unsqueeze(2).to_broadcast([64, 4, 128]), op=ALU.add),
                        reads=[PB[idn], b_gd], writes=[b_rden])
                    if kvh % 2 == 0:
                        fw.op("act", lambda e: e.activation(out=rden[0:64, :], in_=rden[0:64, :], func=AF.Ln), reads=[b_rden], writes=[b_rden])
                        fw.op("act", lambda e: e.activation(out=rden[0:64, :], in_=rden[0:64, :], func=AF.Exp, scale=-1.0), reads=[b_rden], writes=[b_rden])
                        fw.op("dve", lambda e: e.tensor_tensor(
                            out=o32[0:64, 4 * kvh:4 * kvh + 4, :], in0=PS[io][0:64, :].rearrange("p (g q) -> p g q", q=128),
                            in1=rden[0:64, :].rearrange("p (g q) -> p g q", q=128), op=ALU.mult), reads=[PB[io], b_rden], writes=[b_o32])
                    else:
                        fw.op("dve", lambda e: e.reciprocal(out=rden[0:64, :], in_=rden[0:64, :]), reads=[b_rden], writes=[b_rden])
                        fw.op("dve", lambda e: e.tensor_tensor(
                            out=o32[0:64, 4 * kvh:4 * kvh + 4, :], in0=PS[io][0:64, :].rearrange("p (g q) -> p g q", q=128),
                            in1=rden[0:64, :].rearrange("p (g q) -> p g q", q=128), op=ALU.mult), reads=[PB[io], b_rden], writes=[b_o32])
                    if kvh == 3:
                        qs = slice(blk * 128, (blk + 1) * 128)
                        fw.op("act", lambda e: e.activation(out=sq16[0:64], in_=o32[0:64], func=AF.Square), reads=[b_o32], writes=[b_sq16])
                        fresh[6] = True
                        for h in range(16):
                            mm(PS[6][0:64, 0:128], ones[0:64, 0:64], sq16[0:64, h, :], h == 0, h == 15, [b_sq16, b_const], PB[6])
                        rstd_finish(64, 1024, 128)
                        fw.op("dve", lambda e: e.tensor_tensor(out=o32[0:64], in0=o32[0:64], in1=rstd[0:64, 0:128].unsqueeze(1).to_broadcast([64, 16, 128]),
                                                               op=ALU.mult), reads=[b_o32, b_rstd], writes=[b_o32])
                        fw.op("dve", lambda e: e.tensor_tensor(out=mixA[0:64, :, qs], in0=o32[0:64],
                                                               in1=gv[0:64, GV["again"]:GV["again"] + 16].unsqueeze(2).to_broadcast([64, 16, 128]),
                                                               op=ALU.mult), reads=[b_o32, b_gv], writes=[b_mixA])

                items = [(blk, kvh) for blk in range(4) for kvh in range(4)]
                prev = None
                for it in items:
                    cur = (it, att_A(*it))
                    if prev is not None:
                        att_B(*prev[0], *prev[1])
                    prev = cur
                att_B(*prev[0], *prev[1])
                fw.op("dve", lambda e: e.tensor_copy(out=kT[0:64, :, 0:128], in_=kT[0:64, :, T:T + 128]), reads=[b_kT], writes=[b_kT])
                fw.op("dve", lambda e: e.tensor_copy(out=vtok[:, 0, :], in_=vtok[:, 4, :]), reads=[b_vtok[4]], writes=[b_vtok[0]])
                fw.barrier()
                ar.reset(m1)
                tmp, b_tmp = hgrn_tmp()
                Am = [ar.alloc(128, BF16) for _ in range(2)]
                b_Am = [Buf("Am0"), Buf("Am1")]
                ai = 0
                regions = []
                for r in range(2):
                    vh_tok = ar.alloc(4 * 256, BF16).rearrange("p (b k) -> p b k", k=256)
                    pre = []
                    for jj in range(2):
                        hd = {"bt": Buf(f"hdr{r}{jj}")}
                        hd["eLend"] = ar.alloc(16, F32)
                        hd["kt"] = ar.alloc(TT, BF16)
                        hd["khtok"] = ar.alloc(4 * 128, BF16).rearrange("p (b k) -> p b k", k=128)
                        hd["qt"] = ar.alloc(TT, BF16)
                        hd["o32"] = ar.alloc(TT, F32)
                        hd["gs"] = ar.alloc(TT, BF16)
                        hd["Sab"] = ar.alloc(8 * 128, BF16).rearrange("p (c v) -> p c v", v=128)
                        hd["b_Sab"] = Buf(f"Sab{r}{jj}")
                        pre.append(hd)
                    regions.append((vh_tok, Buf(f"vh{r}"), pre))
                for hp in range(4):
                    vh_tok, b_vh, pre = regions[hp % 2]
                    heads = hgrn_pair(T, hp, True, vh_tok, b_vh, tmp, b_tmp, pre=pre)
                    sall = [state_chain_snap(heads[jj], vh_tok, b_vh, jj) for jj in range(2)]
                    for blk in range(4):
                        bs = slice(blk * 128, (blk + 1) * 128)
                        for jj in range(2):
                            j = hp * 2 + jj
                            hd = heads[jj]
                            bt = hd["bt"]
                            Sab, b_Sab = sall[jj]
                            ia = nb()
                            mm(PS[ia][:, 0:128], hd["kt"][:, bs], hd["qt"][:, bs], True, True, [bt], PB[ia])
                            a = ai % 2
                            ai += 1
                            fw.op("dve", lambda e, ia=ia, a=a: e.tensor_tensor(out=Am[a], in0=PS[ia][:, 0:128], in1=hmask[:], op=ALU.mult),
                                  reads=[PB[ia], b_const], writes=[b_Am[a]])
                            io = nb()
                            mm(PS[io][:, 0:128], vh_tok[:, blk, jj * 128:(jj + 1) * 128], Am[a], True, False, [b_vh, b_Am[a]], PB[io])
                            mm(PS[io][:, 0:64], Sab[:, 2 * blk, :], hd["qt"][:, blk * 128:blk * 128 + 64], False, True, [b_Sab, bt], PB[io])
                            mm(PS[io][:, 64:128], Sab[:, 2 * blk + 1, :], hd["qt"][:, blk * 128 + 64:blk * 128 + 128], False, True, [b_Sab, bt], PB[io])
                            fw.op("act", lambda e, io=io, hd=hd, bs=bs: e.activation(out=hd["o32"][:, bs], in_=PS[io][:, 0:128], func=AF.Copy),
                                  reads=[PB[io]], writes=[bt])
                    for jj in range(2):
                        j = hp * 2 + jj
                        hd = heads[jj]
                        bt = hd["bt"]
                        sumsq_rstd([hd["o32"][:, 0:T]], 128, 128, T, [bt])
                        fw.op("dve", lambda e, hd=hd: e.scalar_tensor_tensor(out=hd["o32"][:, 0:T], in0=hd["o32"][:, 0:T], scalar=gcol("hgain"),
                                                                             in1=rstd[:, 0:T], op0=ALU.mult, op1=ALU.mult),
                              reads=[bt, b_rstd, b_gv], writes=[bt])
                        fw.op("dve", lambda e, hd=hd, j=j: e.tensor_tensor(out=mixR[:, j, 0:T], in0=hd["o32"][:, 0:T], in1=hd["gs"][:, 0:T], op=ALU.mult),
                              reads=[bt], writes=[b_mixR])
                fw.barrier()
                if last:
                    bo = Buf("o_phg")
                    fw.dma("sp", lambda e: e.dma_start(out=dr["phg"].rearrange("h k v -> k h v"), in_=S32[:]), reads=b_S32, writes=[bo])
                    outbufs.append(bo)
                ar.reset(m1)
                ar.reset(m1)
                wout_stage(T, mixA, mixR, b_mixA, b_mixR)
                ar.reset(mA)
                main_tail(t, skip, T)

            def main_tail(t, skip, T):
                if "mem" not in skip:
                    mem_attn(T, KmT, Vmt, b_Km, b_Vm)
                if "ffn2" not in skip:
                    ffn(T, "f2pre", lambda c: gd[:, 16 + c:17 + c], "w2g", "w2u", "w2d", fuse_next=False)
                bo = Buf("o_ym")
                fw.dma("sp", lambda e: e.dma_start(out=dr["ym"].rearrange("(c p) n -> p c n", p=128)[:, :, t * TT:(t + 1) * TT], in_=xT[:]),
                       reads=b_x, writes=[bo])
                outbufs.append(bo)

            def mem_attn(T, KT, Vt, b_K, b_V):
                m0 = ar.mark()
                prenorm("epre", T)
                qmT = ar.alloc(4 * TT, BF16).rearrange("p (h t) -> p h t", t=TT)
                omT = ar.alloc(4 * TT, BF16).rearrange("p (h t) -> p h t", t=TT)
                PmT = ar.alloc(2 * TT, BF16).rearrange("p (c t) -> p c t", t=TT)
                rdn = ar.alloc(TT, F32)
                fT = ar.alloc(NCH * TT, F32).rearrange("p (c t) -> p c t", t=TT)
                b_q, b_o, b_P, b_r, b_f = Buf("qm"), Buf("om"), Buf("Pm"), Buf("rdn"), [Buf(f"fT{m}") for m in range(NCH)]
                for g in range(2):
                    wqm, bwq = wq.get(wgroup("wmq", g), 128, NCH, 256)
                    for hh in range(2):
                        h = g * 2 + hh
                        ib = proj_fm(wqm, bwq, hh * 128, 128, T)
                        fw.op("act", lambda e, ib=ib, h=h: e.activation(out=qmT[:, h, 0:T], in_=PS[ib][:, 0:T], func=AF.Copy, scale=128 ** -0.5),
                              reads=[PB[ib]], writes=[b_q])
                for h in range(4):
                    for mc in range(2):
                        ib = nb()
                        mm(PS[ib][:, 0:T], KT[:, h, mc * 128:(mc + 1) * 128], qmT[:, h, 0:T], True, True, [b_K, b_q], PB[ib])
                        fw.op("act", lambda e, ib=ib, mc=mc: e.activation(out=PmT[:, mc, 0:T], in_=PS[ib][:, 0:T], func=AF.Exp),
                              reads=[PB[ib]], writes=[b_P])
                    io, idn = nb(), nb()
                    for mc in range(2):
                        mm(PS[io][:, 0:T], Vt[:, mc, h * 128:(h + 1) * 128], PmT[:, mc, 0:T], mc == 0, mc == 1, [b_V, b_P], PB[io])
                    for mc in range(2):
                        mm(PS[idn][:, 0:T], ones[:], PmT[:, mc, 0:T], mc == 0, mc == 1, [b_const, b_P], PB[idn])
                    if h % 2 == 0:
                        fw.op("dve", lambda e, idn=idn: e.reciprocal(out=rdn[:, 0:T], in_=PS[idn][:, 0:T]), reads=[PB[idn]], writes=[b_r])
                    else:
                        fw.op("act", lambda e, idn=idn: e.activation(out=rdn[:, 0:T], in_=PS[idn][:, 0:T], func=AF.Ln), reads=[PB[idn]], writes=[b_r])
                        fw.op("act", lambda e: e.activation(out=rdn[:, 0:T], in_=rdn[:, 0:T], func=AF.Exp, scale=-1.0), reads=[b_r], writes=[b_r])
                    fw.op("dve", lambda e, io=io, h=h: e.tensor_tensor(out=omT[:, h, 0:T], in0=PS[io][:, 0:T], in1=rdn[:, 0:T], op=ALU.mult),
                          reads=[PB[io], b_r], writes=[b_o])
                mem_out(T, omT, b_o, fT, b_f)
                fw.barrier()
                ar.reset(m0)

            def mem_out(T, omT, b_o, fT, b_f):
                wov = dr["wmo"].rearrange("(h p) n -> p h n", p=128)
                for g in range(2):
                    wo, bwo = wq.get((wov[:, :, g * 1024:(g + 1) * 1024], ("wmo", g)), 128, 4, 1024)
                    for mm_ in range(8):
                        m = g * 8 + mm_
                        ib = nb()
                        for h in range(4):
                            mm(PS[ib][:, 0:T], wo[:, h, mm_ * 128:(mm_ + 1) * 128], omT[:, h, 0:T], h == 0, h == 3, [bwo, b_o], PB[ib])
                        fw.op("act", lambda e, ib=ib, m=m: e.activation(out=fT[:, m, 0:T], in_=PS[ib][:, 0:T], func=AF.Copy),
                              reads=[PB[ib]], writes=[b_f[m]])
                        sumsq_acc(fT[:, m, 0:T], 128, T, [b_f[m]], m == 0, m == NCH - 1)
                postnorm_residual(fT, b_f, lambda c: gcol("epost", c), T)

            def mem_kv():
                m0 = ar.mark()
                T = 256
                fw.dma("sp", lambda e: e.dma_start(out=xT[:, :, 0:256], in_=dr["memT"].rearrange("(c p) n -> p c n", p=128)), writes=b_x)
                prenorm("ekv", T)
                stg = ar.alloc(2 * 512, F32).rearrange("p (c t) -> p c t", t=512)
                stg2 = ar.alloc(2 * 512, F32).rearrange("p (c t) -> p c t", t=512)
                b_stg, b_stg2 = Buf("stg"), Buf("stg2")
                bo = Buf("o_pm")
                for g in range(2):
                    wk, bwk = wq.get(wgroup("wmk", g), 128, NCH, 256)
                    for hh in range(2):
                        h = g * 2 + hh
                        ib = proj_fm(wk, bwk, hh * 128, 128, T)
                        fw.op("act", lambda e, ib=ib, h=h: e.activation(out=KmT[:, h, :], in_=PS[ib][:, 0:T], func=AF.Copy), reads=[PB[ib]], writes=[b_Km])
                        cp(4)
                    for mc in range(2):
                        ib = proj_tm(wk, bwk, mc, 256)
                        fw.op("act", lambda e, ib=ib, mc=mc, g=g: e.activation(out=stg[:, mc, g * 256:(g + 1) * 256], in_=PS[ib][:, 0:256], func=AF.Copy),
                              reads=[PB[ib]], writes=[b_stg])
                cp(5)
                fw.dma("sp", lambda e: e.dma_start(out=dr["pmk"].rearrange("(c p) n -> p c n", p=128), in_=stg), reads=[b_stg], writes=[bo])
                outbufs.append(bo)
                cp(6)
                for g in range(2):
                    wv, bwv = wq.get(wgroup("wmv", g), 128, NCH, 256)
                    for mc in range(2):
                        ib = proj_tm(wv, bwv, mc, 256)
                        fw.op("act", lambda e, ib=ib, mc=mc, g=g: e.activation(out=stg2[:, mc, g * 256:(g + 1) * 256], in_=PS[ib][:, 0:256], func=AF.Copy),
                              reads=[PB[ib]], writes=[b_stg2])
                        fw.op("dve", lambda e, mc=mc, g=g: e.tensor_copy(out=Vmt[:, mc, g * 256:(g + 1) * 256], in_=stg2[:, mc, g * 256:(g + 1) * 256]),
                              reads=[b_stg2], writes=[b_Vm])
                fw.dma("sp", lambda e: e.dma_start(out=dr["pmv"].rearrange("(c p) n -> p c n", p=128), in_=stg2), reads=[b_stg2], writes=[bo])
                outbufs.append(bo)
                fw.barrier()
                ar.reset(m0)

            def sample_tile():
                T = TS
                wq.in_sample = True
                import os
                skip = os.environ.get("KSKIP", "").split(",")
                fw.dma("sp", lambda e: e.dma_start(out=xT[:, :, 0:T], in_=dr["xs"].rearrange("(c p) n -> p c n", p=128)), writes=b_x)
                if "ffn1" not in skip:
                    ffn(T, "f1pre", lambda c: gd[:, c:c + 1], "w1g", "w1u", "w1d")
                if "mix" not in skip:
                    sample_mix(T)
                if "mem" not in skip:
                    sample_mem(T)
                if "ffn2" not in skip:
                    ffn(T, "f2pre", lambda c: gd[:, 16 + c:17 + c], "w2g", "w2u", "w2d", fuse_next=False)
                bo = Buf("o_ys")
                fw.dma("sp", lambda e: e.dma_start(out=dr["ys"].rearrange("(c p) n -> p c n", p=128), in_=xT[:, :, 0:T]), reads=b_x, writes=[bo])
                outbufs.append(bo)

            def sample_mix(T):
                prenorm("mpre", T)
                mA = ar.mark()
                mixA = ar.alloc(16 * T, BF16).rearrange("p (h t) -> p h t", t=T)
                mixR = ar.alloc(8 * T, BF16).rearrange("p (h t) -> p h t", t=T)
                b_mixA, b_mixR = Buf("mixA"), Buf("mixR")
                m1 = ar.mark()
                qT = ar.alloc(16 * T, BF16).rearrange("p (h t) -> p h t", t=T)
                b_qT = Buf("qT")
                ckTb = ar.alloc(64 * 128, BF16).rearrange("p (a k) -> p a k", k=128)
                cvb = ar.alloc(16 * 256, BF16).rearrange("p (n c) -> p n c", c=256)
                sbc = ar.alloc(128, F32)
                sbn = ar.alloc(2048, F32).rearrange("p (n c) -> p n c", c=128)
                b_cache, b_sb = Buf("cache"), Buf("sbias")
                stg32 = ar.alloc(2048, F32)
                b_stg = Buf("stg32")
                for pc in range(4):
                    fw.dma("sp", lambda e, pc=pc: e.dma_start(out=stg32[0:64, :].rearrange("p (a k) -> p a k", k=128),
                                                               in_=dr["ckT"][:, pc * 16:(pc + 1) * 16, :]), writes=[b_stg])
                    fw.op("act", lambda e, pc=pc: e.activation(out=ckTb[0:64, pc * 16:(pc + 1) * 16, :],
                                                               in_=stg32[0:64, :].rearrange("p (a k) -> p a k", k=128), func=AF.Copy),
                          reads=[b_stg], writes=[b_cache])
                for pc in range(2):
                    fw.dma("sp", lambda e, pc=pc: e.dma_start(out=stg32.rearrange("p (n c) -> p n c", c=256),
                                                               in_=dr["cwv"][pc * 8:(pc + 1) * 8].rearrange("n p c -> p n c")), writes=[b_stg])
                    fw.op("dve", lambda e, pc=pc: e.tensor_copy(out=cvb[:, pc * 8:(pc + 1) * 8, :], in_=stg32.rearrange("p (n c) -> p n c", c=256)),
                          reads=[b_stg], writes=[b_cache])
                fw.dma("sp", lambda e: e.dma_start(out=sbc, in_=dr["csc"]), writes=[b_sb])
                fw.dma("sp", lambda e: e.dma_start(out=sbn.rearrange("p n c -> p (n c)"), in_=dr["csn"]), writes=[b_sb])
                bo = Buf("o_swc")
                fw.dma("sp", lambda e: e.dma_start(out=dr["swk"][:, 0:120, :], in_=dr["cwk"][:, 8:128, :]), writes=[bo])
                fw.dma("sp", lambda e: e.dma_start(out=dr["swv"][:, 0:120, :], in_=dr["cwv"][:, 8:128, :]), writes=[bo])
                outbufs.append(bo)
                b_qs = Buf("qstage_s")
                for g in range(4):
                    wqa, bwqa = wq.get(wgroup("win", GQA + g), 128, NCH, 256)
                    for pp in range(2):
                        h0 = g * 4 + pp * 2
                        ib = proj_fm(wqa, bwqa, pp * 128, 128, T)
                        fw.op("act", lambda e, ib=ib, h0=h0: e.activation(out=qT[0:64, h0, 0:T], in_=PS[ib][0:64, 0:T], func=AF.Copy, scale=0.125),
                              reads=[PB[ib]], writes=[b_qT])
                        fw.op("act", lambda e, ib=ib, h0=h0: e.activation(out=qT[64:128, h0 + 1, 0:T], in_=PS[ib][64:128, 0:T], func=AF.Copy, scale=0.125),
                              reads=[PB[ib]], writes=[b_qs])
                qv = qT[:, :, 0:T].rearrange("p (a two) t -> p a two t", two=2)
                fw.dma("sp", lambda e: e.dma_start(out=qv[0:64, :, 1, :], in_=qv[64:128, :, 1, :]), reads=[b_qs], writes=[b_qT])
                wk_, bk_ = wq.get(wgroup("win", GK), 128, NCH, 256)
                wv_, bv_ = wq.get(wgroup("win", GV_), 128, NCH, 256)
                for kvh in range(4):
                    ib = proj_fm(wk_, bk_, kvh * 64, 64, T)
                    fw.op("dve", lambda e, ib=ib, kvh=kvh: e.tensor_copy(out=kT[0:64, kvh, 128:128 + T], in_=PS[ib][0:64, 0:T]),
                          reads=[PB[ib]], writes=[b_kT])
                kvl = ar.alloc(512, F32)
                b_kvl = Buf("kvl")
                ik = proj_tm(wk_, bk_, 0, 256)
                fw.op("dve", lambda e: e.tensor_copy(out=kvl[:, 0:256], in_=PS[ik][:, 0:256]), reads=[PB[ik]], writes=[b_kvl])
                iv = proj_tm(wv_, bv_, 0, 256)
                fw.op("act", lambda e: e.activation(out=kvl[:, 256:512], in_=PS[iv][:, 0:256], func=AF.Copy), reads=[PB[iv]], writes=[b_kvl])
                fw.op("act", lambda e: e.activation(out=vtok[:, 1, :], in_=PS[iv][:, 0:256], func=AF.Copy), reads=[PB[iv]], writes=[b_vtok[1]])
                bo = Buf("o_swn")
                for n in range(NSEQ):
                    fw.dma("sp", lambda e, n=n: e.dma_start(out=dr["swk"][n, 120:128, :], in_=kvl[n * 8:(n + 1) * 8, 0:256]), reads=[b_kvl], writes=[bo])
                    fw.dma("sp", lambda e, n=n: e.dma_start(out=dr["swv"][n, 120:128, :], in_=kvl[n * 8:(n + 1) * 8, 256:512]), reads=[b_kvl], writes=[bo])
                outbufs.append(bo)
                o32 = ar.alloc(16 * 128, F32).rearrange("p (h q) -> p h q", q=128)
                b_o32 = Buf("o32")
                scb = [ar.alloc(512, F32) for _ in range(2)]
                PTt = [ar.alloc(512, BF16) for _ in range(2)]
                b_scb = [Buf("scb0"), Buf("scb1")]
                b_PT = [Buf("PT0"), Buf("PT1")]
                rden = ar.alloc(512, F32)
                b_rden = Buf("rden")
                for nq in range(4):
                    isc, isn = nb(), nb()
                    for n4 in range(4):
                        n = nq * 4 + n4
                        for kvh in range(4):
                            cs = slice(n4 * 128 + kvh * 32, n4 * 128 + kvh * 32 + 32)
                            rq = qT[0:64, 4 * kvh:4 * kvh + 4, n * 8:(n + 1) * 8]
                            mm(PS[isc][:, cs], ckTb[0:64, n * 4 + kvh, :], rq, True, True, [b_cache, b_qT], PB[isc])
                            mm(PS[isn][:, cs], kT[0:64, kvh, 128:128 + T], rq, True, True, [b_kT, b_qT], PB[isn])
                    fw.op("dve", lambda e, isc=isc: e.tensor_tensor(out=scb[0].rearrange("p (n c) -> p n c", c=128),
                                                                     in0=PS[isc][:, :].rearrange("p (n c) -> p n c", c=128),
                                                                     in1=sbc.unsqueeze(1).to_broadcast([128, 4, 128]), op=ALU.add),
                          reads=[PB[isc], b_sb], writes=[b_scb[0]])
                    fw.op("dve", lambda e, isn=isn, nq=nq: e.tensor_tensor(out=scb[1].rearrange("p (n c) -> p n c", c=128),
                                                                            in0=PS[isn][:, :].rearrange("p (n c) -> p n c", c=128),
                                                                            in1=sbn[:, nq * 4:nq * 4 + 4, :], op=ALU.add),
                          reads=[PB[isn], b_sb], writes=[b_scb[1]])
                    for k2 in range(2):
                        fw.op("act", lambda e, k2=k2: e.activation(out=PTt[k2], in_=scb[k2], func=AF.Exp), reads=[b_scb[k2]], writes=[b_PT[k2]])
                    io, idn = nb(), nb()
                    for n4 in range(4):
                        n = nq * 4 + n4
                        for kvh in range(4):
                            cs = slice(n4 * 128 + kvh * 32, n4 * 128 + kvh * 32 + 32)
                            mm(PS[io][0:64, cs], cvb[:, n, kvh * 64:(kvh + 1) * 64], PTt[0][:, cs], True, False, [b_cache, b_PT[0]], PB[io])
                            mm(PS[io][0:64, cs], vtok[:, 1, kvh * 64:(kvh + 1) * 64], PTt[1][:, cs], False, True, [b_vtok[1], b_PT[1]], PB[io])
                    for k2 in range(2):
                        mm(PS[idn][0:64, :], ones[:, 0:64], PTt[k2], k2 == 0, k2 == 1, [b_const, b_PT[k2]], PB[idn])
                    fw.op("dve", lambda e, idn=idn: e.tensor_tensor(
                        out=rden[0:64, :].rearrange("p (n h t) -> p n h t", n=4, h=16), in0=PS[idn][0:64, :].rearrange("p (n h t) -> p n h t", n=4, h=16),
                        in1=gd[0:64, 48:64].unsqueeze(1).unsqueeze(3).to_broadcast([64, 4, 16, 8]), op=ALU.add),
                        reads=[PB[idn], b_gd], writes=[b_rden])
                    fw.op("dve", lambda e: e.reciprocal(out=rden[0:64, :], in_=rden[0:64, :]), reads=[b_rden], writes=[b_rden])
                    fw.op("dve", lambda e, io=io: e.tensor_tensor(out=rden[0:64, :], in0=PS[io][0:64, :], in1=rden[0:64, :], op=ALU.mult),
                          reads=[PB[io], b_rden], writes=[b_rden])
                    fw.op("dve", lambda e, nq=nq: e.tensor_copy(
                        out=o32[0:64, :, nq * 32:(nq + 1) * 32].rearrange("p h (n t) -> p n h t", t=8),
                        in_=rden[0:64, :].rearrange("p (n h t) -> p n h t", n=4, h=16)), reads=[b_rden], writes=[b_o32])
                sumsq_rstd([o32[0:64, h, :] for h in range(16)], 64, 1024, 128, [b_o32])
                fw.op("dve", lambda e: e.tensor_tensor(out=o32[0:64], in0=o32[0:64], in1=rstd[0:64, 0:128].unsqueeze(1).to_broadcast([64, 16, 128]),
                                                       op=ALU.mult), reads=[b_o32, b_rstd], writes=[b_o32])
                fw.op("dve", lambda e: e.tensor_tensor(out=mixA[0:64, :, 0:T], in0=o32[0:64],
                                                       in1=gv[0:64, GV["again"]:GV["again"] + 16].unsqueeze(2).to_broadcast([64, 16, 128]),
                                                       op=ALU.mult), reads=[b_o32, b_gv], writes=[b_mixA])
                fw.barrier()
                ar.reset(m1)
                tmp, b_tmp = hgrn_tmp()
                crs8 = ar.alloc(128, F32)
                hm8 = ar.alloc(128, F32)
                smk = ar.alloc(16, F32)
                b_c8 = Buf("c8")
                fw.dma("sp", lambda e: e.dma_start(out=crs8, in_=dr["crs8"]), writes=[b_c8])
                fw.dma("sp", lambda e: e.dma_start(out=hm8, in_=dr["chm8"]), writes=[b_c8])
                fw.dma("sp", lambda e: e.dma_start(out=smk, in_=dr["csm"]), writes=[b_c8])
                Am = ar.alloc(128, BF16)
                b_Am = Buf("Am")
                for hp in range(4):
                    m2 = ar.mark()
                    vh_tok = ar.alloc(256, BF16).rearrange("p (b k) -> p b k", k=256)
                    b_vh = Buf("vh")
                    heads = hgrn_pair(T, hp, True, vh_tok, b_vh, tmp, b_tmp, chunk=8, crs_ap=crs8)
                    for jj in range(2):
                        j = hp * 2 + jj
                        hd = heads[jj]
                        bt = hd["bt"]
                        S0f = ar.alloc(16 * 128, F32).rearrange("p (n v) -> p n v", v=128)
                        S0b = ar.alloc(16 * 128, BF16).rearrange("p (n v) -> p n v", v=128)
                        vbm = ar.alloc(16 * 128, BF16).rearrange("p (n v) -> p n v", v=128)
                        b_S0, b_vbm = Buf("S0"), Buf("vbm")
                        ssrc = dr["shs"][:, j].rearrange("n k v -> k n v")
                        fw.dma("sp", lambda e, S0f=S0f, ssrc=ssrc: e.dma_start(out=S0f, in_=ssrc), writes=[b_S0])
                        b_S0b = Buf("S0b")
                        fw.op("act", lambda e, S0b=S0b, S0f=S0f: e.activation(out=S0b, in_=S0f, func=AF.Copy), reads=[b_S0], writes=[b_S0b])
                        ia = nb()
                        mm(PS[ia][:, 0:128], hd["kt"][:, 0:T], hd["qt"][:, 0:T], True, True, [bt], PB[ia])
                        fw.op("dve", lambda e, ia=ia: e.tensor_tensor(out=Am, in0=PS[ia][:, 0:128], in1=hm8, op=ALU.mult),
                              reads=[PB[ia], b_c8], writes=[b_Am])
                        io = nb()
                        mm(PS[io][:, 0:128], vh_tok[:, 0, jj * 128:(jj + 1) * 128], Am, True, False, [b_vh, b_Am], PB[io])
                        for n in range(NSEQ):
                            mm(PS[io][:, n * 8:(n + 1) * 8], S0b[:, n, :], hd["qt"][:, n * 8:(n + 1) * 8], False, True, [b_S0b, bt], PB[io])
                        fw.op("act", lambda e, io=io, hd=hd: e.activation(out=hd["o32"][:, 0:T], in_=PS[io][:, 0:128], func=AF.Copy),
                              reads=[PB[io]], writes=[bt])
                        fw.op("dve", lambda e, vbm=vbm, jj=jj: e.tensor_tensor(
                            out=vbm, in0=vh_tok[:, 0, jj * 128:(jj + 1) * 128].unsqueeze(1).to_broadcast([128, 16, 128]),
                            in1=smk.unsqueeze(2).to_broadcast([128, 16, 128]), op=ALU.mult), reads=[b_vh, b_c8], writes=[b_vbm])
                        for nq in range(4):
                            iu = nb()
                            mm(PS[iu][:, :], hd["khtok"][:, 0, :], vbm[:, nq * 4:(nq + 1) * 4, :], True, True, [bt, b_vbm], PB[iu])
                            fw.op("dve", lambda e, S0f=S0f, nq=nq, hd=hd: e.tensor_tensor(
                                out=S0f[:, nq * 4:(nq + 1) * 4, :], in0=S0f[:, nq * 4:(nq + 1) * 4, :],
                                in1=hd["eLend"][:, nq * 4:(nq + 1) * 4].unsqueeze(2).to_broadcast([128, 4, 128]), op=ALU.mult),
                                reads=[b_S0, bt], writes=[b_S0])
                            fw.op("dve", lambda e, S0f=S0f, nq=nq, iu=iu: e.tensor_tensor(
                                out=S0f[:, nq * 4:(nq + 1) * 4, :], in0=S0f[:, nq * 4:(nq + 1) * 4, :],
                                in1=PS[iu][:, :].rearrange("p (n v) -> p n v", v=128), op=ALU.add), reads=[b_S0, PB[iu]], writes=[b_S0])
                        bo = Buf("o_shg")
                        fw.dma("sp", lambda e, S0f=S0f, j=j: e.dma_start(out=dr["shg"][:, j].rearrange("n k v -> k n v"), in_=S0f), reads=[b_S0], writes=[bo])
                        outbufs.append(bo)
                        sumsq_rstd([hd["o32"][:, 0:T]], 128, 128, T, [bt])
                        fw.op("dve", lambda e, hd=hd: e.scalar_tensor_tensor(out=hd["o32"][:, 0:T], in0=hd["o32"][:, 0:T], scalar=gcol("hgain"),
                                                                             in1=rstd[:, 0:T], op0=ALU.mult, op1=ALU.mult),
                              reads=[bt, b_rstd, b_gv], writes=[bt])
                        fw.op("dve", lambda e, hd=hd, j=j: e.tensor_tensor(out=mixR[:, j, 0:T], in0=hd["o32"][:, 0:T], in1=hd["gs"][:, 0:T], op=ALU.mult),
                              reads=[bt], writes=[b_mixR])
                    fw.barrier()
                    ar.reset(m2)
                ar.reset(m1)
                wout_stage(T, mixA, mixR, b_mixA, b_mixR)
                ar.reset(mA)

            def sample_mem(T):
                m0 = ar.mark()
                prenorm("epre", T)
                qmT = ar.alloc(4 * T, BF16).rearrange("p (h t) -> p h t", t=T)
                omT = ar.alloc(4 * T, BF16).rearrange("p (h t) -> p h t", t=T)
                PmT = ar.alloc(256, BF16)
                rdn = ar.alloc(128, F32)
                fT = ar.alloc(NCH * TT, F32).rearrange("p (c t) -> p c t", t=TT)
                b_q, b_o, b_P, b_r, b_f = Buf("qm"), Buf("om"), Buf("Pm"), Buf("rdn"), [Buf(f"fT{m}") for m in range(NCH)]
                for g in range(2):
                    wqm, bwq = wq.get(wgroup("wmq", g), 128, NCH, 256)
                    for hh in range(2):
                        h = g * 2 + hh
                        ib = proj_fm(wqm, bwq, hh * 128, 128, T)
                        fw.op("act", lambda e, ib=ib, h=h: e.activation(out=qmT[:, h, 0:T], in_=PS[ib][:, 0:T], func=AF.Copy, scale=128 ** -0.5),
                              reads=[PB[ib]], writes=[b_q])
                for nq in range(4):
                    kts, bks = wq.get(dr["cmkT"][:, nq * 16:(nq + 1) * 16, :], 128, 16, 256)
                    vts, bvs = wq.get(dr["cmv"][nq * 4:(nq + 1) * 4].rearrange("n (c p) k -> p (n c) k", p=128), 128, 8, 512)
                    isc = nb()
                    for n4 in range(4):
                        n = nq * 4 + n4
                        for h in range(4):
                            for mc in range(2):
                                c0 = mc * 128 + n4 * 32 + h * 8
                                mm(PS[isc][:, c0:c0 + 8], kts[:, n4 * 4 + h, mc * 128:(mc + 1) * 128], qmT[:, h, n * 8:(n + 1) * 8], True, True,
                                   [bks, b_q], PB[isc])
                    fw.op("act", lambda e, isc=isc: e.activation(out=PmT, in_=PS[isc][:, 0:256], func=AF.Exp), reads=[PB[isc]], writes=[b_P])
                    io, idn = nb(), nb()
                    for n4 in range(4):
                        for h in range(4):
                            c1 = n4 * 32 + h * 8
                            for mc in range(2):
                                c0 = mc * 128 + c1
                                mm(PS[io][:, c1:c1 + 8], vts[:, n4 * 2 + mc, h * 128:(h + 1) * 128], PmT[:, c0:c0 + 8], mc == 0, mc == 1, [bvs, b_P], PB[io])
                    for mc in range(2):
                        mm(PS[idn][:, 0:128], ones[:], PmT[:, mc * 128:(mc + 1) * 128], mc == 0, mc == 1, [b_const, b_P], PB[idn])
                    fw.op("dve", lambda e, idn=idn: e.reciprocal(out=rdn, in_=PS[idn][:, 0:128]), reads=[PB[idn]], writes=[b_r])
                    fw.op("dve", lambda e, io=io, nq=nq: e.tensor_tensor(
                        out=omT[:, :, nq * 32:(nq + 1) * 32].rearrange("p h (n t) -> p n h t", t=8),
                        in0=PS[io][:, 0:128].rearrange("p (n h t) -> p n h t", n=4, h=4),
                        in1=rdn.rearrange("p (n h t) -> p n h t", n=4, h=4), op=ALU.mult), reads=[PB[io], b_r], writes=[b_o])
                mem_out(T, omT, b_o, fT, b_f)
                fw.barrier()
                ar.reset(m0)

            fw.op("dve", lambda e: e.memset(kT[:], 0.0), writes=[b_kT])
            fw.op("dve", lambda e: e.memset(vtok[:, 0, :], 0.0), writes=[b_vtok[0]])
            for j in range(8):
                fw.op("dve", lambda e, j=j: e.memset(S32[:, j, :], 0.0), writes=[b_S32[j]])
                fw.op("dve", lambda e, j=j: e.memset(Sbf[:, j, :], 0.0), writes=[b_Sbf[j]])
            import os
            stg = os.environ.get("KSTAGE", "all")
            try:
                if stg == "all":
                    mem_kv()
                    for t in range(NPT):
                        pass0_tile(t, t == NPT - 1)
                    for t in range(NPT):
                        main_tile(t)
                    sample_tile()
                elif stg == "sample":
                    sample_tile()
                elif stg == "memkv":
                    mem_kv()
                elif stg == "p0":
                    mem_kv()
                    pass0_tile(0, True)
                elif stg == "main":
                    mem_kv()
                    main_tile(0)
            except StopEmit:
                fw.barrier()
            fw.wait_all("sp", outbufs)

        fw0 = FW(nc, st, dry=True)
        wq0 = WQ(fw0, [w[:] for w in wsl], None)
        emit(fw0, wq0)
        plan = wq0.rec
        fw = FW(nc, st)
        fw.mark_tile = sb("mark", [128, 8], F32)[:]
        wq = WQ(fw, [w[:] for w in wsl], plan, nc)
        emit(fw, wq)
        print("instructions:", fw.n_inst, "weight groups:", len(plan))
        fw.replay()
    _NC_CACHE["used"] = set(dr.keys())
    return nc


def _consts():
    slopes = 2.0 ** (-8.0 * np.arange(1, 17, dtype=np.float64) / 16)
    j = np.arange(128)[:, None, None, None]
    kc = np.arange(2)[None, :, None, None]
    i = np.arange(128)[None, None, None, :]
    dist = (128 + i) - (kc * 128 + j)
    valid = (dist >= 0) & (dist < 128)
    cab = np.where(valid, -slopes[None, None, :, None] * dist, NEG).astype(np.float32).reshape(128, -1)
    s = np.arange(128)[:, None]
    t = np.arange(128)[None, :]
    chm = ((s // 64 == t // 64) & (s <= t)).astype(np.float32)
    crs = np.ones((128, 512), np.float32)
    crs[:, ::64] = 0.0
    cid = np.eye(128, dtype=np.float32)
    jj = np.arange(128)[:, None, None]
    hh = np.arange(16)[None, :, None]
    tt = np.arange(8)[None, None, :]
    dist = 128 + tt - jj
    csc = np.where(dist < 128, -slopes[hh] * dist, NEG).astype(np.float32).reshape(128, 128)
    kn = (np.arange(128) // 8)[:, None, None, None]
    ks = (np.arange(128) % 8)[:, None, None, None]
    qn = np.arange(16)[None, :, None, None]
    h4 = np.arange(16)[None, None, :, None]
    t4 = np.arange(8)[None, None, None, :]
    csn = np.where((kn == qn) & (ks <= t4), -slopes[h4] * (t4 - ks), NEG).astype(np.float32).reshape(128, 2048)
    chm8 = ((s // 8 == t // 8) & (s <= t)).astype(np.float32)
    crs8 = np.ones((128, 128), np.float32)
    crs8[:, ::8] = 0.0
    csm = (np.arange(128)[:, None] // 8 == np.arange(16)[None, :]).astype(np.float32)
    return cab, chm, crs, cid, csc, csn, chm8, crs8, csm


def _pack16(v):
    return np.ascontiguousarray(v.reshape(16, 128).T)


_NC_CACHE = {}
_OUT_SHAPES = dict(ym=(D, NPT * TT), ys=(D, TS), pwk=(128, 256), pwv=(128, 256), phg=(8, 128, 128), pmk=(256, 512), pmv=(256, 512),
                   swk=(NSEQ, 128, 256), swv=(NSEQ, 128, 256), shg=(NSEQ, 8, 128, 128))


def kernel(x_prompt, x_sample, mem_prompt, cache_win_k, cache_win_v, state_hgrn, cache_mem_k, cache_mem_v,
           ffn1_norm_pre, ffn1_norm_post, ffn1_w_gate, ffn1_w_up, ffn1_w_down,
           mix_norm_pre, mix_norm_post, w_in, attn_sinks, hgrn_lb_logits, attn_out_gain, hgrn_out_gain, w_out,
           mem_norm_pre, mem_norm_post, mem_norm_kv, w_mem_q, w_mem_k, w_mem_v, w_mem_o,
           ffn2_norm_pre, ffn2_norm_post, ffn2_w_gate, ffn2_w_up, ffn2_w_down):
    f = lambda a: np.ascontiguousarray(np.asarray(a, dtype=np.float32))
    x_prompt, x_sample, mem_prompt = f(x_prompt), f(x_sample), f(mem_prompt)
    cab, chm, crs, cid, csc, csn, chm8, crs8, csm = _consts()
    gvb = np.zeros((128, NGV), np.float32)
    for name, arr in (("f1pre", ffn1_norm_pre), ("f1post", ffn1_norm_post), ("mpre", mix_norm_pre), ("mpost", mix_norm_post),
                      ("epre", mem_norm_pre), ("epost", mem_norm_post), ("ekv", mem_norm_kv), ("f2pre", ffn2_norm_pre),
                      ("f2post", ffn2_norm_post)):
        gvb[:, GV[name]:GV[name] + 16] = _pack16(f(arr)[0])
    gvb[0:64, GV["again"]:GV["again"] + 16] = f(attn_out_gain)[0].reshape(16, 64).T
    gvb[:, GV["hgain"]] = f(hgrn_out_gain)[0]
    lbl = f(hgrn_lb_logits)
    gvb[:, GV["lb0"]:GV["lb0"] + 8] = lbl[0].reshape(8, 128).T
    gvb[:, GV["lb1"]:GV["lb1"] + 8] = lbl[1].reshape(8, 128).T
    gvb[:, GV["sinks"]:GV["sinks"] + 16] = f(attn_sinks)[0][None, :]
    shared = dict(w1g=f(ffn1_w_gate)[0], w1u=f(ffn1_w_up)[0], w1d=f(ffn1_w_down)[0], w2g=f(ffn2_w_gate)[0], w2u=f(ffn2_w_up)[0],
                  w2d=f(ffn2_w_down)[0], win=f(w_in)[0], wout=f(w_out)[0], wmq=f(w_mem_q)[0], wmk=f(w_mem_k)[0], wmv=f(w_mem_v)[0],
                  wmo=f(w_mem_o)[0], cab=cab, chm=chm, crs=crs, cid=cid,
                  csc=csc, csn=csn, chm8=chm8, crs8=crs8, csm=csm)
    in_maps = []
    for c in range(8):
        b, half = c // 2, c % 2
        g = gvb.copy()
        g[:, GV["flag"]] = float(half)
        xm = np.ascontiguousarray(x_prompt[b, half * 2048:(half + 1) * 2048, :].T)
        xp = np.ascontiguousarray(x_prompt[b, 0:2048, :].T) if half == 1 else np.zeros((D, 2048), np.float32)
        xs = np.ascontiguousarray(x_sample[c * NSEQ:(c + 1) * NSEQ].reshape(TS, D).T)
        m = dict(shared)
        m.update(xm=xm, xp=xp, xs=xs, memT=np.ascontiguousarray(mem_prompt[b].T), gv=g)
        sq = slice(c * NSEQ, (c + 1) * NSEQ)
        cwk_ = f(cache_win_k)[0, sq]
        m["ckT"] = np.ascontiguousarray(cwk_.transpose(3, 0, 2, 1).reshape(64, NSEQ * 4, 128))
        m["cwk"] = np.ascontiguousarray(cwk_.reshape(NSEQ, 128, 256))
        m["cwv"] = np.ascontiguousarray(f(cache_win_v)[0, sq].reshape(NSEQ, 128, 256))
        m["shs"] = np.ascontiguousarray(f(state_hgrn)[0, sq])
        m["cmkT"] = np.ascontiguousarray(f(cache_mem_k)[0, sq].transpose(3, 0, 2, 1).reshape(128, NSEQ * 4, 256))
        m["cmv"] = np.ascontiguousarray(f(cache_mem_v)[0, sq].reshape(NSEQ, 256, 512))
        in_maps.append(m)
    if "nc" not in _NC_CACHE:
        _NC_CACHE["nc"] = build_program()
    nc = _NC_CACHE["nc"]
    import os
    ncores = int(os.environ.get("KCORES", "8"))
    in_maps = [{k: v for k, v in m.items() if k in _NC_CACHE["used"]} for m in in_maps[:ncores]]
    res = run_bass_kernel_spmd(nc, in_maps, core_ids=list(range(ncores)))
    R = list(res.results)
    if ncores < 8 or os.environ.get("KSTAGE", "all") != "all":
        class ZD(dict):
            def __missing__(self, k):
                return np.zeros(_OUT_SHAPES[k], np.float32)
        R = [ZD(r) for r in R] + [ZD() for _ in range(8 - ncores)]
    y_p = np.zeros((4, 4096, D), np.float32)
    y_s = np.zeros((128, 8, D), np.float32)
    p_wk = np.zeros((1, 4, 128, 4, 64), np.float32); p_wv = np.zeros_like(p_wk)
    p_S = np.zeros((1, 4, 8, 128, 128), np.float32)
    p_mk = np.zeros((1, 4, 256, 4, 128), np.float32); p_mv = np.zeros_like(p_mk)
    s_wk = np.zeros((1, 128, 128, 4, 64), np.float32); s_wv = np.zeros_like(s_wk)
    s_S = np.zeros((1, 128, 8, 128, 128), np.float32)
    for c in range(8):
        b, half = c // 2, c % 2
        r = R[c]
        y_p[b, half * 2048:(half + 1) * 2048, :] = r["ym"].T
        y_s[c * NSEQ:(c + 1) * NSEQ] = r["ys"].T.reshape(NSEQ, 8, D)
        if half == 1:
            p_wk[0, b] = r["pwk"].reshape(128, 4, 64)
            p_wv[0, b] = r["pwv"].reshape(128, 4, 64)
            p_S[0, b] = r["phg"]
        else:
            p_mk[0, b] = r["pmk"].reshape(256, 4, 128)
            p_mv[0, b] = r["pmv"].reshape(256, 4, 128)
        s_wk[0, c * NSEQ:(c + 1) * NSEQ] = r["swk"].reshape(NSEQ, 128, 4, 64)
        s_wv[0, c * NSEQ:(c + 1) * NSEQ] = r["swv"].reshape(NSEQ, 128, 4, 64)
        s_S[0, c * NSEQ:(c + 1) * NSEQ] = r["shg"]
    return (y_p, y_s, p_wk, p_wv, p_S, p_mk, p_mv, s_wk, s_wv, s_S)
```

```python
import numpy as np
from contextlib import ExitStack
import concourse.bass as bass
import concourse.mybir as mybir
from concourse.bass_utils import run_bass_kernel_spmd

F32 = mybir.dt.float32
BF16 = mybir.dt.bfloat16
AF = mybir.ActivationFunctionType
ALU = mybir.AluOpType

D = 2048
DFF = 5632
NCH = 16
NFF = 44
EPS = 1e-6
NEG = -30000.0
TT = 512
NPT = 4
NSEQ = 16
TS = 128


class Sem:
    def __init__(self, handle, name):
        self.h = handle
        self.name = name
        self.val = 0


class Buf:
    __slots__ = ("name", "w", "r")

    def __init__(self, name=""):
        self.name = name
        self.w = None
        self.r = {}


class Eng:
    def __init__(self, name, sem, selfsync):
        self.name = name
        self.sem = sem
        self.selfsync = selfsync
        self.known = {}
        self.prog = []
        self.dma_pool = []
        self.dma_i = 0


class FW:
    def __init__(self, nc, stack, dry=False):
        self.nc = nc
        self.dry = dry
        self.engs = {}
        self.n_inst = 0
        self.dma_events = {}
        self.bar_buf = Buf("bar")
        self.mark_tile = None
        if dry:
            return
        for name, ss in (("pe", False), ("act", True), ("dve", True), ("pool", True), ("sp", False)):
            s = Sem(stack.enter_context(nc.semaphore("s_" + name)), name)
            self.engs[name] = Eng(name, s, ss)
        for q, n in (("sp", 24), ("pool", 16)):
            for i in range(n):
                s = Sem(stack.enter_context(nc.semaphore(f"d_{q}{i}")), f"d_{q}{i}")
                self.engs[q].dma_pool.append(s)

    def _collect(self, E, reads, writes):
        waits = {}

        def need(ev, raw):
            if ev is None:
                return
            sem, v = ev
            if sem is E.sem and not E.selfsync:
                return
            if E.known.get(sem, 0) >= v:
                return
            if waits.get(sem, 0) < v:
                waits[sem] = v

        for b in reads:
            need(b.w, True)
        for b in writes:
            need(b.w, False)
            for s, v in b.r.items():
                need((s, v), False)
        return waits

    def _commit(self, E, waits, fi, ev, reads, writes):
        for s, v in waits.items():
            E.known[s] = v
        E.prog.append((list(waits.items()), fi, ev))
        sem, v = ev
        for b in reads:
            if b.r.get(sem, 0) < v:
                b.r[sem] = v
        for b in writes:
            b.w = ev
            b.r = {}
        self.n_inst += 1

    def op(self, eng, fn, reads=(), writes=()):
        if self.dry:
            return
        E = self.engs[eng]
        waits = self._collect(E, reads, writes)
        E.sem.val += 1
        ev = (E.sem, E.sem.val)
        self._commit(E, waits, (fn, 1), ev, reads, writes)

    def dma(self, q, fn, reads=(), writes=(), track=True):
        if self.dry:
            return
        E = self.engs[q]
        s = E.dma_pool[E.dma_i % len(E.dma_pool)]
        E.dma_i += 1
        waits = self._collect(E, reads, writes)
        if s.val > 0 and E.known.get(s, 0) < s.val:
            waits[s] = s.val
        s.val += 16
        ev = (s, s.val)
        self._commit(E, waits, (fn, 16), ev, reads, writes)
        if track:
            self.dma_events[s] = s.val

    def barrier(self, engs=("pe", "act", "dve", "sp")):
        if self.dry:
            return
        for e in engs:
            E = self.engs[e]
            waits = {}
            for x in ("pe", "act", "dve"):
                X = self.engs[x]
                if X.sem.val > 0 and E.known.get(X.sem, 0) < X.sem.val:
                    waits[X.sem] = X.sem.val
            for s, v in self.dma_events.items():
                if E.known.get(s, 0) < v:
                    waits[s] = v
            for s, v in waits.items():
                E.known[s] = v
            if waits:
                E.prog.append((list(waits.items()), None, None))
        self.dma_events = {}

    def wait_all(self, eng, bufs):
        if self.dry:
            return
        E = self.engs[eng]
        waits = {}
        for b in bufs:
            for ev in [b.w] + list(b.r.items()):
                if ev is None:
                    continue
                s, v = ev
                if E.known.get(s, 0) < v and waits.get(s, 0) < v:
                    waits[s] = v
        for s, v in waits.items():
            E.known[s] = v
        E.prog.append((list(waits.items()), None, None))

    def replay(self):
        nc = self.nc
        engs = self.engs

        def run(E, h):
            for waits, fi, ev in E.prog:
                for s, v in waits:
                    h.wait_ge(s.h, v)
                if fi is None:
                    continue
                fn, inc = fi
                fn(h).then_inc(ev[0].h, inc)

        with nc.Block() as block:
            @block.tensor
            def _(h):
                run(engs["pe"], h)

            @block.scalar
            def _(h):
                run(engs["act"], h)

            @block.vector
            def _(h):
                run(engs["dve"], h)

            @block.gpsimd
            def _(h):
                run(engs["pool"], h)

            @block.sync
            def _(h):
                run(engs["sp"], h)


class StopEmit(Exception):
    pass


def cp(n):
    import os
    if int(os.environ.get('KCUT', '999')) <= n:
        raise StopEmit()


class Arena:
    def __init__(self, t2d, nwords):
        self.t = t2d
        self.n = nwords
        self.top = 0

    def alloc(self, nelem, dt):
        words = nelem if dt == F32 else (nelem + 1) // 2
        a = self.top
        self.top += words
        assert self.top <= self.n, f"arena overflow {self.top} > {self.n}"
        ap = self.t[:, a:a + words]
        return ap if dt == F32 else ap.bitcast(BF16)

    def mark(self):
        return self.top

    def reset(self, m):
        self.top = m


class WQ:
    SLOT = 4096

    def __init__(self, fw, slots, plan, nc=None):
        self.fw = fw
        self.nc = nc
        self.slots = slots
        self.bufs = [Buf(f"wslot{i}") for i in range(len(slots))]
        self.plan = plan
        self.rec = []
        self.i = 0
        self.issued = 0
        self.LA = len(slots) - 2
        self.in_sample = False
        self.store_after = {}
        self.load_from = {}
        if plan is not None:
            occ = {}
            for i, p in enumerate(plan):
                if p[4] is not None:
                    occ.setdefault(p[4], []).append(i)
            for key, idx in occ.items():
                if len(idx) < 2:
                    continue
                _, p_, a_, b_, _, _ = plan[idx[0]]
                name = "scr_" + "_".join(str(x) for x in key)
                scr = nc.dram_tensor(name, [p_, a_ * b_], BF16).ap()
                sb = Buf(name)
                self.store_after[idx[0]] = (scr, sb)
                for j in idx[1:]:
                    self.load_from[j] = (scr, sb)

    def view(self, k, npart, a, b):
        return self.slots[k % len(self.slots)][0:npart, 0:a * b].rearrange("p (a b) -> p a b", b=b)

    def get(self, src, npart, a, b):
        key = None
        if isinstance(src, tuple):
            src, key = src
        assert a * b <= self.SLOT
        if self.plan is None:
            self.rec.append((src, npart, a, b, key, self.in_sample))
            return self.view(0, npart, a, b), self.bufs[0]
        i = self.i
        self.i += 1
        lim = min(i + self.LA, len(self.plan) - 1)
        while self.issued <= lim:
            k = self.issued
            s, p_, a_, b_, _, _ = self.plan[k]
            dst = self.view(k, p_, a_, b_)
            if k in self.load_from:
                scr, sb = self.load_from[k]
                dst2 = self.slots[k % len(self.slots)][0:p_, 0:a_ * b_]
                self.fw.dma("pool", lambda e, dst2=dst2, scr=scr: e.dma_start(out=dst2, in_=scr),
                            reads=[sb], writes=[self.bufs[k % len(self.slots)]], track=False)
            else:
                self.fw.dma("pool", lambda e, dst=dst, s=s: e.dma_start(out=dst, in_=s),
                            writes=[self.bufs[k % len(self.slots)]], track=False)
            self.issued += 1
        if i in self.store_after:
            scr, sb = self.store_after[i]
            p_, a_, b_ = self.plan[i][1:4]
            src2 = self.slots[i % len(self.slots)][0:p_, 0:a_ * b_]
            self.fw.dma("sp", lambda e, src2=src2, scr=scr: e.dma_start(out=scr, in_=src2),
                        reads=[self.bufs[i % len(self.slots)]], writes=[sb], track=False)
        return self.view(i, npart, a, b), self.bufs[i % len(self.slots)]


GV = dict(f1pre=0, f1post=16, mpre=32, mpost=48, epre=64, epost=80, ekv=96, f2pre=112, f2post=128,
          again=144, hgain=160, lb0=161, lb1=169, sinks=177, flag=193)
NGV = 194
GQA, GK, GV_, GQH, GF, GI, GG = 0, 4, 5, 6, 10, 14, 18


def build_program(with_sample=True):
    nc = bass.Bass("TRN2", target_bir_lowering=False)
    specs = {}

    def din(name, shape):
        specs[name] = (list(shape), "ExternalInput")

    def dout(name, shape):
        specs[name] = (list(shape), "ExternalOutput")

    class LazyDR(dict):
        def __missing__(self, name):
            shape, kind = specs[name]
            ap = nc.dram_tensor(name, shape, F32, kind=kind).ap()
            self[name] = ap
            return ap

    dr = LazyDR()
    din("xm", [D, NPT * TT]); din("xp", [D, NPT * TT]); din("xs", [D, TS]); din("memT", [D, 256])
    for n in ("w1g", "w1u", "w2g", "w2u", "win"):
        din(n, [D, DFF])
    for n in ("w1d", "w2d"):
        din(n, [DFF, D])
    din("wout", [D, D]); din("wmq", [D, 512]); din("wmk", [D, 512]); din("wmv", [D, 512]); din("wmo", [512, D])
    din("gv", [128, NGV]); din("cab", [128, 2 * 16 * 128]); din("chm", [128, 128]); din("crs", [128, 512])
    din("cid", [128, 128])
    din("ckT", [64, NSEQ * 4, 128]); din("cwk", [NSEQ, 128, 256]); din("cwv", [NSEQ, 128, 256])
    din("shs", [NSEQ, 8, 128, 128]); din("cmkT", [128, NSEQ * 4, 256]); din("cmv", [NSEQ, 256, 512])
    din("csc", [128, 128]); din("csn", [128, 2048]); din("chm8", [128, 128]); din("crs8", [128, 128]); din("csm", [128, 16])
    dout("ym", [D, NPT * TT]); dout("ys", [D, TS])
    dout("pwk", [128, 256]); dout("pwv", [128, 256]); dout("phg", [8, 128, 128])
    dout("pmk", [256, 512]); dout("pmv", [256, 512])
    dout("swk", [NSEQ, 128, 256]); dout("swv", [NSEQ, 128, 256]); dout("shg", [NSEQ, 8, 128, 128])
    import os
    if os.environ.get("KSTAGE", "all") == "all":
        for n in specs:
            dr[n]

    with ExitStack() as st:
        def sb(name, shape, dt):
            return st.enter_context(nc.sbuf_tensor("sb_" + name, list(shape), dt))

        xT = sb("xT", [128, NCH, TT], F32)
        hT = sb("hT", [128, NCH, TT], BF16)
        wsl = [sb(f"wsl{i}", [128, WQ.SLOT], BF16) for i in range(6)]
        gv = sb("gv", [128, NGV], F32)
        gd = sb("gd", [128, 64], F32)
        negflag = sb("negflag", [128, 1], F32)
        hmask = sb("hmask", [128, 128], F32)
        crs = sb("crs", [128, 512], F32)
        ident = sb("ident", [128, 128], BF16)
        ones = sb("ones", [128, 128], BF16)
        S32 = sb("S32", [128, 8, 128], F32)
        Sbf = sb("Sbf", [128, 8, 128], BF16)
        kT = sb("kT", [128, 4, TT + 128], BF16)
        vtok = sb("vtok", [128, 5, 256], BF16)
        KmT = sb("KmT", [128, 4, 256], BF16)
        Vmt = sb("Vmt", [128, 2, 512], BF16)
        rstd = sb("rstd", [128, TT], F32)
        lnv = sb("lnv", [128, TT], F32)
        sqt = [sb(f"sqt{i}", [128, TT], BF16) for i in range(2)]
        sgt = [sb(f"sgt{i}", [128, TT], F32) for i in range(2)]
        AW = 20224
        arena_t = sb("arena", [128, AW], F32)
        PS = [st.enter_context(nc.psum_tensor(f"ps{i}", [128, 512], F32)) for i in range(7)]
        PT7 = st.enter_context(nc.psum_tensor("ps7", [128, 1024], BF16))

        def emit(fw, wq):
            ar = Arena(arena_t, AW)
            PB = [Buf(f"pb{i}") for i in range(8)]
            bank_i = [0]

            fresh = [False] * 8

            def nb():
                i = bank_i[0] % 6
                bank_i[0] += 1
                fresh[i] = True
                return i

            b_x = [Buf(f"xT{c}") for c in range(NCH)]; b_h = [Buf(f"hT{c}") for c in range(NCH)]; b_rstd = Buf("rstd"); b_lnv = Buf("lnv")
            pend = {"ss": False, "mm": None}
            b_sq = [Buf("sq0"), Buf("sq1")]; b_sg = [Buf("sg0"), Buf("sg1")]
            b_gv = Buf("gv"); b_gd = Buf("gd"); b_const = Buf("const")
            b_S32 = [Buf(f"S32_{j}") for j in range(8)]; b_Sbf = [Buf(f"Sbf_{j}") for j in range(8)]
            b_kT = Buf("kT"); b_vtok = [Buf(f"vtok{i}") for i in range(5)]
            b_Km = Buf("Km"); b_Vm = Buf("Vm")
            cnt = {"sq": 0, "sg": 0}
            outbufs = []

            def gcol(name, c=0):
                o = GV[name] + c
                return gv[:, o:o + 1]

            fw.dma("sp", lambda e: e.dma_start(out=gv[:], in_=dr["gv"]), writes=[b_gv])
            fw.dma("sp", lambda e: e.dma_start(out=hmask[:], in_=dr["chm"]), writes=[b_const])
            fw.dma("sp", lambda e: e.dma_start(out=crs[:], in_=dr["crs"]), writes=[b_const])
            fw.dma("pool", lambda e: e.dma_start(out=ident[:], in_=dr["cid"]), writes=[b_const])
            fw.op("dve", lambda e: e.memset(ones[:], 1.0), writes=[b_const])
            fw.op("dve", lambda e: e.tensor_scalar(out=gd[:, 0:16], in0=gv[:, GV["f1post"]:GV["f1post"] + 16],
                                                   scalar1=0.5, scalar2=None, op0=ALU.mult), reads=[b_gv], writes=[b_gd])
            fw.op("dve", lambda e: e.tensor_scalar(out=gd[:, 16:32], in0=gv[:, GV["f2post"]:GV["f2post"] + 16],
                                                   scalar1=0.5, scalar2=None, op0=ALU.mult), reads=[b_gv], writes=[b_gd])
            fw.op("dve", lambda e: e.tensor_tensor(out=gd[:, 32:40], in0=gv[:, GV["lb1"]:GV["lb1"] + 8],
                                                   in1=gv[:, GV["lb0"]:GV["lb0"] + 8], op=ALU.subtract),
                  reads=[b_gv], writes=[b_gd])
            fw.op("act", lambda e: e.activation(out=gd[:, 32:40], in_=gd[:, 32:40], func=AF.Exp), reads=[b_gd], writes=[b_gd])
            fw.op("dve", lambda e: e.tensor_scalar(out=gd[:, 32:40], in0=gd[:, 32:40], scalar1=1.0, scalar2=None, op0=ALU.add),
                  reads=[b_gd], writes=[b_gd])
            fw.op("dve", lambda e: e.reciprocal(out=gd[:, 32:40], in_=gd[:, 32:40]), reads=[b_gd], writes=[b_gd])
            fw.op("dve", lambda e: e.tensor_scalar(out=gd[:, 40:48], in0=gd[:, 32:40], scalar1=-1.0, scalar2=1.0,
                                                   op0=ALU.mult, op1=ALU.add), reads=[b_gd], writes=[b_gd])
            fw.op("act", lambda e: e.activation(out=gd[:, 48:64], in_=gv[:, GV["sinks"]:GV["sinks"] + 16], func=AF.Exp),
                  reads=[b_gv], writes=[b_gd])
            fw.op("dve", lambda e: e.tensor_scalar(out=negflag[:], in0=gv[:, GV["flag"]:GV["flag"] + 1], scalar1=-1.0,
                                                   scalar2=-NEG, op0=ALU.add, op1=ALU.mult), reads=[b_gv], writes=[b_gd])

            def wgroup(w, g, width=256):
                return (dr[w].rearrange("(c p) n -> p c n", p=128)[:, :, g * width:(g + 1) * width], (w, g))

            def mm(out, lhsT, rhs, start, stop, reads, pb, **kw):
                idx = PB.index(pb)
                st_ = fresh[idx]
                if st_:
                    assert start
                    fresh[idx] = False
                kw.setdefault("skip_group_check", True)
                fw.op("pe", lambda e: e.matmul(out, lhsT=lhsT, rhs=rhs, start=st_, stop=stop, **kw),
                      reads=reads, writes=[pb])

            def proj_fm(wt, bw, col0, M, T, c0=0):
                ib = nb()
                for c in range(NCH):
                    mm(PS[ib][0:M, 0:T], wt[:, c, col0:col0 + M], hT[:, c, c0:c0 + T], c == 0, c == NCH - 1, [bw, b_h[c]], PB[ib])
                return ib

            def proj_tm(wt, bw, blk, N):
                ib = nb()
                for c in range(NCH):
                    mm(PS[ib][:, 0:N], hT[:, c, blk * 128:(blk + 1) * 128], wt[:, c, 0:N], c == 0, c == NCH - 1, [bw, b_h[c]], PB[ib])
                return ib

            def sumsq_acc(cap, nparts, T, bufs_in, first, last):
                if first:
                    fresh[6] = True
                    pend["mm"] = None
                k = cnt["sq"] % 2
                cnt["sq"] += 1
                fw.op("act", lambda e: e.activation(out=sqt[k][0:nparts, 0:T], in_=cap, func=AF.Square),
                      reads=bufs_in, writes=[b_sq[k]])

                def emit_mm(kk, f_, l_):
                    mm(PS[6][0:nparts, 0:T], ones[0:nparts, 0:nparts], sqt[kk][0:nparts, 0:T], f_, l_, [b_sq[kk], b_const], PB[6])

                if pend["mm"] is not None:
                    emit_mm(*pend["mm"])
                pend["mm"] = (k, first, last)
                if last:
                    emit_mm(*pend["mm"])
                    pend["mm"] = None

            def rstd_finish(nparts, Dn, T):
                fw.op("act", lambda e: e.activation(out=lnv[0:nparts, 0:T], in_=PS[6][0:nparts, 0:T], func=AF.Ln,
                                                    scale=1.0 / Dn, bias=EPS), reads=[PB[6]], writes=[b_lnv])
                fw.op("act", lambda e: e.activation(out=rstd[0:nparts, 0:T], in_=lnv[0:nparts, 0:T], func=AF.Exp, scale=-0.5),
                      reads=[b_lnv], writes=[b_rstd])

            def sumsq_rstd(chunks, nparts, Dn, T, bufs_in):
                n = len(chunks)
                for i, cap in enumerate(chunks):
                    sumsq_acc(cap, nparts, T, bufs_in, i == 0, i == n - 1)
                rstd_finish(nparts, Dn, T)

            def prenorm(gname, T):
                if pend["ss"]:
                    pend["ss"] = False
                else:
                    for c in range(NCH):
                        sumsq_acc(xT[:, c, 0:T], 128, T, [b_x[c]], c == 0, c == NCH - 1)
                rstd_finish(128, D, T)
                for c in range(NCH):
                    fw.op("dve", lambda e, c=c: e.scalar_tensor_tensor(out=hT[:, c, 0:T], in0=xT[:, c, 0:T], scalar=gcol(gname, c),
                                                                       in1=rstd[:, 0:T], op0=ALU.mult, op1=ALU.mult),
                          reads=[b_x[c], b_rstd, b_gv], writes=[b_h[c]])

            def postnorm_residual(fT, b_f, gap_fn, T, fuse_next=True):
                rstd_finish(128, D, T)
                for c in range(NCH):
                    fw.op("dve", lambda e, c=c: e.scalar_tensor_tensor(out=fT[:, c, 0:T], in0=fT[:, c, 0:T], scalar=gap_fn(c),
                                                                       in1=rstd[:, 0:T], op0=ALU.mult, op1=ALU.mult),
                          reads=[b_f[c], b_rstd, b_gv, b_gd], writes=[b_f[c]])
                    fw.op("dve", lambda e, c=c: e.tensor_tensor(out=xT[:, c, 0:T], in0=xT[:, c, 0:T], in1=fT[:, c, 0:T], op=ALU.add),
                          reads=[b_f[c], b_x[c]], writes=[b_x[c]])
                    if fuse_next:
                        sumsq_acc(xT[:, c, 0:T], 128, T, [b_x[c]], c == 0, c == NCH - 1)
                pend["ss"] = fuse_next

            def ffn(T, gpre, gpost_fn, wg, wu, wd, fuse_next=True):
                m0 = ar.mark()
                fT = ar.alloc(NCH * TT, F32).rearrange("p (c t) -> p c t", t=TT)
                actT = ar.alloc(NFF * TT, BF16).rearrange("p (c t) -> p c t", t=TT)
                b_f, b_act = [Buf(f"fT{m}") for m in range(NCH)], Buf("actT")
                prenorm(gpre, T)
                for g in range(NFF // 2):
                    wgs, bg = wq.get(wgroup(wg, g), 128, NCH, 256)
                    wus, bu = wq.get(wgroup(wu, g), 128, NCH, 256)
                    if g == 0:
                        first_banks = [nb() for _ in range(4)]
                        for c in range(NCH):
                            for q4, (wt_, bw_, col) in enumerate(((wgs, bg, 0), (wus, bu, 0), (wgs, bg, 128), (wus, bu, 128))):
                                ib4 = first_banks[q4]
                                mm(PS[ib4][:, 0:T], wt_[:, c, col:col + 128], hT[:, c, 0:T], c == 0, c == NCH - 1, [bw_, b_h[c]], PB[ib4])
                    for jj in range(2):
                        j = g * 2 + jj
                        if g == 0:
                            ig, iu = first_banks[2 * jj], first_banks[2 * jj + 1]
                        else:
                            ig = proj_fm(wgs, bg, jj * 128, 128, T)
                            iu = proj_fm(wus, bu, jj * 128, 128, T)
                        k = cnt["sg"] % 2
                        cnt["sg"] += 1
                        fw.op("act", lambda e, ig=ig, k=k: e.activation(out=sgt[k][:, 0:T], in_=PS[ig][:, 0:T], func=AF.Silu),
                              reads=[PB[ig]], writes=[b_sg[k]])
                        fw.op("dve", lambda e, iu=iu, k=k, j=j: e.tensor_tensor(out=actT[:, j, 0:T], in0=sgt[k][:, 0:T], in1=PS[iu][:, 0:T],
                                                                                op=ALU.mult),
                              reads=[b_sg[k], PB[iu]], writes=[b_act])
                wdv = dr[wd].rearrange("(j p) n -> p j n", p=128)
                for m in range(NCH):
                    ib = nb()
                    for half in range(2):
                        wds, bd = wq.get((wdv[:, half * 22:(half + 1) * 22, m * 128:(m + 1) * 128], (wd, m, half)), 128, 22, 128)
                        for jj in range(22):
                            j = half * 22 + jj
                            mm(PS[ib][:, 0:T], wds[:, jj, :], actT[:, j, 0:T], j == 0, j == NFF - 1, [bd, b_act], PB[ib])
                    fw.op("act", lambda e, ib=ib, m=m: e.activation(out=fT[:, m, 0:T], in_=PS[ib][:, 0:T], func=AF.Copy),
                          reads=[PB[ib]], writes=[b_f[m]])
                    sumsq_acc(fT[:, m, 0:T], 128, T, [b_f[m]], m == 0, m == NCH - 1)
                postnorm_residual(fT, b_f, gpost_fn, T, fuse_next)
                fw.barrier()
                ar.reset(m0)

            def hgrn_pair(T, hp, need_out, vh_tok, b_vh, tmp, b_tmp, chunk=64, crs_ap=None, pre=None):
                nblk = T // 128
                nchk = T // chunk
                if crs_ap is None:
                    crs_ap = crs
                wf, bwf = wq.get(wgroup("win", GF + hp), 128, NCH, 256)
                if need_out:
                    wqh, bwq = wq.get(wgroup("win", GQH + hp), 128, NCH, 256)
                heads = []
                for jj in range(2):
                    j = hp * 2 + jj
                    if pre is not None:
                        hd = pre[jj]
                        hd["j"] = j
                        hd["tA"], hd["tB"], hd["tC"], hd["tD"], hd["kh"] = tmp[jj]
                        hd["btmp"] = b_tmp[jj]
                        heads.append(hd)
                        continue
                    hd = {"bt": Buf(f"hd{j}"), "j": j}
                    hd["eLend"] = ar.alloc(16, F32)
                    hd["kt"] = ar.alloc(TT, BF16)
                    hd["khtok"] = ar.alloc(nblk * 128, BF16).rearrange("p (b k) -> p b k", k=128)
                    if need_out:
                        hd["qt"] = ar.alloc(TT, BF16)
                        hd["o32"] = ar.alloc(TT, F32)
                        hd["gs"] = ar.alloc(TT, BF16)
                    hd["tA"], hd["tB"], hd["tC"], hd["tD"], hd["kh"] = tmp[jj]
                    hd["btmp"] = b_tmp[jj]
                    heads.append(hd)
                R2 = range(2)
                ibs = [proj_fm(wf, bwf, jj * 128, 128, T) for jj in R2]
                for jj in R2:
                    hd = heads[jj]
                    fw.op("act", lambda e, hd=hd, ib=ibs[jj]: e.activation(out=hd["tA"][:, 0:T], in_=PS[ib][:, 0:T], func=AF.Sigmoid),
                          reads=[PB[ibs[jj]]], writes=[hd["btmp"]])
                    fw.op("act", lambda e, hd=hd, ib=ibs[jj]: e.activation(out=hd["tB"][:, 0:T], in_=PS[ib][:, 0:T], func=AF.Sigmoid, scale=-1.0),
                          reads=[PB[ibs[jj]]], writes=[hd["btmp"]])
                for jj in R2:
                    hd = heads[jj]
                    j = hd["j"]
                    fw.op("act", lambda e, hd=hd, j=j: e.activation(out=hd["tA"][:, 0:T], in_=hd["tA"][:, 0:T], func=AF.Ln,
                                                                    scale=gd[:, 40 + j:41 + j], bias=gd[:, 32 + j:33 + j]),
                          reads=[hd["btmp"], b_gd], writes=[hd["btmp"]])
                for jj in R2:
                    hd = heads[jj]
                    fw.op("dve", lambda e, hd=hd: e.tensor_tensor_scan(out=hd["tC"][:, 0:T], data0=crs_ap[:, 0:T], data1=hd["tA"][:, 0:T],
                                                                       initial=0.0, op0=ALU.mult, op1=ALU.add),
                          reads=[hd["btmp"], b_const], writes=[hd["btmp"]])
                for jj in R2:
                    hd = heads[jj]
                    fw.op("act", lambda e, hd=hd: e.activation(out=hd["tD"][:, 0:T], in_=hd["tC"][:, 0:T], func=AF.Exp, scale=-1.0),
                          reads=[hd["btmp"]], writes=[hd["btmp"]])
                    fw.op("act", lambda e, hd=hd: e.activation(
                        out=hd["eLend"][:, 0:nchk], in_=hd["tC"][:, 0:T].rearrange("p (c t) -> p c t", t=chunk)[:, :, chunk - 1], func=AF.Exp),
                        reads=[hd["btmp"]], writes=[hd["bt"]])
                    if need_out:
                        fw.op("act", lambda e, hd=hd: e.activation(out=hd["tA"][:, 0:T], in_=hd["tC"][:, 0:T], func=AF.Exp),
                              reads=[hd["btmp"]], writes=[hd["btmp"]])
                for jj in R2:
                    hd = heads[jj]
                    j = hd["j"]
                    fw.op("dve", lambda e, hd=hd, j=j: e.scalar_tensor_tensor(out=hd["kt"][:, 0:T], in0=hd["tB"][:, 0:T], scalar=gd[:, 40 + j:41 + j],
                                                                              in1=hd["tD"][:, 0:T], op0=ALU.mult, op1=ALU.mult),
                          reads=[hd["btmp"], b_gd], writes=[hd["bt"]])
                    fw.op("dve", lambda e, hd=hd: e.tensor_tensor(
                        out=hd["kh"][:, 0:T].rearrange("p (c t) -> p c t", t=chunk), in0=hd["kt"][:, 0:T].rearrange("p (c t) -> p c t", t=chunk),
                        in1=hd["eLend"][:, 0:nchk].unsqueeze(2).to_broadcast([128, nchk, chunk]), op=ALU.mult),
                        reads=[hd["bt"]], writes=[hd["btmp"]])
                for jj in R2:
                    hd = heads[jj]
                    for blk in range(nblk):
                        fw.op("pe", lambda e, blk=blk, hd=hd, jj=jj: e.transpose(out=PT7[:, jj * 512 + blk * 128:jj * 512 + (blk + 1) * 128],
                                                                              in_=hd["kh"][:, blk * 128:(blk + 1) * 128], identity=ident[:]),
                              reads=[hd["btmp"], b_const], writes=[PB[7]])
                for jj in R2:
                    hd = heads[jj]
                    fw.op("act", lambda e, hd=hd, jj=jj: e.activation(out=hd["khtok"],
                                                                      in_=PT7[:, jj * 512:jj * 512 + nblk * 128].rearrange("p (b k) -> p b k", k=128),
                                                                      func=AF.Copy), reads=[PB[7]], writes=[hd["bt"]])
                if need_out:
                    ibq = [proj_fm(wqh, bwq, jj * 128, 128, T) for jj in R2]
                    for jj in R2:
                        hd = heads[jj]
                        fw.op("act", lambda e, hd=hd, ib=ibq[jj]: e.activation(out=hd["tB"][:, 0:T], in_=PS[ib][:, 0:T], func=AF.Silu),
                              reads=[PB[ibq[jj]]], writes=[hd["btmp"]])
                    for jj in R2:
                        hd = heads[jj]
                        fw.op("dve", lambda e, hd=hd: e.tensor_tensor(out=hd["qt"][:, 0:T], in0=hd["tB"][:, 0:T], in1=hd["tA"][:, 0:T], op=ALU.mult),
                              reads=[hd["btmp"]], writes=[hd["bt"]])
                wi, bwi = wq.get(wgroup("win", GI + hp), 128, NCH, 256)
                for blk in range(nblk):
                    ib = proj_tm(wi, bwi, blk, 256)
                    fw.op("act", lambda e, ib=ib, blk=blk: e.activation(out=vh_tok[:, blk, :], in_=PS[ib][:, 0:256], func=AF.Copy),
                          reads=[PB[ib]], writes=[b_vh])
                if need_out:
                    wgh, bwg = wq.get(wgroup("win", GG + hp), 128, NCH, 256)
                    for jj in R2:
                        hd = heads[jj]
                        ib = proj_fm(wgh, bwg, jj * 128, 128, T)
                        fw.op("act", lambda e, ib=ib, hd=hd: e.activation(out=hd["gs"][:, 0:T], in_=PS[ib][:, 0:T], func=AF.Silu),
                              reads=[PB[ib]], writes=[hd["bt"]])
                return heads

            def state_chain(hd, vh_tok, b_vh, jj, keep):
                j = hd["j"]
                iu = [nb(), nb()]
                for c in range(8):
                    blk, e2 = c // 2, c % 2
                    r0 = 64 * e2
                    mm(PS[iu[c % 2]][:, (c // 2) * 128:(c // 2 + 1) * 128], hd["khtok"][r0:r0 + 64, blk, :],
                       vh_tok[r0:r0 + 64, blk, jj * 128:(jj + 1) * 128], True, True, [hd["bt"], b_vh], PB[iu[c % 2]])
                cp(13)
                if keep:
                    Sall = ar.alloc(9 * 128, F32).rearrange("p (c v) -> p c v", v=128)
                    Sallbf = ar.alloc(8 * 128, BF16).rearrange("p (c v) -> p c v", v=128)
                    b_Sall, b_Sbf2 = Buf("Sall"), Buf("Sallbf")
                    fw.op("dve", lambda e: e.tensor_copy(out=Sall[:, 0, :], in_=S32[:, j, :]), reads=[b_S32[j]], writes=[b_Sall])
                    for c in range(8):
                        fw.op("dve", lambda e, c=c: e.scalar_tensor_tensor(out=Sall[:, c + 1, :], in0=Sall[:, c, :], scalar=hd["eLend"][:, c:c + 1],
                                                                           in1=PS[iu[c % 2]][:, (c // 2) * 128:(c // 2 + 1) * 128], op0=ALU.mult, op1=ALU.add),
                              reads=[b_Sall, hd["bt"], PB[iu[c % 2]]], writes=[b_Sall])
                    cp(14)
                    fw.op("act", lambda e: e.activation(out=Sallbf, in_=Sall[:, 0:8, :], func=AF.Copy), reads=[b_Sall], writes=[b_Sbf2])
                    cp(15)
                    fw.op("dve", lambda e: e.tensor_copy(out=S32[:, j, :], in_=Sall[:, 8, :]), reads=[b_Sall], writes=[b_S32[j]])
                    return Sallbf, b_Sbf2
                for c in range(8):
                    fw.op("dve", lambda e, c=c: e.scalar_tensor_tensor(out=S32[:, j, :], in0=S32[:, j, :], scalar=hd["eLend"][:, c:c + 1],
                                                                       in1=PS[iu[c % 2]][:, (c // 2) * 128:(c // 2 + 1) * 128], op0=ALU.mult, op1=ALU.add),
                          reads=[b_S32[j], hd["bt"], PB[iu[c % 2]]], writes=[b_S32[j]])
                return None, None

            def state_chain_snap(hd, vh_tok, b_vh, jj):
                j = hd["j"]
                iu = [nb(), nb()]
                for c in range(8):
                    blk, e2 = c // 2, c % 2
                    r0 = 64 * e2
                    mm(PS[iu[c % 2]][:, (c // 2) * 128:(c // 2 + 1) * 128], hd["khtok"][r0:r0 + 64, blk, :],
                       vh_tok[r0:r0 + 64, blk, jj * 128:(jj + 1) * 128], True, True, [hd["bt"], b_vh], PB[iu[c % 2]])
                Sab, b_Sab = hd["Sab"], hd["b_Sab"]
                for c in range(8):
                    fw.op("dve", lambda e, c=c: e.tensor_copy(out=Sab[:, c, :], in_=S32[:, j, :]), reads=[b_S32[j]], writes=[b_Sab])
                    fw.op("dve", lambda e, c=c: e.scalar_tensor_tensor(out=S32[:, j, :], in0=S32[:, j, :], scalar=hd["eLend"][:, c:c + 1],
                                                                       in1=PS[iu[c % 2]][:, (c // 2) * 128:(c // 2 + 1) * 128], op0=ALU.mult, op1=ALU.add),
                          reads=[b_S32[j], hd["bt"], PB[iu[c % 2]]], writes=[b_S32[j]])
                return Sab, b_Sab

            def hgrn_tmp():
                tmp = [[ar.alloc(TT, F32) for _ in range(4)] + [ar.alloc(TT, BF16)] for _ in range(2)]
                return tmp, [Buf("hgtmp0"), Buf("hgtmp1")]

            def kv_proj(T, blks, prev_slot=False, out_last=False):
                wk_, bk_ = wq.get(wgroup("win", GK), 128, NCH, 256)
                wv_, bv_ = wq.get(wgroup("win", GV_), 128, NCH, 256)
                if prev_slot:
                    c0, n, dst0 = T - 128, 128, 0
                else:
                    c0, n, dst0 = 0, T, 128
                for kvh in range(4):
                    ib = proj_fm(wk_, bk_, kvh * 64, 64, n, c0)
                    fw.op("dve", lambda e, ib=ib, kvh=kvh: e.tensor_copy(out=kT[0:64, kvh, dst0:dst0 + n], in_=PS[ib][0:64, 0:n]),
                          reads=[PB[ib]], writes=[b_kT])
                for blk in blks:
                    slot = 0 if prev_slot else blk + 1
                    iv = proj_tm(wv_, bv_, blk, 256)
                    fw.op("act", lambda e, iv=iv, slot=slot: e.activation(out=vtok[:, slot, :], in_=PS[iv][:, 0:256], func=AF.Copy),
                          reads=[PB[iv]], writes=[b_vtok[slot]])
                    if out_last and blk == blks[-1]:
                        ik = proj_tm(wk_, bk_, blk, 256)
                        kvl = ar.alloc(512, F32)
                        b_kvl = Buf("kvl")
                        fw.op("dve", lambda e, ik=ik, kvl=kvl: e.tensor_copy(out=kvl[:, 0:256], in_=PS[ik][:, 0:256]), reads=[PB[ik]], writes=[b_kvl])
                        fw.op("act", lambda e, iv=iv, kvl=kvl: e.activation(out=kvl[:, 256:512], in_=PS[iv][:, 0:256], func=AF.Copy), reads=[PB[iv]], writes=[b_kvl])
                        bo = Buf("o_pwk")
                        fw.dma("sp", lambda e, kvl=kvl: e.dma_start(out=dr["pwk"], in_=kvl[:, 0:256]), reads=[b_kvl], writes=[bo])
                        fw.dma("sp", lambda e, kvl=kvl: e.dma_start(out=dr["pwv"], in_=kvl[:, 256:512]), reads=[b_kvl], writes=[bo])
                        outbufs.append(bo)

            def pass0_tile(t, last):
                T = TT
                fw.dma("sp", lambda e: e.dma_start(out=xT[:], in_=dr["xp"].rearrange("(c p) n -> p c n", p=128)[:, :, t * TT:(t + 1) * TT]),
                       writes=b_x)
                ffn(T, "f1pre", lambda c: gd[:, c:c + 1], "w1g", "w1u", "w1d")
                prenorm("mpre", T)
                m0 = ar.mark()
                tmps = [hgrn_tmp(), hgrn_tmp()]
                for hp in range(4):
                    vh_tok = ar.alloc(4 * 256, BF16).rearrange("p (b k) -> p b k", k=256)
                    b_vh = Buf("vh")
                    tmp, b_tmp = tmps[hp % 2]
                    heads = hgrn_pair(T, hp, False, vh_tok, b_vh, tmp, b_tmp)
                    for jj in range(2):
                        state_chain(heads[jj], vh_tok, b_vh, jj, False)
                if last:
                    kv_proj(T, [3], prev_slot=True)
                fw.barrier()
                ar.reset(m0)

            def wout_stage(T, mixA, mixR, b_mixA, b_mixR):
                mixA2 = ar.alloc(8 * T, BF16).rearrange("p (h t) -> p h t", t=T)
                b_mixA2 = Buf("mixA2")
                mv = mixA[0:64, :, 0:T].rearrange("p (a two) t -> p a two t", two=2)
                fw.dma("sp", lambda e: e.dma_start(out=mixA2[0:64, :, :], in_=mv[:, :, 0, :]), reads=[b_mixA], writes=[b_mixA2])
                fw.dma("sp", lambda e: e.dma_start(out=mixA2[64:128, :, :], in_=mv[:, :, 1, :]), reads=[b_mixA], writes=[b_mixA2])
                fT = ar.alloc(NCH * TT, F32).rearrange("p (c t) -> p c t", t=TT)
                b_f = [Buf(f"fT{m}") for m in range(NCH)]
                for g in range(8):
                    wo_, bwo_ = wq.get(wgroup("wout", g), 128, NCH, 256)
                    for mm_ in range(2):
                        m = g * 2 + mm_
                        ib = nb()
                        for c in range(8):
                            mm(PS[ib][:, 0:T], wo_[:, c, mm_ * 128:(mm_ + 1) * 128], mixA2[:, c, 0:T], c == 0, False, [bwo_, b_mixA2], PB[ib])
                        for j in range(8):
                            mm(PS[ib][:, 0:T], wo_[:, 8 + j, mm_ * 128:(mm_ + 1) * 128], mixR[:, j, 0:T], False, j == 7, [bwo_, b_mixR], PB[ib])
                        fw.op("act", lambda e, ib=ib, m=m: e.activation(out=fT[:, m, 0:T], in_=PS[ib][:, 0:T], func=AF.Copy),
                              reads=[PB[ib]], writes=[b_f[m]])
                        sumsq_acc(fT[:, m, 0:T], 128, T, [b_f[m]], m == 0, m == NCH - 1)
                postnorm_residual(fT, b_f, lambda c: gcol("mpost", c), T)
                fw.barrier()

            def main_tile(t):
                import os
                skip = os.environ.get("KSKIP", "").split(",")
                T = TT
                last = (t == NPT - 1)
                fw.dma("sp", lambda e: e.dma_start(out=xT[:], in_=dr["xm"].rearrange("(c p) n -> p c n", p=128)[:, :, t * TT:(t + 1) * TT]),
                       writes=b_x)
                if "ffn1" not in skip:
                    ffn(T, "f1pre", lambda c: gd[:, c:c + 1], "w1g", "w1u", "w1d")
                if "mix" in skip:
                    return main_tail(t, skip, T)
                prenorm("mpre", T)
                mA = ar.mark()
                mixA = ar.alloc(16 * TT, BF16).rearrange("p (h t) -> p h t", t=TT)
                mixR = ar.alloc(8 * TT, BF16).rearrange("p (h t) -> p h t", t=TT)
                b_mixA, b_mixR = Buf("mixA"), Buf("mixR")
                m1 = ar.mark()
                abias = ar.alloc(2 * 16 * 128, F32).rearrange("p (k h q) -> p k h q", k=2, h=16)
                b_ab = Buf("abias")
                fw.dma("sp", lambda e: e.dma_start(out=abias.rearrange("p k h q -> p (k h q)"), in_=dr["cab"]), writes=[b_ab])
                qT = ar.alloc(16 * TT, BF16).rearrange("p (h t) -> p h t", t=TT)
                b_qT = Buf("qT")
                b_qs = Buf("qstage")
                for g in range(4):
                    wqa, bwqa = wq.get(wgroup("win", GQA + g), 128, NCH, 256)
                    if g == 0:
                        qb = [nb(), nb()]
                        for c in range(NCH):
                            for pp in range(2):
                                mm(PS[qb[pp]][:, 0:T], wqa[:, c, pp * 128:(pp + 1) * 128], hT[:, c, 0:T], c == 0, c == NCH - 1, [bwqa, b_h[c]], PB[qb[pp]])
                    for pp in range(2):
                        h0 = g * 4 + pp * 2
                        ib = qb[pp] if g == 0 else proj_fm(wqa, bwqa, pp * 128, 128, T)
                        fw.op("act", lambda e, ib=ib, h0=h0: e.activation(out=qT[0:64, h0, 0:T], in_=PS[ib][0:64, 0:T], func=AF.Copy, scale=0.125),
                              reads=[PB[ib]], writes=[b_qT])
                        fw.op("act", lambda e, ib=ib, h0=h0: e.activation(out=qT[64:128, h0 + 1, 0:T], in_=PS[ib][64:128, 0:T], func=AF.Copy, scale=0.125),
                              reads=[PB[ib]], writes=[b_qs])
                qv = qT[:, :, 0:T].rearrange("p (a two) t -> p a two t", two=2)
                fw.dma("sp", lambda e: e.dma_start(out=qv[0:64, :, 1, :], in_=qv[64:128, :, 1, :]), reads=[b_qs], writes=[b_qT])
                kv_proj(T, [0, 1, 2, 3], out_last=last)
                o32 = ar.alloc(16 * 128, F32).rearrange("p (h q) -> p h q", q=128)
                b_o32 = Buf("o32")
                scb = [ar.alloc(512, F32) for _ in range(2)]
                PTt = [ar.alloc(512, BF16) for _ in range(2)]
                b_scb = [Buf("scb0"), Buf("scb1")]
                b_PT = [Buf("PT0"), Buf("PT1")]
                rden = ar.alloc(512, F32)
                b_rden = Buf("rden")
                sq16 = ar.alloc(16 * 128, BF16).rearrange("p (h q) -> p h q", q=128)
                b_sq16 = Buf("sq16")
                def att_A(blk, kvh):
                    qs = slice(blk * 128, (blk + 1) * 128)
                    ibs = [nb(), nb()]
                    for kc in range(2):
                        ks = slice(blk * 128 + kc * 128, blk * 128 + kc * 128 + 128)
                        for g in range(4):
                            h = 4 * kvh + g
                            mm(PS[ibs[kc]][:, g * 128:(g + 1) * 128], kT[0:64, kvh, ks], qT[0:64, h, qs], True, True, [b_kT, b_qT], PB[ibs[kc]])
                        fw.op("dve", lambda e, kc=kc, ib=ibs[kc]: e.tensor_tensor(
                            out=scb[kc].rearrange("p (g q) -> p g q", q=128), in0=PS[ib][:, :].rearrange("p (g q) -> p g q", q=128),
                            in1=abias[:, kc, 4 * kvh:4 * kvh + 4, :], op=ALU.add), reads=[PB[ibs[kc]], b_ab], writes=[b_scb[kc]])
                        if t == 0 and blk == 0 and kc == 0:
                            fw.op("act", lambda e, kc=kc: e.activation(out=PTt[kc], in_=scb[kc], func=AF.Exp, bias=negflag[:, 0:1]),
                                  reads=[b_scb[kc], b_gd], writes=[b_PT[kc]])
                        else:
                            fw.op("act", lambda e, kc=kc: e.activation(out=PTt[kc], in_=scb[kc], func=AF.Exp),
                                  reads=[b_scb[kc]], writes=[b_PT[kc]])
                    io, idn = nb(), nb()
                    for kc in range(2):
                        for g in range(4):
                            mm(PS[io][0:64, g * 128:(g + 1) * 128], vtok[:, blk + kc, kvh * 64:(kvh + 1) * 64], PTt[kc][:, g * 128:(g + 1) * 128],
                               kc == 0, kc == 1, [b_vtok[blk + kc], b_PT[kc]], PB[io])
                    for kc in range(2):
                        for g in range(4):
                            mm(PS[idn][0:64, g * 128:(g + 1) * 128], ones[:, 0:64], PTt[kc][:, g * 128:(g + 1) * 128],
                               kc == 0, kc == 1, [b_const, b_PT[kc]], PB[idn])
                    return io, idn

                def att_B(blk, kvh, io, idn):
                    fw.op("dve", lambda e: e.tensor_tensor(
                        out=rden[0:64, :].rearrange("p (g q) -> p g q", q=128), in0=PS[idn][0:64, :].rearrange("p (g q) -> p g q", q=128),
                        in1=gd[0:64, 48 + 4 * kvh:52 + 4 * kvh].unsqueeze(2).to_broadcast([64, 4, 128]), op=ALU.add),
                        reads=[PB[idn], b_gd], writes=[b_rden])
                    if kvh % 2 == 0:
                        fw.op("act", lambda e: e.activation(out=rden[0:64, :], in_=rden[0:64, :], func=AF.Ln), reads=[b_rden], writes=[b_rden])
                        fw.op("act", lambda e: e.activation(out=rden[0:64, :], in_=rden[0:64, :], func=AF.Exp, scale=-1.0), reads=[b_rden], writes=[b_rden])
                        fw.op("dve", lambda e: e.tensor_tensor(
                            out=o32[0:64, 4 * kvh:4 * kvh + 4, :], in0=PS[io][0:64, :].rearrange("p (g q) -> p g q", q=128),
                            in1=rden[0:64, :].rearrange("p (g q) -> p g q", q=128), op=ALU.mult), reads=[PB[io], b_rden], writes=[b_o32])
                    else:
                        fw.op("dve", lambda e: e.reciprocal(out=rden[0:64, :], in_=rden[0:64, :]), reads=[b_rden], writes=[b_rden])
                        fw.op("dve", lambda e: e.tensor_tensor(
                            out=o32[0:64, 4 * kvh:4 * kvh + 4, :], in0=PS[io][0:64, :].rearrange("p (g q) -> p g q", q=128),
                            in1=rden[0:64, :].rearrange("p (g q) -> p g q", q=128), op=ALU.mult), reads=[PB[io], b_rden], writes=[b_o32])
                    if kvh == 3:
                        qs = slice(blk * 128, (blk + 1) * 128)
                        fw.op("act", lambda e: e.activation(out=sq16[0:64], in_=o32[0:64], func=AF.Square), reads=[b_o32], writes=[b_sq16])
                        fresh[6] = True
                        for h in range(16):
                            mm(PS[6][0:64, 0:128], ones[0:64, 0:64], sq16[0:64, h, :], h == 0, h == 15, [b_sq16, b_const], PB[6])
                        rstd_finish(64, 1024, 128)
                        fw.op("dve", lambda e: e.tensor_tensor(out=o32[0:64], in0=o32[0:64], in1=rstd[0:64, 0:128].unsqueeze(1).to_broadcast([64, 16, 128]),
                                                               op=ALU.mult), reads=[b_o32, b_rstd], writes=[b_o32])
                        fw.op("dve", lambda e: e.tensor_tensor(out=mixA[0:64, :, qs], in0=o32[0:64],
                                                               in1=gv[0:64, GV["again"]:GV["again"] + 16].unsqueeze(2).to_broadcast([64, 16, 128]),
                                                               op=ALU.mult), reads=[b_o32, b_gv], writes=[b_mixA])

                items = [(blk, kvh) for blk in range(4) for kvh in range(4)]
                prev = None
                for it in items:
                    cur = (it, att_A(*it))
                    if prev is not None:
                        att_B(*prev[0], *prev[1])
                    prev = cur
                att_B(*prev[0], *prev[1])
                fw.op("dve", lambda e: e.tensor_copy(out=kT[0:64, :, 0:128], in_=kT[0:64, :, T:T + 128]), reads=[b_kT], writes=[b_kT])
                fw.op("dve", lambda e: e.tensor_copy(out=vtok[:, 0, :], in_=vtok[:, 4, :]), reads=[b_vtok[4]], writes=[b_vtok[0]])
                fw.barrier()
                ar.reset(m1)
                tmp, b_tmp = hgrn_tmp()
                Am = [ar.alloc(128, BF16) for _ in range(2)]
                b_Am = [Buf("Am0"), Buf("Am1")]
                ai = 0
                regions = []
                for r in range(2):
                    vh_tok = ar.alloc(4 * 256, BF16).rearrange("p (b k) -> p b k", k=256)
                    pre = []
                    for jj in range(2):
                        hd = {"bt": Buf(f"hdr{r}{jj}")}
                        hd["eLend"] = ar.alloc(16, F32)
                        hd["kt"] = ar.alloc(TT, BF16)
                        hd["khtok"] = ar.alloc(4 * 128, BF16).rearrange("p (b k) -> p b k", k=128)
                        hd["qt"] = ar.alloc(TT, BF16)
                        hd["o32"] = ar.alloc(TT, F32)
                        hd["gs"] = ar.alloc(TT, BF16)
                        hd["Sab"] = ar.alloc(8 * 128, BF16).rearrange("p (c v) -> p c v", v=128)
                        hd["b_Sab"] = Buf(f"Sab{r}{jj}")
                        pre.append(hd)
                    regions.append((vh_tok, Buf(f"vh{r}"), pre))
                for hp in range(4):
                    vh_tok, b_vh, pre = regions[hp % 2]
                    heads = hgrn_pair(T, hp, True, vh_tok, b_vh, tmp, b_tmp, pre=pre)
                    sall = [state_chain_snap(heads[jj], vh_tok, b_vh, jj) for jj in range(2)]
                    for blk in range(4):
                        bs = slice(blk * 128, (blk + 1) * 128)
                        for jj in range(2):
                            j = hp * 2 + jj
                            hd = heads[jj]
                            bt = hd["bt"]
                            Sab, b_Sab = sall[jj]
                            ia = nb()
                            mm(PS[ia][:, 0:128], hd["kt"][:, bs], hd["qt"][:, bs], True, True, [bt], PB[ia])
                            a = ai % 2
                            ai += 1
                            fw.op("dve", lambda e, ia=ia, a=a: e.tensor_tensor(out=Am[a], in0=PS[ia][:, 0:128], in1=hmask[:], op=ALU.mult),
                                  reads=[PB[ia], b_const], writes=[b_Am[a]])
                            io = nb()
                            mm(PS[io][:, 0:128], vh_tok[:, blk, jj * 128:(jj + 1) * 128], Am[a], True, False, [b_vh, b_Am[a]], PB[io])
                            mm(PS[io][:, 0:64], Sab[:, 2 * blk, :], hd["qt"][:, blk * 128:blk * 128 + 64], False, True, [b_Sab, bt], PB[io])
                            mm(PS[io][:, 64:128], Sab[:, 2 * blk + 1, :], hd["qt"][:, blk * 128 + 64:blk * 128 + 128], False, True, [b_Sab, bt], PB[io])
                            fw.op("act", lambda e, io=io, hd=hd, bs=bs: e.activation(out=hd["o32"][:, bs], in_=PS[io][:, 0:128], func=AF.Copy),
                                  reads=[PB[io]], writes=[bt])
                    for jj in range(2):
                        j = hp * 2 + jj
                        hd = heads[jj]
                        bt = hd["bt"]
                        sumsq_rstd([hd["o32"][:, 0:T]], 128, 128, T, [bt])
                        fw.op("dve", lambda e, hd=hd: e.scalar_tensor_tensor(out=hd["o32"][:, 0:T], in0=hd["o32"][:, 0:T], scalar=gcol("hgain"),
                                                                             in1=rstd[:, 0:T], op0=ALU.mult, op1=ALU.mult),
                              reads=[bt, b_rstd, b_gv], writes=[bt])
                        fw.op("dve", lambda e, hd=hd, j=j: e.tensor_tensor(out=mixR[:, j, 0:T], in0=hd["o32"][:, 0:T], in1=hd["gs"][:, 0:T], op=ALU.mult),
                              reads=[bt], writes=[b_mixR])
                fw.barrier()
                if last:
                    bo = Buf("o_phg")
                    fw.dma("sp", lambda e: e.dma_start(out=dr["phg"].rearrange("h k v -> k h v"), in_=S32[:]), reads=b_S32, writes=[bo])
                    outbufs.append(bo)
                ar.reset(m1)
                ar.reset(m1)
                wout_stage(T, mixA, mixR, b_mixA, b_mixR)
                ar.reset(mA)
                main_tail(t, skip, T)

            def main_tail(t, skip, T):
                if "mem" not in skip:
                    mem_attn(T, KmT, Vmt, b_Km, b_Vm)
                if "ffn2" not in skip:
                    ffn(T, "f2pre", lambda c: gd[:, 16 + c:17 + c], "w2g", "w2u", "w2d", fuse_next=False)
                bo = Buf("o_ym")
                fw.dma("sp", lambda e: e.dma_start(out=dr["ym"].rearrange("(c p) n -> p c n", p=128)[:, :, t * TT:(t + 1) * TT], in_=xT[:]),
                       reads=b_x, writes=[bo])
                outbufs.append(bo)

            def mem_attn(T, KT, Vt, b_K, b_V):
                m0 = ar.mark()
                prenorm("epre", T)
                qmT = ar.alloc(4 * TT, BF16).rearrange("p (h t) -> p h t", t=TT)
                omT = ar.alloc(4 * TT, BF16).rearrange("p (h t) -> p h t", t=TT)
                PmT = ar.alloc(2 * TT, BF16).rearrange("p (c t) -> p c t", t=TT)
                rdn = ar.alloc(TT, F32)
                fT = ar.alloc(NCH * TT, F32).rearrange("p (c t) -> p c t", t=TT)
                b_q, b_o, b_P, b_r, b_f = Buf("qm"), Buf("om"), Buf("Pm"), Buf("rdn"), [Buf(f"fT{m}") for m in range(NCH)]
                for g in range(2):
                    wqm, bwq = wq.get(wgroup("wmq", g), 128, NCH, 256)
                    for hh in range(2):
                        h = g * 2 + hh
                        ib = proj_fm(wqm, bwq, hh * 128, 128, T)
                        fw.op("act", lambda e, ib=ib, h=h: e.activation(out=qmT[:, h, 0:T], in_=PS[ib][:, 0:T], func=AF.Copy, scale=128 ** -0.5),
                              reads=[PB[ib]], writes=[b_q])
                for h in range(4):
                    for mc in range(2):
                        ib = nb()
                        mm(PS[ib][:, 0:T], KT[:, h, mc * 128:(mc + 1) * 128], qmT[:, h, 0:T], True, True, [b_K, b_q], PB[ib])
                        fw.op("act", lambda e, ib=ib, mc=mc: e.activation(out=PmT[:, mc, 0:T], in_=PS[ib][:, 0:T], func=AF.Exp),
                              reads=[PB[ib]], writes=[b_P])
                    io, idn = nb(), nb()
                    for mc in range(2):
                        mm(PS[io][:, 0:T], Vt[:, mc, h * 128:(h + 1) * 128], PmT[:, mc, 0:T], mc == 0, mc == 1, [b_V, b_P], PB[io])
                    for mc in range(2):
                        mm(PS[idn][:, 0:T], ones[:], PmT[:, mc, 0:T], mc == 0, mc == 1, [b_const, b_P], PB[idn])
                    if h % 2 == 0:
                        fw.op("dve", lambda e, idn=idn: e.reciprocal(out=rdn[:, 0:T], in_=PS[idn][:, 0:T]), reads=[PB[idn]], writes=[b_r])
                    else:
                        fw.op("act", lambda e, idn=idn: e.activation(out=rdn[:, 0:T], in_=PS[idn][:, 0:T], func=AF.Ln), reads=[PB[idn]], writes=[b_r])
                        fw.op("act", lambda e: e.activation(out=rdn[:, 0:T], in_=rdn[:, 0:T], func=AF.Exp, scale=-1.0), reads=[b_r], writes=[b_r])
                    fw.op("dve", lambda e, io=io, h=h: e.tensor_tensor(out=omT[:, h, 0:T], in0=PS[io][:, 0:T], in1=rdn[:, 0:T], op=ALU.mult),
                          reads=[PB[io], b_r], writes=[b_o])
                mem_out(T, omT, b_o, fT, b_f)
                fw.barrier()
                ar.reset(m0)

            def mem_out(T, omT, b_o, fT, b_f):
                wov = dr["wmo"].rearrange("(h p) n -> p h n", p=128)
                for g in range(2):
                    wo, bwo = wq.get((wov[:, :, g * 1024:(g + 1) * 1024], ("wmo", g)), 128, 4, 1024)
                    for mm_ in range(8):
                        m = g * 8 + mm_
                        ib = nb()
                        for h in range(4):
                            mm(PS[ib][:, 0:T], wo[:, h, mm_ * 128:(mm_ + 1) * 128], omT[:, h, 0:T], h == 0, h == 3, [bwo, b_o], PB[ib])
                        fw.op("act", lambda e, ib=ib, m=m: e.activation(out=fT[:, m, 0:T], in_=PS[ib][:, 0:T], func=AF.Copy),
                              reads=[PB[ib]], writes=[b_f[m]])
                        sumsq_acc(fT[:, m, 0:T], 128, T, [b_f[m]], m == 0, m == NCH - 1)
                postnorm_residual(fT, b_f, lambda c: gcol("epost", c), T)

            def mem_kv():
                m0 = ar.mark()
                T = 256
                fw.dma("sp", lambda e: e.dma_start(out=xT[:, :, 0:256], in_=dr["memT"].rearrange("(c p) n -> p c n", p=128)), writes=b_x)
                prenorm("ekv", T)
                stg = ar.alloc(2 * 512, F32).rearrange("p (c t) -> p c t", t=512)
                stg2 = ar.alloc(2 * 512, F32).rearrange("p (c t) -> p c t", t=512)
                b_stg, b_stg2 = Buf("stg"), Buf("stg2")
                bo = Buf("o_pm")
                for g in range(2):
                    wk, bwk = wq.get(wgroup("wmk", g), 128, NCH, 256)
                    for hh in range(2):
                        h = g * 2 + hh
                        ib = proj_fm(wk, bwk, hh * 128, 128, T)
                        fw.op("act", lambda e, ib=ib, h=h: e.activation(out=KmT[:, h, :], in_=PS[ib][:, 0:T], func=AF.Copy), reads=[PB[ib]], writes=[b_Km])
                        cp(4)
                    for mc in range(2):
                        ib = proj_tm(wk, bwk, mc, 256)
                        fw.op("act", lambda e, ib=ib, mc=mc, g=g: e.activation(out=stg[:, mc, g * 256:(g + 1) * 256], in_=PS[ib][:, 0:256], func=AF.Copy),
                              reads=[PB[ib]], writes=[b_stg])
                cp(5)
                fw.dma("sp", lambda e: e.dma_start(out=dr["pmk"].rearrange("(c p) n -> p c n", p=128), in_=stg), reads=[b_stg], writes=[bo])
                outbufs.append(bo)
                cp(6)
                for g in range(2):
                    wv, bwv = wq.get(wgroup("wmv", g), 128, NCH, 256)
                    for mc in range(2):
                        ib = proj_tm(wv, bwv, mc, 256)
                        fw.op("act", lambda e, ib=ib, mc=mc, g=g: e.activation(out=stg2[:, mc, g * 256:(g + 1) * 256], in_=PS[ib][:, 0:256], func=AF.Copy),
                              reads=[PB[ib]], writes=[b_stg2])
                        fw.op("dve", lambda e, mc=mc, g=g: e.tensor_copy(out=Vmt[:, mc, g * 256:(g + 1) * 256], in_=stg2[:, mc, g * 256:(g + 1) * 256]),
                              reads=[b_stg2], writes=[b_Vm])
                fw.dma("sp", lambda e: e.dma_start(out=dr["pmv"].rearrange("(c p) n -> p c n", p=128), in_=stg2), reads=[b_stg2], writes=[bo])
                outbufs.append(bo)
                fw.barrier()
                ar.reset(m0)

            def sample_tile():
                T = TS
                wq.in_sample = True
                import os
                skip = os.environ.get("KSKIP", "").split(",")
                fw.dma("sp", lambda e: e.dma_start(out=xT[:, :, 0:T], in_=dr["xs"].rearrange("(c p) n -> p c n", p=128)), writes=b_x)
                if "ffn1" not in skip:
                    ffn(T, "f1pre", lambda c: gd[:, c:c + 1], "w1g", "w1u", "w1d")
                if "mix" not in skip:
                    sample_mix(T)
                if "mem" not in skip:
                    sample_mem(T)
                if "ffn2" not in skip:
                    ffn(T, "f2pre", lambda c: gd[:, 16 + c:17 + c], "w2g", "w2u", "w2d", fuse_next=False)
                bo = Buf("o_ys")
                fw.dma("sp", lambda e: e.dma_start(out=dr["ys"].rearrange("(c p) n -> p c n", p=128), in_=xT[:, :, 0:T]), reads=b_x, writes=[bo])
                outbufs.append(bo)

            def sample_mix(T):
                prenorm("mpre", T)
                mA = ar.mark()
                mixA = ar.alloc(16 * T, BF16).rearrange("p (h t) -> p h t", t=T)
                mixR = ar.alloc(8 * T, BF16).rearrange("p (h t) -> p h t", t=T)
                b_mixA, b_mixR = Buf("mixA"), Buf("mixR")
                m1 = ar.mark()
                qT = ar.alloc(16 * T, BF16).rearrange("p (h t) -> p h t", t=T)
                b_qT = Buf("qT")
                ckTb = ar.alloc(64 * 128, BF16).rearrange("p (a k) -> p a k", k=128)
                cvb = ar.alloc(16 * 256, BF16).rearrange("p (n c) -> p n c", c=256)
                sbc = ar.alloc(128, F32)
                sbn = ar.alloc(2048, F32).rearrange("p (n c) -> p n c", c=128)
                b_cache, b_sb = Buf("cache"), Buf("sbias")
                stg32 = ar.alloc(2048, F32)
                b_stg = Buf("stg32")
                for pc in range(4):
                    fw.dma("sp", lambda e, pc=pc: e.dma_start(out=stg32[0:64, :].rearrange("p (a k) -> p a k", k=128),
                                                               in_=dr["ckT"][:, pc * 16:(pc + 1) * 16, :]), writes=[b_stg])
                    fw.op("act", lambda e, pc=pc: e.activation(out=ckTb[0:64, pc * 16:(pc + 1) * 16, :],
                                                               in_=stg32[0:64, :].rearrange("p (a k) -> p a k", k=128), func=AF.Copy),
                          reads=[b_stg], writes=[b_cache])
                for pc in range(2):
                    fw.dma("sp", lambda e, pc=pc: e.dma_start(out=stg32.rearrange("p (n c) -> p n c", c=256),
                                                               in_=dr["cwv"][pc * 8:(pc + 1) * 8].rearrange("n p c -> p n c")), writes=[b_stg])
                    fw.op("dve", lambda e, pc=pc: e.tensor_copy(out=cvb[:, pc * 8:(pc + 1) * 8, :], in_=stg32.rearrange("p (n c) -> p n c", c=256)),
                          reads=[b_stg], writes=[b_cache])
                fw.dma("sp", lambda e: e.dma_start(out=sbc, in_=dr["csc"]), writes=[b_sb])
                fw.dma("sp", lambda e: e.dma_start(out=sbn.rearrange("p n c -> p (n c)"), in_=dr["csn"]), writes=[b_sb])
                bo = Buf("o_swc")
                fw.dma("sp", lambda e: e.dma_start(out=dr["swk"][:, 0:120, :], in_=dr["cwk"][:, 8:128, :]), writes=[bo])
                fw.dma("sp", lambda e: e.dma_start(out=dr["swv"][:, 0:120, :], in_=dr["cwv"][:, 8:128, :]), writes=[bo])
                outbufs.append(bo)
                for g in range(4):
                    wqa, bwqa = wq.get(wgroup("win", GQA + g), 128, NCH, 256)
                    for hh in range(4):
                        h = g * 4 + hh
                        ib = proj_fm(wqa, bwqa, hh * 64, 64, T)
                        fw.op("act", lambda e, ib=ib, h=h: e.activation(out=qT[0:64, h, 0:T], in_=PS[ib][0:64, 0:T], func=AF.Copy, scale=0.125),
                              reads=[PB[ib]], writes=[b_qT])
                wk_, bk_ = wq.get(wgroup("win", GK), 128, NCH, 256)
                wv_, bv_ = wq.get(wgroup("win", GV_), 128, NCH, 256)
                for kvh in range(4):
                    ib = proj_fm(wk_, bk_, kvh * 64, 64, T)
                    fw.op("dve", lambda e, ib=ib, kvh=kvh: e.tensor_copy(out=kT[0:64, kvh, 128:128 + T], in_=PS[ib][0:64, 0:T]),
                          reads=[PB[ib]], writes=[b_kT])
                kvl = ar.alloc(512, F32)
                b_kvl = Buf("kvl")
                ik = proj_tm(wk_, bk_, 0, 256)
                fw.op("dve", lambda e: e.tensor_copy(out=kvl[:, 0:256], in_=PS[ik][:, 0:256]), reads=[PB[ik]], writes=[b_kvl])
                iv = proj_tm(wv_, bv_, 0, 256)
                fw.op("act", lambda e: e.activation(out=kvl[:, 256:512], in_=PS[iv][:, 0:256], func=AF.Copy), reads=[PB[iv]], writes=[b_kvl])
                fw.op("act", lambda e: e.activation(out=vtok[:, 1, :], in_=PS[iv][:, 0:256], func=AF.Copy), reads=[PB[iv]], writes=[b_vtok[1]])
                bo = Buf("o_swn")
                for n in range(NSEQ):
                    fw.dma("sp", lambda e, n=n: e.dma_start(out=dr["swk"][n, 120:128, :], in_=kvl[n * 8:(n + 1) * 8, 0:256]), reads=[b_kvl], writes=[bo])
                    fw.dma("sp", lambda e, n=n: e.dma_start(out=dr["swv"][n, 120:128, :], in_=kvl[n * 8:(n + 1) * 8, 256:512]), reads=[b_kvl], writes=[bo])
                outbufs.append(bo)
                o32 = ar.alloc(16 * 128, F32).rearrange("p (h q) -> p h q", q=128)
                b_o32 = Buf("o32")
                scb = [ar.alloc(512, F32) for _ in range(2)]
                PTt = [ar.alloc(512, BF16) for _ in range(2)]
                b_scb = [Buf("scb0"), Buf("scb1")]
                b_PT = [Buf("PT0"), Buf("PT1")]
                rden = ar.alloc(512, F32)
                b_rden = Buf("rden")
                for nq in range(4):
                    isc, isn = nb(), nb()
                    for n4 in range(4):
                        n = nq * 4 + n4
                        for kvh in range(4):
                            cs = slice(n4 * 128 + kvh * 32, n4 * 128 + kvh * 32 + 32)
                            rq = qT[0:64, 4 * kvh:4 * kvh + 4, n * 8:(n + 1) * 8]
                            mm(PS[isc][:, cs], ckTb[0:64, n * 4 + kvh, :], rq, True, True, [b_cache, b_qT], PB[isc])
                            mm(PS[isn][:, cs], kT[0:64, kvh, 128:128 + T], rq, True, True, [b_kT, b_qT], PB[isn])
                    fw.op("dve", lambda e, isc=isc: e.tensor_tensor(out=scb[0].rearrange("p (n c) -> p n c", c=128),
                                                                     in0=PS[isc][:, :].rearrange("p (n c) -> p n c", c=128),
                                                                     in1=sbc.unsqueeze(1).to_broadcast([128, 4, 128]), op=ALU.add),
                          reads=[PB[isc], b_sb], writes=[b_scb[0]])
                    fw.op("dve", lambda e, isn=isn, nq=nq: e.tensor_tensor(out=scb[1].rearrange("p (n c) -> p n c", c=128),
                                                                            in0=PS[isn][:, :].rearrange("p (n c) -> p n c", c=128),
                                                                            in1=sbn[:, nq * 4:nq * 4 + 4, :], op=ALU.add),
                          reads=[PB[isn], b_sb], writes=[b_scb[1]])
                    for k2 in range(2):
                        fw.op("act", lambda e, k2=k2: e.activation(out=PTt[k2], in_=scb[k2], func=AF.Exp), reads=[b_scb[k2]], writes=[b_PT[k2]])
                    io, idn = nb(), nb()
                    for n4 in range(4):
                        n = nq * 4 + n4
                        for kvh in range(4):
                            cs = slice(n4 * 128 + kvh * 32, n4 * 128 + kvh * 32 + 32)
                            mm(PS[io][0:64, cs], cvb[:, n, kvh * 64:(kvh + 1) * 64], PTt[0][:, cs], True, False, [b_cache, b_PT[0]], PB[io])
                            mm(PS[io][0:64, cs], vtok[:, 1, kvh * 64:(kvh + 1) * 64], PTt[1][:, cs], False, True, [b_vtok[1], b_PT[1]], PB[io])
                    for k2 in range(2):
                        mm(PS[idn][0:64, :], ones[:, 0:64], PTt[k2], k2 == 0, k2 == 1, [b_const, b_PT[k2]], PB[idn])
                    fw.op("dve", lambda e, idn=idn: e.tensor_tensor(
                        out=rden[0:64, :].rearrange("p (n h t) -> p n h t", n=4, h=16), in0=PS[idn][0:64, :].rearrange("p (n h t) -> p n h t", n=4, h=16),
                        in1=gd[0:64, 48:64].unsqueeze(1).unsqueeze(3).to_broadcast([64, 4, 16, 8]), op=ALU.add),
                        reads=[PB[idn], b_gd], writes=[b_rden])
                    fw.op("dve", lambda e: e.reciprocal(out=rden[0:64, :], in_=rden[0:64, :]), reads=[b_rden], writes=[b_rden])
                    fw.op("dve", lambda e, io=io: e.tensor_tensor(out=rden[0:64, :], in0=PS[io][0:64, :], in1=rden[0:64, :], op=ALU.mult),
                          reads=[PB[io], b_rden], writes=[b_rden])
                    fw.op("dve", lambda e, nq=nq: e.tensor_copy(
                        out=o32[0:64, :, nq * 32:(nq + 1) * 32].rearrange("p h (n t) -> p n h t", t=8),
                        in_=rden[0:64, :].rearrange("p (n h t) -> p n h t", n=4, h=16)), reads=[b_rden], writes=[b_o32])
                sumsq_rstd([o32[0:64, h, :] for h in range(16)], 64, 1024, 128, [b_o32])
                fw.op("dve", lambda e: e.tensor_tensor(out=o32[0:64], in0=o32[0:64], in1=rstd[0:64, 0:128].unsqueeze(1).to_broadcast([64, 16, 128]),
                                                       op=ALU.mult), reads=[b_o32, b_rstd], writes=[b_o32])
                fw.op("dve", lambda e: e.tensor_tensor(out=mixA[0:64, :, 0:T], in0=o32[0:64],
                                                       in1=gv[0:64, GV["again"]:GV["again"] + 16].unsqueeze(2).to_broadcast([64, 16, 128]),
                                                       op=ALU.mult), reads=[b_o32, b_gv], writes=[b_mixA])
                fw.barrier()
                ar.reset(m1)
                tmp, b_tmp = hgrn_tmp()
                crs8 = ar.alloc(128, F32)
                hm8 = ar.alloc(128, F32)
                smk = ar.alloc(16, F32)
                b_c8 = Buf("c8")
                fw.dma("sp", lambda e: e.dma_start(out=crs8, in_=dr["crs8"]), writes=[b_c8])
                fw.dma("sp", lambda e: e.dma_start(out=hm8, in_=dr["chm8"]), writes=[b_c8])
                fw.dma("sp", lambda e: e.dma_start(out=smk, in_=dr["csm"]), writes=[b_c8])
                Am = ar.alloc(128, BF16)
                b_Am = Buf("Am")
                for hp in range(4):
                    m2 = ar.mark()
                    vh_tok = ar.alloc(256, BF16).rearrange("p (b k) -> p b k", k=256)
                    b_vh = Buf("vh")
                    heads = hgrn_pair(T, hp, True, vh_tok, b_vh, tmp, b_tmp, chunk=8, crs_ap=crs8)
                    for jj in range(2):
                        j = hp * 2 + jj
                        hd = heads[jj]
                        bt = hd["bt"]
                        S0f = ar.alloc(16 * 128, F32).rearrange("p (n v) -> p n v", v=128)
                        S0b = ar.alloc(16 * 128, BF16).rearrange("p (n v) -> p n v", v=128)
                        vbm = ar.alloc(16 * 128, BF16).rearrange("p (n v) -> p n v", v=128)
                        b_S0, b_vbm = Buf("S0"), Buf("vbm")
                        ssrc = dr["shs"][:, j].rearrange("n k v -> k n v")
                        fw.dma("sp", lambda e, S0f=S0f, ssrc=ssrc: e.dma_start(out=S0f, in_=ssrc), writes=[b_S0])
                        b_S0b = Buf("S0b")
                        fw.op("act", lambda e, S0b=S0b, S0f=S0f: e.activation(out=S0b, in_=S0f, func=AF.Copy), reads=[b_S0], writes=[b_S0b])
                        ia = nb()
                        mm(PS[ia][:, 0:128], hd["kt"][:, 0:T], hd["qt"][:, 0:T], True, True, [bt], PB[ia])
                        fw.op("dve", lambda e, ia=ia: e.tensor_tensor(out=Am, in0=PS[ia][:, 0:128], in1=hm8, op=ALU.mult),
                              reads=[PB[ia], b_c8], writes=[b_Am])
                        io = nb()
                        mm(PS[io][:, 0:128], vh_tok[:, 0, jj * 128:(jj + 1) * 128], Am, True, False, [b_vh, b_Am], PB[io])
                        for n in range(NSEQ):
                            mm(PS[io][:, n * 8:(n + 1) * 8], S0b[:, n, :], hd["qt"][:, n * 8:(n + 1) * 8], False, True, [b_S0b, bt], PB[io])
                        fw.op("act", lambda e, io=io, hd=hd: e.activation(out=hd["o32"][:, 0:T], in_=PS[io][:, 0:128], func=AF.Copy),
                              reads=[PB[io]], writes=[bt])
                        fw.op("dve", lambda e, vbm=vbm, jj=jj: e.tensor_tensor(
                            out=vbm, in0=vh_tok[:, 0, jj * 128:(jj + 1) * 128].unsqueeze(1).to_broadcast([128, 16, 128]),
                            in1=smk.unsqueeze(2).to_broadcast([128, 16, 128]), op=ALU.mult), reads=[b_vh, b_c8], writes=[b_vbm])
                        for nq in range(4):
                            iu = nb()
                            mm(PS[iu][:, :], hd["khtok"][:, 0, :], vbm[:, nq * 4:(nq + 1) * 4, :], True, True, [bt, b_vbm], PB[iu])
                            fw.op("dve", lambda e, S0f=S0f, nq=nq, hd=hd: e.tensor_tensor(
                                out=S0f[:, nq * 4:(nq + 1) * 4, :], in0=S0f[:, nq * 4:(nq + 1) * 4, :],
                                in1=hd["eLend"][:, nq * 4:(nq + 1) * 4].unsqueeze(2).to_broadcast([128, 4, 128]), op=ALU.mult),
                                reads=[b_S0, bt], writes=[b_S0])
                            fw.op("dve", lambda e, S0f=S0f, nq=nq, iu=iu: e.tensor_tensor(
                                out=S0f[:, nq * 4:(nq + 1) * 4, :], in0=S0f[:, nq * 4:(nq + 1) * 4, :],
                                in1=PS[iu][:, :].rearrange("p (n v) -> p n v", v=128), op=ALU.add), reads=[b_S0, PB[iu]], writes=[b_S0])
                        bo = Buf("o_shg")
                        fw.dma("sp", lambda e, S0f=S0f, j=j: e.dma_start(out=dr["shg"][:, j].rearrange("n k v -> k n v"), in_=S0f), reads=[b_S0], writes=[bo])
                        outbufs.append(bo)
                        sumsq_rstd([hd["o32"][:, 0:T]], 128, 128, T, [bt])
                        fw.op("dve", lambda e, hd=hd: e.scalar_tensor_tensor(out=hd["o32"][:, 0:T], in0=hd["o32"][:, 0:T], scalar=gcol("hgain"),
                                                                             in1=rstd[:, 0:T], op0=ALU.mult, op1=ALU.mult),
                              reads=[bt, b_rstd, b_gv], writes=[bt])
                        fw.op("dve", lambda e, hd=hd, j=j: e.tensor_tensor(out=mixR[:, j, 0:T], in0=hd["o32"][:, 0:T], in1=hd["gs"][:, 0:T], op=ALU.mult),
                              reads=[bt], writes=[b_mixR])
                    fw.barrier()
                    ar.reset(m2)
                ar.reset(m1)
                wout_stage(T, mixA, mixR, b_mixA, b_mixR)
                ar.reset(mA)

            def sample_mem(T):
                m0 = ar.mark()
                prenorm("epre", T)
                qmT = ar.alloc(4 * T, BF16).rearrange("p (h t) -> p h t", t=T)
                omT = ar.alloc(4 * T, BF16).rearrange("p (h t) -> p h t", t=T)
                PmT = ar.alloc(256, BF16)
                rdn = ar.alloc(128, F32)
                fT = ar.alloc(NCH * TT, F32).rearrange("p (c t) -> p c t", t=TT)
                b_q, b_o, b_P, b_r, b_f = Buf("qm"), Buf("om"), Buf("Pm"), Buf("rdn"), [Buf(f"fT{m}") for m in range(NCH)]
                for g in range(2):
                    wqm, bwq = wq.get(wgroup("wmq", g), 128, NCH, 256)
                    for hh in range(2):
                        h = g * 2 + hh
                        ib = proj_fm(wqm, bwq, hh * 128, 128, T)
                        fw.op("act", lambda e, ib=ib, h=h: e.activation(out=qmT[:, h, 0:T], in_=PS[ib][:, 0:T], func=AF.Copy, scale=128 ** -0.5),
                              reads=[PB[ib]], writes=[b_q])
                for nq in range(4):
                    kts, bks = wq.get(dr["cmkT"][:, nq * 16:(nq + 1) * 16, :], 128, 16, 256)
                    vts, bvs = wq.get(dr["cmv"][nq * 4:(nq + 1) * 4].rearrange("n (c p) k -> p (n c) k", p=128), 128, 8, 512)
                    isc = nb()
                    for n4 in range(4):
                        n = nq * 4 + n4
                        for h in range(4):
                            for mc in range(2):
                                c0 = mc * 128 + n4 * 32 + h * 8
                                mm(PS[isc][:, c0:c0 + 8], kts[:, n4 * 4 + h, mc * 128:(mc + 1) * 128], qmT[:, h, n * 8:(n + 1) * 8], True, True,
                                   [bks, b_q], PB[isc])
                    fw.op("act", lambda e, isc=isc: e.activation(out=PmT, in_=PS[isc][:, 0:256], func=AF.Exp), reads=[PB[isc]], writes=[b_P])
                    io, idn = nb(), nb()
                    for n4 in range(4):
                        for h in range(4):
                            c1 = n4 * 32 + h * 8
                            for mc in range(2):
                                c0 = mc * 128 + c1
                                mm(PS[io][:, c1:c1 + 8], vts[:, n4 * 2 + mc, h * 128:(h + 1) * 128], PmT[:, c0:c0 + 8], mc == 0, mc == 1, [bvs, b_P], PB[io])
                    for mc in range(2):
                        mm(PS[idn][:, 0:128], ones[:], PmT[:, mc * 128:(mc + 1) * 128], mc == 0, mc == 1, [b_const, b_P], PB[idn])
                    fw.op("dve", lambda e, idn=idn: e.reciprocal(out=rdn, in_=PS[idn][:, 0:128]), reads=[PB[idn]], writes=[b_r])
                    fw.op("dve", lambda e, io=io, nq=nq: e.tensor_tensor(
                        out=omT[:, :, nq * 32:(nq + 1) * 32].rearrange("p h (n t) -> p n h t", t=8),
                        in0=PS[io][:, 0:128].rearrange("p (n h t) -> p n h t", n=4, h=4),
                        in1=rdn.rearrange("p (n h t) -> p n h t", n=4, h=4), op=ALU.mult), reads=[PB[io], b_r], writes=[b_o])
                mem_out(T, omT, b_o, fT, b_f)
                fw.barrier()
                ar.reset(m0)

            fw.op("dve", lambda e: e.memset(kT[:], 0.0), writes=[b_kT])
            fw.op("dve", lambda e: e.memset(vtok[:, 0, :], 0.0), writes=[b_vtok[0]])
            for j in range(8):
                fw.op("dve", lambda e, j=j: e.memset(S32[:, j, :], 0.0), writes=[b_S32[j]])
                fw.op("dve", lambda e, j=j: e.memset(Sbf[:, j, :], 0.0), writes=[b_Sbf[j]])
            import os
            stg = os.environ.get("KSTAGE", "all")
            try:
                if stg == "all":
                    mem_kv()
                    for t in range(NPT):
                        pass0_tile(t, t == NPT - 1)
                    for t in range(NPT):
                        main_tile(t)
                    sample_tile()
                elif stg == "sample":
                    sample_tile()
                elif stg == "memkv":
                    mem_kv()
                elif stg == "p0":
                    mem_kv()
                    pass0_tile(0, True)
                elif stg == "main":
                    mem_kv()
                    main_tile(0)
            except StopEmit:
                fw.barrier()
            fw.wait_all("sp", outbufs)

        fw0 = FW(nc, st, dry=True)
        wq0 = WQ(fw0, [w[:] for w in wsl], None)
        emit(fw0, wq0)
        plan = wq0.rec
        fw = FW(nc, st)
        fw.mark_tile = sb("mark", [128, 8], F32)[:]
        wq = WQ(fw, [w[:] for w in wsl], plan, nc)
        emit(fw, wq)
        print("instructions:", fw.n_inst, "weight groups:", len(plan))
        fw.replay()
    _NC_CACHE["used"] = set(dr.keys())
    return nc


def _consts():
    slopes = 2.0 ** (-8.0 * np.arange(1, 17, dtype=np.float64) / 16)
    j = np.arange(128)[:, None, None, None]
    kc = np.arange(2)[None, :, None, None]
    i = np.arange(128)[None, None, None, :]
    dist = (128 + i) - (kc * 128 + j)
    valid = (dist >= 0) & (dist < 128)
    cab = np.where(valid, -slopes[None, None, :, None] * dist, NEG).astype(np.float32).reshape(128, -1)
    s = np.arange(128)[:, None]
    t = np.arange(128)[None, :]
    chm = ((s // 64 == t // 64) & (s <= t)).astype(np.float32)
    crs = np.ones((128, 512), np.float32)
    crs[:, ::64] = 0.0
    cid = np.eye(128, dtype=np.float32)
    jj = np.arange(128)[:, None, None]
    hh = np.arange(16)[None, :, None]
    tt = np.arange(8)[None, None, :]
    dist = 128 + tt - jj
    csc = np.where(dist < 128, -slopes[hh] * dist, NEG).astype(np.float32).reshape(128, 128)
    kn = (np.arange(128) // 8)[:, None, None, None]
    ks = (np.arange(128) % 8)[:, None, None, None]
    qn = np.arange(16)[None, :, None, None]
    h4 = np.arange(16)[None, None, :, None]
    t4 = np.arange(8)[None, None, None, :]
    csn = np.where((kn == qn) & (ks <= t4), -slopes[h4] * (t4 - ks), NEG).astype(np.float32).reshape(128, 2048)
    chm8 = ((s // 8 == t // 8) & (s <= t)).astype(np.float32)
    crs8 = np.ones((128, 128), np.float32)
    crs8[:, ::8] = 0.0
    csm = (np.arange(128)[:, None] // 8 == np.arange(16)[None, :]).astype(np.float32)
    return cab, chm, crs, cid, csc, csn, chm8, crs8, csm


def _pack16(v):
    return np.ascontiguousarray(v.reshape(16, 128).T)


_NC_CACHE = {}
_OUT_SHAPES = dict(ym=(D, NPT * TT), ys=(D, TS), pwk=(128, 256), pwv=(128, 256), phg=(8, 128, 128), pmk=(256, 512), pmv=(256, 512),
                   swk=(NSEQ, 128, 256), swv=(NSEQ, 128, 256), shg=(NSEQ, 8, 128, 128))


def kernel(x_prompt, x_sample, mem_prompt, cache_win_k, cache_win_v, state_hgrn, cache_mem_k, cache_mem_v,
           ffn1_norm_pre, ffn1_norm_post, ffn1_w_gate, ffn1_w_up, ffn1_w_down,
           mix_norm_pre, mix_norm_post, w_in, attn_sinks, hgrn_lb_logits, attn_out_gain, hgrn_out_gain, w_out,
           mem_norm_pre, mem_norm_post, mem_norm_kv, w_mem_q, w_mem_k, w_mem_v, w_mem_o,
           ffn2_norm_pre, ffn2_norm_post, ffn2_w_gate, ffn2_w_up, ffn2_w_down):
    f = lambda a: np.ascontiguousarray(np.asarray(a, dtype=np.float32))
    x_prompt, x_sample, mem_prompt = f(x_prompt), f(x_sample), f(mem_prompt)
    cab, chm, crs, cid, csc, csn, chm8, crs8, csm = _consts()
    gvb = np.zeros((128, NGV), np.float32)
    for name, arr in (("f1pre", ffn1_norm_pre), ("f1post", ffn1_norm_post), ("mpre", mix_norm_pre), ("mpost", mix_norm_post),
                      ("epre", mem_norm_pre), ("epost", mem_norm_post), ("ekv", mem_norm_kv), ("f2pre", ffn2_norm_pre),
                      ("f2post", ffn2_norm_post)):
        gvb[:, GV[name]:GV[name] + 16] = _pack16(f(arr)[0])
    gvb[0:64, GV["again"]:GV["again"] + 16] = f(attn_out_gain)[0].reshape(16, 64).T
    gvb[:, GV["hgain"]] = f(hgrn_out_gain)[0]
    lbl = f(hgrn_lb_logits)
    gvb[:, GV["lb0"]:GV["lb0"] + 8] = lbl[0].reshape(8, 128).T
    gvb[:, GV["lb1"]:GV["lb1"] + 8] = lbl[1].reshape(8, 128).T
    gvb[:, GV["sinks"]:GV["sinks"] + 16] = f(attn_sinks)[0][None, :]
    shared = dict(w1g=f(ffn1_w_gate)[0], w1u=f(ffn1_w_up)[0], w1d=f(ffn1_w_down)[0], w2g=f(ffn2_w_gate)[0], w2u=f(ffn2_w_up)[0],
                  w2d=f(ffn2_w_down)[0], win=f(w_in)[0], wout=f(w_out)[0], wmq=f(w_mem_q)[0], wmk=f(w_mem_k)[0], wmv=f(w_mem_v)[0],
                  wmo=f(w_mem_o)[0], cab=cab, chm=chm, crs=crs, cid=cid,
                  csc=csc, csn=csn, chm8=chm8, crs8=crs8, csm=csm)
    in_maps = []
    for c in range(8):
        b, half = c // 2, c % 2
        g = gvb.copy()
        g[:, GV["flag"]] = float(half)
        xm = np.ascontiguousarray(x_prompt[b, half * 2048:(half + 1) * 2048, :].T)
        xp = np.ascontiguousarray(x_prompt[b, 0:2048, :].T) if half == 1 else np.zeros((D, 2048), np.float32)
        xs = np.ascontiguousarray(x_sample[c * NSEQ:(c + 1) * NSEQ].reshape(TS, D).T)
        m = dict(shared)
        m.update(xm=xm, xp=xp, xs=xs, memT=np.ascontiguousarray(mem_prompt[b].T), gv=g)
        sq = slice(c * NSEQ, (c + 1) * NSEQ)
        cwk_ = f(cache_win_k)[0, sq]
        m["ckT"] = np.ascontiguousarray(cwk_.transpose(3, 0, 2, 1).reshape(64, NSEQ * 4, 128))
        m["cwk"] = np.ascontiguousarray(cwk_.reshape(NSEQ, 128, 256))
        m["cwv"] = np.ascontiguousarray(f(cache_win_v)[0, sq].reshape(NSEQ, 128, 256))
        m["shs"] = np.ascontiguousarray(f(state_hgrn)[0, sq])
        m["cmkT"] = np.ascontiguousarray(f(cache_mem_k)[0, sq].transpose(3, 0, 2, 1).reshape(128, NSEQ * 4, 256))
        m["cmv"] = np.ascontiguousarray(f(cache_mem_v)[0, sq].reshape(NSEQ, 256, 512))
        in_maps.append(m)
    if "nc" not in _NC_CACHE:
        _NC_CACHE["nc"] = build_program()
    nc = _NC_CACHE["nc"]
    import os
    ncores = int(os.environ.get("KCORES", "8"))
    in_maps = [{k: v for k, v in m.items() if k in _NC_CACHE["used"]} for m in in_maps[:ncores]]
    res = run_bass_kernel_spmd(nc, in_maps, core_ids=list(range(ncores)))
    R = list(res.results)
    if ncores < 8 or os.environ.get("KSTAGE", "all") != "all":
        class ZD(dict):
            def __missing__(self, k):
                return np.zeros(_OUT_SHAPES[k], np.float32)
        R = [ZD(r) for r in R] + [ZD() for _ in range(8 - ncores)]
    y_p = np.zeros((4, 4096, D), np.float32)
    y_s = np.zeros((128, 8, D), np.float32)
    p_wk = np.zeros((1, 4, 128, 4, 64), np.float32); p_wv = np.zeros_like(p_wk)
    p_S = np.zeros((1, 4, 8, 128, 128), np.float32)
    p_mk = np.zeros((1, 4, 256, 4, 128), np.float32); p_mv = np.zeros_like(p_mk)
    s_wk = np.zeros((1, 128, 128, 4, 64), np.float32); s_wv = np.zeros_like(s_wk)
    s_S = np.zeros((1, 128, 8, 128, 128), np.float32)
    for c in range(8):
        b, half = c // 2, c % 2
        r = R[c]
        y_p[b, half * 2048:(half + 1) * 2048, :] = r["ym"].T
        y_s[c * NSEQ:(c + 1) * NSEQ] = r["ys"].T.reshape(NSEQ, 8, D)
        if half == 1:
            p_wk[0, b] = r["pwk"].reshape(128, 4, 64)
            p_wv[0, b] = r["pwv"].reshape(128, 4, 64)
            p_S[0, b] = r["phg"]
        else:
            p_mk[0, b] = r["pmk"].reshape(256, 4, 128)
            p_mv[0, b] = r["pmv"].reshape(256, 4, 128)
        s_wk[0, c * NSEQ:(c + 1) * NSEQ] = r["swk"].reshape(NSEQ, 128, 4, 64)
        s_wv[0, c * NSEQ:(c + 1) * NSEQ] = r["swv"].reshape(NSEQ, 128, 4, 64)
        s_S[0, c * NSEQ:(c + 1) * NSEQ] = r["shg"]
    return (y_p, y_s, p_wk, p_wv, p_S, p_mk, p_mv, s_wk, s_wv, s_S)
```
